# Optimizing a Trainium2 kernel written in Bass

```python
import jax, jax.numpy as jnp
from jax import lax
import numpy as np

D_MODEL = 2048
BATCH = 2
SEQ = 16384
DEPTH = 4

CHUNK = 64
N_MIXERS = 2
N_MLA_LAYERS = (DEPTH + N_MIXERS - 1) // N_MIXERS
N_LRU_LAYERS = DEPTH // N_MIXERS
MLA_HEADS = 16
Q_LORA = 512
KV_LORA = 512
QK_NOPE = 128
QK_ROPE = 64
V_HEAD = 128
QK_HEAD = QK_NOPE + QK_ROPE
ROPE_THETA = 10000.0
Q_BLOCK = 128
D_RNN = 2688
LRU_BLOCKS = 16
LRU_BLOCK = D_RNN // LRU_BLOCKS
CONV_WIDTH = 4
LRU_C = 8.0
D_FF = 4 * D_MODEL
EPS = 1e-6
MAX_POS_OFFSET = 4096

kernel_name = "hybrid_mla_rglru_sqrelu_trunk"


def rmsnorm(x, g):
    xf = x.astype(jnp.float32)
    y = xf * lax.rsqrt(jnp.mean(xf * xf, axis=-1, keepdims=True) + EPS)
    return (y * g.astype(jnp.float32)).astype(x.dtype)


def rope_tables(positions):
    half = QK_ROPE // 2
    inv_freq = ROPE_THETA ** (-jnp.arange(half, dtype=jnp.float32) / half)
    ang = positions.astype(jnp.float32)[..., None] * inv_freq
    return jnp.cos(ang), jnp.sin(ang)


def apply_rope(x, cos, sin):
    xf = x.astype(jnp.float32)
    x1, x2 = jnp.split(xf, 2, axis=-1)
    return jnp.concatenate([x1 * cos - x2 * sin, x2 * cos + x1 * sin], axis=-1).astype(x.dtype)


def mla(x, positions, w_dq, g_q, w_uq, w_dkv, g_kv, w_ukv, w_o):
    B, S, _ = x.shape
    cq = rmsnorm(x @ w_dq, g_q)
    q = (cq @ w_uq).reshape(B, S, MLA_HEADS, QK_HEAD)
    q_nope, q_rope = q[..., :QK_NOPE], q[..., QK_NOPE:]
    kv_a = x @ w_dkv
    ckv = rmsnorm(kv_a[..., :KV_LORA], g_kv)
    kv = (ckv @ w_ukv).reshape(B, S, MLA_HEADS, QK_NOPE + V_HEAD)
    k_nope, v = kv[..., :QK_NOPE], kv[..., QK_NOPE:]
    cos, sin = rope_tables(positions)
    q_rope = apply_rope(q_rope, cos[:, :, None, :], sin[:, :, None, :])
    k_rope = apply_rope(kv_a[..., KV_LORA:], cos, sin)
    scale = QK_HEAD ** -0.5
    n_blocks = S // Q_BLOCK
    qn_b = q_nope.reshape(B, n_blocks, Q_BLOCK, MLA_HEADS, QK_NOPE).transpose(1, 0, 2, 3, 4)
    qr_b = q_rope.reshape(B, n_blocks, Q_BLOCK, MLA_HEADS, QK_ROPE).transpose(1, 0, 2, 3, 4)
    k_chunk = jnp.arange(S) // CHUNK
    neg = jnp.finfo(jnp.float32).min

    def attend(args):
        blk, qn, qr = args
        s = (jnp.einsum('bqhd,bkhd->bhqk', qn, k_nope)
             + jnp.einsum('bqhr,bkr->bhqk', qr, k_rope)).astype(jnp.float32) * scale
        q_chunk = (blk * Q_BLOCK + jnp.arange(Q_BLOCK)) // CHUNK
        allowed = k_chunk[None, :] <= q_chunk[:, None]
        s = jnp.where(allowed[None, None], s, neg)
        p = jax.nn.softmax(s, axis=-1).astype(v.dtype)
        return jnp.einsum('bhqk,bkhd->bqhd', p, v)

    o = lax.map(attend, (jnp.arange(n_blocks), qn_b, qr_b))
    o = o.transpose(1, 0, 2, 3, 4).reshape(B, S, MLA_HEADS * V_HEAD)
    return o @ w_o


def causal_depthwise_conv(u, w, b):
    out = lax.conv_general_dilated(
        u, w[:, None, :].astype(u.dtype), window_strides=(1,),
        padding=[(CONV_WIDTH - 1, 0)],
        dimension_numbers=('NWC', 'WIO', 'NWC'),
        feature_group_count=u.shape[-1])
    return out + b


def block_diag(u, w, b):
    B, S, _ = u.shape
    ub = u.reshape(B, S, LRU_BLOCKS, LRU_BLOCK)
    return jnp.einsum('bsnd,nde->bsne', ub, w).reshape(B, S, D_RNN) + b


def scan_combine(c1, c2):
    a1, b1 = c1
    a2, b2 = c2
    return a1 * a2, a2 * b1 + b2


def rglru_block(x, w_y, b_y, w_x, b_x, conv_w, conv_b, w_ga, b_ga, w_gi, b_gi, lam, w_out, b_out):
    y = jax.nn.gelu(x @ w_y + b_y)
    u = causal_depthwise_conv(x @ w_x + b_x, conv_w, conv_b)
    r = jax.nn.sigmoid(block_diag(u, w_ga, b_ga)).astype(jnp.float32)
    i = jax.nn.sigmoid(block_diag(u, w_gi, b_gi))
    log_a = LRU_C * r * jax.nn.log_sigmoid(lam.astype(jnp.float32))
    a = jnp.exp(log_a)
    inp = jnp.sqrt(-jnp.expm1(2.0 * log_a)) * (i * u).astype(jnp.float32)
    _, h = lax.associative_scan(scan_combine, (a, inp), axis=1)
    return (h.astype(x.dtype) * y) @ w_out + b_out


def sq_relu_mlp(x, w1, w2):
    return jnp.square(jax.nn.relu(x @ w1)) @ w2


def setup_inputs(seed: int = 0) -> dict:
    key = jax.random.key(seed)
    ks = iter(jax.random.split(key, 48))

    def nrm(shape, fan_in):
        return jax.random.normal(next(ks), shape, jnp.float32) * (fan_in ** -0.5)

    def gain(shape):
        return 1.0 + 0.02 * jax.random.normal(next(ks), shape, jnp.float32)

    def bias(shape):
        return 0.01 * jax.random.normal(next(ks), shape, jnp.float32)

    x = jax.random.normal(next(ks), (BATCH, SEQ, D_MODEL), jnp.float32)
    offset = jax.random.randint(next(ks), (BATCH, 1), 0, MAX_POS_OFFSET, dtype=jnp.int32)
    positions = (offset + jnp.arange(SEQ, dtype=jnp.int32)[None, :]).astype(jnp.int32)

    NA, NB = N_MLA_LAYERS, N_LRU_LAYERS
    a0 = jax.random.uniform(next(ks), (NB, D_RNN), jnp.float32, 0.9, 0.999)
    p = a0 ** (1.0 / LRU_C)
    lam = jnp.log(p) - jnp.log1p(-p)

    return {
        'x': x,
        'positions': positions,
        'mix_pre_g': gain((DEPTH, D_MODEL)),
        'mix_post_g': gain((DEPTH, D_MODEL)),
        'mlp_pre_g': gain((DEPTH, D_MODEL)),
        'mlp_post_g': gain((DEPTH, D_MODEL)),
        'mla_w_dq': nrm((NA, D_MODEL, Q_LORA), D_MODEL),
        'mla_g_q': gain((NA, Q_LORA)),
        'mla_w_uq': nrm((NA, Q_LORA, MLA_HEADS * QK_HEAD), Q_LORA),
        'mla_w_dkv': nrm((NA, D_MODEL, KV_LORA + QK_ROPE), D_MODEL),
        'mla_g_kv': gain((NA, KV_LORA)),
        'mla_w_ukv': nrm((NA, KV_LORA, MLA_HEADS * (QK_NOPE + V_HEAD)), KV_LORA),
        'mla_w_o': nrm((NA, MLA_HEADS * V_HEAD, D_MODEL), MLA_HEADS * V_HEAD),
        'lru_w_y': nrm((NB, D_MODEL, D_RNN), D_MODEL),
        'lru_b_y': bias((NB, D_RNN)),
        'lru_w_x': nrm((NB, D_MODEL, D_RNN), D_MODEL),
        'lru_b_x': bias((NB, D_RNN)),
        'lru_conv_w': nrm((NB, CONV_WIDTH, D_RNN), CONV_WIDTH),
        'lru_conv_b': bias((NB, D_RNN)),
        'lru_w_ga': nrm((NB, LRU_BLOCKS, LRU_BLOCK, LRU_BLOCK), LRU_BLOCK),
        'lru_b_ga': bias((NB, D_RNN)),
        'lru_w_gi': nrm((NB, LRU_BLOCKS, LRU_BLOCK, LRU_BLOCK), LRU_BLOCK),
        'lru_b_gi': bias((NB, D_RNN)),
        'lru_lam': lam,
        'lru_w_out': nrm((NB, D_RNN, D_MODEL), D_RNN),
        'lru_b_out': bias((NB, D_MODEL)),
        'mlp_w1': nrm((DEPTH, D_MODEL, D_FF), D_MODEL),
        'mlp_w2': nrm((DEPTH, D_FF, D_MODEL), D_FF),
    }


def reference(x, positions, mix_pre_g, mix_post_g, mlp_pre_g, mlp_post_g,
              mla_w_dq, mla_g_q, mla_w_uq, mla_w_dkv, mla_g_kv, mla_w_ukv, mla_w_o,
              lru_w_y, lru_b_y, lru_w_x, lru_b_x, lru_conv_w, lru_conv_b,
              lru_w_ga, lru_b_ga, lru_w_gi, lru_b_gi, lru_lam, lru_w_out, lru_b_out,
              mlp_w1, mlp_w2):
    for i in range(DEPTH):
        j = i // N_MIXERS
        h = rmsnorm(x, mix_pre_g[i])
        if i % N_MIXERS == 0:
            h = mla(h, positions, mla_w_dq[j], mla_g_q[j], mla_w_uq[j], mla_w_dkv[j],
                    mla_g_kv[j], mla_w_ukv[j], mla_w_o[j])
        else:
            h = rglru_block(h, lru_w_y[j], lru_b_y[j], lru_w_x[j], lru_b_x[j],
                            lru_conv_w[j], lru_conv_b[j], lru_w_ga[j], lru_b_ga[j],
                            lru_w_gi[j], lru_b_gi[j], lru_lam[j], lru_w_out[j], lru_b_out[j])
        x = x + rmsnorm(h, mix_post_g[i])
        h = rmsnorm(x, mlp_pre_g[i])
        h = sq_relu_mlp(h, mlp_w1[i], mlp_w2[i])
        x = x + rmsnorm(h, mlp_post_g[i])
    return x
```

```python
import contextlib
import numpy as np
import ml_dtypes
import concourse.bass as bass
import concourse.mybir as mybir
from concourse.bass_utils import run_bass_kernel_spmd

F32 = mybir.dt.float32
BF16 = mybir.dt.bfloat16
I32 = mybir.dt.int32
ALU = mybir.AluOpType
AF = mybir.ActivationFunctionType
NPBF16 = ml_dtypes.bfloat16

D = 2048
DC = D // 128
DFF = 8192
FC = DFF // 128
EPS = 1e-6
TB = 512
NCORES = 8
HEADS = 16
DRNN = 2688
RC = DRNN // 128


class Op:
    __slots__ = ("eng", "fn", "deps", "idx", "signal", "cnt", "is_dma", "dsem", "dval", "gid")


class Prog:
    ENGS = ["pe", "act", "dve", "pool", "sp"]
    EPOCH = 12000
    NDMA_SEM = 8

    _uid = [0]
    G = {}

    def __init__(self, nc):
        self.nc = nc
        Prog._uid[0] += 1
        self.pfx = f"p{Prog._uid[0]}_"
        self.ops = {e: [] for e in self.ENGS}
        self.last_w = {}
        self.readers = {}
        self.stack = contextlib.ExitStack()
        self.ngid = 0

    def sb(self, name, shape, dt):
        return self.stack.enter_context(self.nc.sbuf_tensor(self.pfx + name, list(shape), dt))

    def ps(self, name, shape=(128, 512), dt=F32):
        return self.stack.enter_context(self.nc.psum_tensor(self.pfx + name, list(shape), dt))

    def add(self, eng, fn, reads=(), writes=(), dma=False):
        op = Op()
        op.eng = eng
        op.fn = fn
        op.is_dma = dma
        op.signal = False
        op.gid = self.ngid
        self.ngid += 1
        deps = {}
        for k in reads:
            w = self.last_w.get(k)
            if w is not None:
                deps[w.gid] = w
        for k in writes:
            w = self.last_w.get(k)
            if w is not None:
                deps[w.gid] = w
            for r in self.readers.get(k, ()):
                deps[r.gid] = r
        for k in reads:
            self.readers.setdefault(k, []).append(op)
        for k in writes:
            self.last_w[k] = op
            self.readers[k] = []
        op.deps = list(deps.values())
        op.idx = len(self.ops[eng])
        self.ops[eng].append(op)
        return op

    def dma(self, eng, out, in_, reads=(), writes=()):
        return self.add(eng, lambda e: e.dma_start(out=out, in_=in_), reads, writes, dma=True)

    def emit(self):
        nc = self.nc
        for e in self.ENGS:
            for op in self.ops[e]:
                best = {}
                dd = []
                for d in op.deps:
                    if d.is_dma:
                        dd.append(d)
                        continue
                    if d.eng == op.eng and op.eng == "pe" and not op.is_dma:
                        continue
                    b = best.get(d.eng)
                    if b is None or d.idx > b.idx:
                        best[d.eng] = d
                for d in best.values():
                    d.signal = True
                    dd.append(d)
                op.deps = dd
        G = Prog.G
        if G.get("nc") is not nc:
            G.clear()
            G["nc"] = nc
            G["sems"] = {e: [] for e in self.ENGS}
            G["cnt"] = {e: 0 for e in self.ENGS}
            G["dsets"] = {e: [] for e in self.ENGS}
            G["dvals"] = {e: None for e in self.ENGS}
            G["dj"] = {e: 0 for e in self.ENGS}
            G["nalloc"] = 0

        def new_sem(name):
            G["nalloc"] += 1
            return nc.alloc_semaphore(name=f"g{G['nalloc']}_{name}")
        base = dict(G["cnt"])
        for e in self.ENGS:
            c = G["cnt"][e]
            for op in self.ops[e]:
                if op.signal and not op.is_dma:
                    c += 1
                    op.cnt = c
            G["cnt"][e] = c
            need = (c + self.EPOCH - 1) // self.EPOCH
            while len(G["sems"][e]) < max(1, need):
                G["sems"][e].append(new_sem(f"s_{e}"))
        sems = G["sems"]
        DLIM = 30000
        for e in self.ENGS:
            for op in self.ops[e]:
                if op.is_dma:
                    if G["dvals"][e] is None or max(G["dvals"][e]) + 16 > DLIM:
                        G["dsets"][e].append([new_sem(f"d_{e}") for _ in range(self.NDMA_SEM)])
                        G["dvals"][e] = [0] * self.NDMA_SEM
                    sidx = G["dj"][e] % self.NDMA_SEM
                    G["dj"][e] += 1
                    G["dvals"][e][sidx] += 16
                    op.dsem = G["dsets"][e][-1][sidx]
                    op.dval = G["dvals"][e][sidx]
        EP = self.EPOCH
        self.stats = {}

        def run(eng_name, eng):
            known = dict(base)
            known_dma = set()
            nw = 0
            for op in self.ops[eng_name]:
                for d in op.deps:
                    if d.is_dma:
                        if d.gid in known_dma:
                            continue
                        eng.wait_ge(d.dsem, d.dval)
                        known_dma.add(d.gid)
                        nw += 1
                    else:
                        if d.cnt <= known[d.eng]:
                            continue
                        ep = (d.cnt - 1) // EP
                        eng.wait_ge(sems[d.eng][ep], d.cnt - ep * EP)
                        known[d.eng] = d.cnt
                        nw += 1
                if op.is_dma:
                    if op.dval > 16:
                        eng.wait_ge(op.dsem, op.dval - 16)
                    op.fn(eng).then_inc(op.dsem, 16)
                else:
                    ins = op.fn(eng)
                    if op.signal:
                        ep = (op.cnt - 1) // EP
                        ins.then_inc(sems[eng_name][ep], 1)
            last = {}
            for op in self.ops[eng_name]:
                if op.is_dma:
                    last[id(op.dsem)] = op
            for op in last.values():
                eng.wait_ge(op.dsem, op.dval)
            self.stats[eng_name] = (len(self.ops[eng_name]), nw)

        with nc.Block() as block:
            @block.tensor
            def _(pe):
                run("pe", pe)

            @block.scalar
            def _(act):
                run("act", act)

            @block.vector
            def _(dve):
                run("dve", dve)

            @block.gpsimd
            def _(pool):
                run("pool", pool)

            @block.sync
            def _(sp):
                run("sp", sp)
        self.stack.close()


class WStream:
    def __init__(self):
        self.tiles = []
        self.meta = []
        self.off = 0

    def add_matrix(self, w, mw=128):
        K, M = w.shape
        kc = K // 128
        kmax = 2048 // mw
        out = []
        wt = w.reshape(kc, 128, M // mw, mw)
        for m in range(M // mw):
            lst = []
            for k0 in range(0, kc, kmax):
                nk = min(kmax, kc - k0)
                t = np.ascontiguousarray(wt[k0:k0 + nk, :, m, :].transpose(1, 0, 2))
                lst.append((len(self.tiles), nk))
                self.tiles.append(t.reshape(128, nk * mw))
                self.meta.append((self.off, nk * mw))
                self.off += nk * mw
            out.append(lst)
        return out

    def add_raw(self, t):
        n = t.shape[1]
        tid = len(self.tiles)
        self.tiles.append(np.ascontiguousarray(t))
        self.meta.append((self.off, n))
        self.off += n
        return tid

    def array(self):
        return np.ascontiguousarray(np.concatenate(self.tiles, axis=1).astype(np.float32))


class WRing:
    def __init__(self, P, wdram, meta, nslots=6):
        self.P = P
        self.wdram = wdram
        self.meta = meta
        self.nslots = nslots
        self.buf = P.sb("wring", [128, nslots, 16 * 128], BF16)
        self.next = 0
        self.loaded = {}

    def fetch(self, tid):
        slot = self.next % self.nslots
        self.next += 1
        off, ne = self.meta[tid]
        key = ("w", slot)
        self.P.dma("pool", self.buf[:, slot, 0:ne], self.wdram[:, off:off + ne], writes=[key])
        return self.buf[:, slot, :], key


def gvec(g):
    g = np.asarray(g, np.float32)
    return np.ascontiguousarray(g.reshape(-1, 128).T)


class Ctx:
    pass


def emit_rmsnorm_stats(P, C, src, src_keys, sq, sq_keys, nch, ps_ss, ps_key, rstd, rstd_key, tmp, tmp_key,
                       dim, ntok=TB, sq_eng="act"):
    for c in range(nch):
        if sq_eng == "act":
            P.add("act", lambda e, c=c: e.activation(out=sq[:, c, :ntok], in_=src[:, c, :ntok], func=AF.Square),
                  reads=[src_keys[c]], writes=[sq_keys[c]])
        else:
            P.add(sq_eng, lambda e, c=c: e.tensor_tensor(out=sq[:, c, :ntok], in0=src[:, c, :ntok],
                                                         in1=src[:, c, :ntok], op=ALU.mult),
                  reads=[src_keys[c]], writes=[sq_keys[c]])
    for c in range(nch):
        P.add("pe", lambda e, c=c: e.matmul(ps_ss[:, :ntok], C.ones[:, :], sq[:, c, :ntok], start=(c == 0),
                                            stop=(c == nch - 1)),
              reads=[sq_keys[c], "ones"], writes=[ps_key])
    P.add("act", lambda e: e.activation(out=tmp[:, :ntok], in_=ps_ss[:, :ntok], func=AF.Sqrt, scale=1.0 / dim,
                                        bias=C.eps[:, 0:1]),
          reads=[ps_key, "eps"], writes=[tmp_key])
    P.add("dve", lambda e: e.reciprocal(out=rstd[:, :ntok], in_=tmp[:, :ntok]),
          reads=[tmp_key], writes=[rstd_key])


def build_post_mlp(T, KIN, has_bias, wmeta, wplan):
    nc = bass.Bass("TRN2", target_bir_lowering=False)
    KC = KIN // 128
    NB = T // TB
    xT = nc.dram_tensor("xT", [D, T], F32, kind="ExternalInput").ap()
    mT = nc.dram_tensor("mT", [KIN, T], BF16, kind="ExternalInput").ap()
    wdram = nc.dram_tensor("wstream", [128, wmeta[-1][0] + wmeta[-1][1]], F32, kind="ExternalInput").ap()
    gpost = nc.dram_tensor("gpost", [128, DC], F32, kind="ExternalInput").ap()
    gpre = nc.dram_tensor("gpre", [128, DC], F32, kind="ExternalInput").ap()
    gpost2 = nc.dram_tensor("gpost2", [128, DC], F32, kind="ExternalInput").ap()
    bout = nc.dram_tensor("bout", [128, DC], F32, kind="ExternalInput").ap()
    oT = nc.dram_tensor("oT", [D, T], F32, kind="ExternalOutput").ap()
    xTv = xT.rearrange("(c p) t -> p c t", p=128)
    oTv = oT.rearrange("(c p) t -> p c t", p=128)
    mTv = mT.rearrange("(c p) t -> p c t", p=128)

    P = Prog(nc)
    C = Ctx()
    C.ones = P.sb("ones", [128, 128], BF16)
    x_sb = P.sb("x_sb", [128, DC, TB], F32)
    y_sb = P.sb("y_sb", [128, DC, TB], F32)
    h_sb = P.sb("h_sb", [128, DC, TB], BF16)
    big = P.sb("big", [128, FC, TB], BF16)
    g_sb = P.sb("g_sb", [128, 4, DC], F32)
    rstd = P.sb("rstd", [128, TB], F32)
    tmp = P.sb("tmp", [128, TB], F32)
    tmp2 = P.sb("tmp2", [128, 2, TB], F32)
    NPS = 6
    pss = [P.ps(f"ps{i}") for i in range(NPS)]
    ps_ss = P.ps("ps_ss")
    ring = WRing(P, wdram, wmeta, nslots=6)

    C.eps = P.sb("eps", [128, 1], F32)
    P.add("dve", lambda e: e.memset(C.ones[:, :], 1.0), writes=["ones"])
    P.add("dve", lambda e: e.memset(C.eps[:, :], EPS), writes=["eps"])
    for i, g in enumerate([gpost, gpre, gpost2, bout]):
        P.dma("sp", g_sb[:, i, :], g[:, :], writes=[("g", i)])

    psi = [0]

    def next_ps():
        i = psi[0] % NPS
        psi[0] += 1
        return pss[i], ("ps", i)

    def matmul_group(wtiles, rhs_of_k, rhs_keys_of_k, evac):
        ps, pkey = next_ps()
        kbase = 0
        total = sum(nk for _, nk in wtiles)
        for tid, nk in wtiles:
            wap, wkey = ring.fetch(tid)
            for k in range(nk):
                kk = kbase + k
                P.add("pe", lambda e, wap=wap, k=k, kk=kk: e.matmul(
                    ps[:, :], wap[:, k * 128:(k + 1) * 128], rhs_of_k(kk), start=(kk == 0), stop=(kk == total - 1)),
                    reads=[wkey, rhs_keys_of_k(kk)], writes=[pkey])
            kbase += nk
        evac(ps, pkey)

    def post_norm_residual(gi):
        emit_rmsnorm_stats(P, C, y_sb, [("y", c) for c in range(DC)], h_sb, [("h", c) for c in range(DC)], DC,
                           ps_ss, "ps_ss", rstd, "rstd", tmp, "tmp", D)
        for c in range(DC):
            t2 = tmp2[:, c % 2, :]
            P.add("dve", lambda e, c=c, t2=t2: e.scalar_tensor_tensor(
                out=t2, in0=y_sb[:, c, :], scalar=g_sb[:, gi, c:c + 1], in1=rstd[:, :], op0=ALU.mult, op1=ALU.mult),
                reads=[("y", c), "rstd", ("g", gi)], writes=[("tmp2", c % 2)])
            P.add("dve", lambda e, c=c, t2=t2: e.tensor_tensor(out=x_sb[:, c, :], in0=x_sb[:, c, :], in1=t2,
                                                                 op=ALU.add),
                  reads=[("tmp2", c % 2), ("x", c)], writes=[("x", c)])

    for b in range(NB):
        ts = slice(b * TB, (b + 1) * TB)
        for c0 in range(0, DC, 4):
            P.dma("sp", x_sb[:, c0:c0 + 4, :], xTv[:, c0:c0 + 4, ts], writes=[("x", c) for c in range(c0, c0 + 4)])
        for c0 in range(0, KC, 8):
            c1 = min(KC, c0 + 8)
            P.dma("sp", big[:, c0:c1, :], mTv[:, c0:c1, ts], writes=[("big", c) for c in range(c0, c1)])
        for m in range(DC):
            def evac(ps, pkey, m=m):
                if has_bias:
                    P.add("act", lambda e: e.activation(out=y_sb[:, m, :], in_=ps[:, :], func=AF.Identity,
                                                        bias=g_sb[:, 3, m:m + 1]),
                          reads=[pkey, ("g", 3)], writes=[("y", m)])
                else:
                    P.add("act", lambda e: e.copy(out=y_sb[:, m, :], in_=ps[:, :]), reads=[pkey], writes=[("y", m)])
            matmul_group(wplan["wo"][m], lambda kk: big[:, kk, :], lambda kk: ("big", kk), evac)
        post_norm_residual(0)
        emit_rmsnorm_stats(P, C, x_sb, [("x", c) for c in range(DC)], h_sb, [("h", c) for c in range(DC)], DC,
                           ps_ss, "ps_ss", rstd, "rstd", tmp, "tmp", D)
        for c in range(DC):
            P.add("dve", lambda e, c=c: e.scalar_tensor_tensor(
                out=h_sb[:, c, :], in0=x_sb[:, c, :], scalar=g_sb[:, 1, c:c + 1], in1=rstd[:, :], op0=ALU.mult,
                op1=ALU.mult), reads=[("x", c), "rstd", ("g", 1)], writes=[("h", c)])
        for m in range(FC):
            def evac(ps, pkey, m=m):
                t2 = tmp2[:, m % 2, :]
                P.add("act", lambda e: e.activation(out=t2, in_=ps[:, :], func=AF.Relu), reads=[pkey],
                      writes=[("tmp2", m % 2)])
                P.add("dve", lambda e: e.tensor_tensor(out=big[:, m, :], in0=t2, in1=t2, op=ALU.mult),
                      reads=[("tmp2", m % 2)], writes=[("big", m)])
            matmul_group(wplan["w1"][m], lambda kk: h_sb[:, kk, :], lambda kk: ("h", kk), evac)
        for m in range(DC):
            def evac(ps, pkey, m=m):
                P.add("act", lambda e: e.copy(out=y_sb[:, m, :], in_=ps[:, :]), reads=[pkey], writes=[("y", m)])
            matmul_group(wplan["w2"][m], lambda kk: big[:, kk, :], lambda kk: ("big", kk), evac)
        post_norm_residual(2)
        for c0 in range(0, DC, 4):
            P.dma("sp", oTv[:, c0:c0 + 4, ts], x_sb[:, c0:c0 + 4, :], reads=[("x", c) for c in range(c0, c0 + 4)])
    P.emit()
    return nc


def prep_post_mlp_weights(w_o, w1, w2):
    ws = WStream()
    plan = {"wo": ws.add_matrix(w_o), "w1": ws.add_matrix(w1), "w2": ws.add_matrix(w2)}
    return ws.array(), ws.meta, plan


class Bld:
    def __init__(self, nc, wdram, wmeta, nps=6, nslots=6, with_ss=True):
        self.P = Prog(nc)
        P = self.P
        self.C = Ctx()
        self.C.ones = P.sb("ones", [128, 128], BF16)
        self.C.eps = P.sb("eps", [128, 1], F32)
        P.add("dve", lambda e: e.memset(self.C.ones[:, :], 1.0), writes=["ones"])
        P.add("dve", lambda e: e.memset(self.C.eps[:, :], EPS), writes=["eps"])
        self.pss = [P.ps(f"ps{i}") for i in range(nps)]
        self.nps = nps
        self.psi = 0
        if with_ss:
            self.ps_ss = P.ps("ps_ss")
            self.rstd = P.sb("rstd", [128, TB], F32)
            self.tmp = P.sb("tmp", [128, TB], F32)
        self.ring = WRing(P, wdram, wmeta, nslots=nslots) if wdram is not None else None

    def next_ps(self):
        i = self.psi % self.nps
        self.psi += 1
        return self.pss[i], ("ps", i)

    def group(self, wtiles, rhs_of_k, key_of_k, evac, mw=128, ncol=TB):
        P = self.P
        ps, pkey = self.next_ps()
        total = sum(nk for _, nk in wtiles)
        kbase = 0
        for tid, nk in wtiles:
            wap, wkey = self.ring.fetch(tid)
            for k in range(nk):
                kk = kbase + k
                P.add("pe", lambda e, wap=wap, k=k, kk=kk: e.matmul(
                    ps[0:mw, :ncol], wap[:, k * mw:(k + 1) * mw], rhs_of_k(kk), start=(kk == 0),
                    stop=(kk == total - 1)), reads=[wkey, key_of_k(kk)], writes=[pkey])
            kbase += nk
        evac(ps, pkey)

    def norm_stats(self, src, src_keys, sq, sq_keys, nch, dim, ntok=TB):
        emit_rmsnorm_stats(self.P, self.C, src, src_keys, sq, sq_keys, nch, self.ps_ss, "ps_ss", self.rstd, "rstd",
                           self.tmp, "tmp", dim, ntok)

    def norm_apply(self, dst, dst_keys, src, src_keys, g_ap_of_c, g_key, nch, eng="dve"):
        for c in range(nch):
            self.P.add(eng, lambda e, c=c: e.scalar_tensor_tensor(
                out=dst[:, c, :], in0=src[:, c, :], scalar=g_ap_of_c(c), in1=self.rstd[:, :], op0=ALU.mult,
                op1=ALU.mult), reads=[src_keys[c], "rstd", g_key], writes=[dst_keys[c]])


TWO_PI = 6.283185307179586
CW1 = 6.28125
CW2 = TWO_PI - CW1
PI_LO = 3.1415925


def emit_rope_tables(B, pos_i, pos_key, rc, n, cos2, sin2, tkey):
    P = B.P
    W = B.rope_ws
    ang, kfl, r, m = W[:, 0, :n], W[:, 1, :n], W[:, 2, :n], W[:, 3, :n]
    ki = B.rope_wi[:, :n]
    K = "ropews"
    P.add("dve", lambda e: e.tensor_copy(out=kfl, in_=pos_i), reads=[pos_key], writes=[K])
    P.add("dve", lambda e: e.tensor_scalar(out=ang, in0=kfl, scalar1=rc[:, 0:1], scalar2=None, op0=ALU.mult),
          reads=[K, "rc"], writes=[K])
    P.add("dve", lambda e: e.tensor_scalar(out=ki, in0=ang, scalar1=1.0 / TWO_PI, scalar2=None, op0=ALU.mult),
          reads=[K], writes=[K])
    P.add("dve", lambda e: e.tensor_copy(out=kfl, in_=ki), reads=[K], writes=[K])
    P.add("dve", lambda e: e.scalar_tensor_tensor(out=r, in0=kfl, scalar=-CW1, in1=ang, op0=ALU.mult, op1=ALU.add),
          reads=[K], writes=[K])
    P.add("dve", lambda e: e.scalar_tensor_tensor(out=r, in0=kfl, scalar=-CW2, in1=r, op0=ALU.mult, op1=ALU.add),
          reads=[K], writes=[K])
    P.add("dve", lambda e: e.tensor_scalar(out=m, in0=r, scalar1=np.pi, scalar2=-TWO_PI, op0=ALU.is_gt,
                                           op1=ALU.mult), reads=[K], writes=[K])
    P.add("dve", lambda e: e.tensor_tensor(out=r, in0=r, in1=m, op=ALU.add), reads=[K], writes=[K])
    P.add("dve", lambda e: e.tensor_scalar(out=r, in0=r, scalar1=PI_LO, scalar2=-PI_LO, op0=ALU.min, op1=ALU.max),
          reads=[K], writes=[K])
    P.add("act", lambda e: e.activation(out=m, in_=r, func=AF.Sin), reads=[K], writes=[K])
    P.add("act", lambda e: e.activation(out=ang, in_=r, func=AF.Sin, scale=0.5), reads=[K], writes=[K])
    P.add("dve", lambda e: e.tensor_scalar(out=sin2, in0=m, scalar1=rc[:, 1:2], scalar2=None, op0=ALU.mult),
          reads=[K, "rc"], writes=[tkey])
    P.add("dve", lambda e: e.tensor_tensor(out=ang, in0=ang, in1=ang, op=ALU.mult), reads=[K], writes=[K])
    P.add("dve", lambda e: e.tensor_scalar(out=cos2, in0=ang, scalar1=-2.0, scalar2=1.0, op0=ALU.mult, op1=ALU.add),
          reads=[K], writes=[tkey])


def rope_consts():
    half = 32
    inv = (10000.0 ** (-(np.arange(half, dtype=np.float32) / np.float32(half)))).astype(np.float32)
    rc = np.zeros((64, 2), np.float32)
    rc[:, 0] = np.concatenate([inv, inv])
    rc[:32, 1] = -1.0
    rc[32:, 1] = 1.0
    return rc


def build_mla_proj(T, wmeta, wplan, stop=99):
    nc = bass.Bass("TRN2", target_bir_lowering=False)
    NB = T // TB
    xT = nc.dram_tensor("xT", [D, T], F32, kind="ExternalInput").ap()
    pos = nc.dram_tensor("pos", [64, T], I32, kind="ExternalInput").ap()
    rcd = nc.dram_tensor("rc", [64, 2], F32, kind="ExternalInput").ap()
    wdram = nc.dram_tensor("wstream", [128, wmeta[-1][0] + wmeta[-1][1]], F32, kind="ExternalInput").ap()
    gd = nc.dram_tensor("gvecs", [128, DC + 8], F32, kind="ExternalInput").ap()
    qn_o = nc.dram_tensor("qn", [128, HEADS, T], BF16, kind="ExternalOutput").ap()
    qr_o = nc.dram_tensor("qr", [64, HEADS, T], BF16, kind="ExternalOutput").ap()
    kn_o = nc.dram_tensor("kn", [128, HEADS, T], BF16, kind="ExternalOutput").ap()
    kr_o = nc.dram_tensor("kr", [64, T], BF16, kind="ExternalOutput").ap()
    v_o = nc.dram_tensor("v", [T, D], BF16, kind="ExternalOutput").ap()
    xTv = xT.rearrange("(c p) t -> p c t", p=128)
    v_ov = v_o.rearrange("(t p) c -> p t c", p=128)

    B = Bld(nc, wdram, wmeta)
    P = B.P
    x_sb = P.sb("x_sb", [128, DC, TB], F32)
    h_sb = P.sb("h_sb", [128, DC, TB], BF16)
    cqp = P.sb("cqp", [128, 4, TB], F32)
    ckvp = P.sb("ckvp", [128, 4, TB], F32)
    cqn = P.sb("cqn", [128, 4, TB], BF16)
    ckvn = P.sb("ckvn", [128, 4, TB], BF16)
    g_sb = P.sb("g_sb", [128, DC + 8], F32)
    rc = P.sb("rc_sb", [64, 2], F32)
    pos_sb = P.sb("pos_sb", [64, TB], I32)
    B.rope_ws = P.sb("rope_ws", [64, 4, TB], F32)
    B.rope_wi = P.sb("rope_wi", [64, TB], I32)
    cos2 = P.sb("cos2", [64, TB], F32)
    sin2 = P.sb("sin2", [64, TB], F32)
    rt = P.sb("rt", [64, 2, 2, TB], F32)
    st_qn = P.sb("st_qn", [128, HEADS, TB], BF16)
    st_qr = P.sb("st_qr", [64, HEADS, TB], BF16)
    st_kn = P.sb("st_kn", [128, HEADS, TB], BF16)
    st_kr = P.sb("st_kr", [64, TB], BF16)
    st_v = P.sb("st_v", [128, 4, D], BF16)

    P.dma("sp", g_sb[:, :], gd[:, :], writes=["g"])
    P.dma("sp", rc[:, :], rcd[:, :], writes=["rc"])

    rti = [0]

    def rope_apply(ps_a, ka, ps_b, kb, out_ap, out_key):
        i = rti[0] % 2
        rti[0] += 1
        t1, t2 = rt[:, i, 0, :], rt[:, i, 1, :]
        P.add("dve", lambda e: e.tensor_tensor(out=t1, in0=ps_a[0:64, :], in1=cos2[:, :], op=ALU.mult),
              reads=[ka, "tables"], writes=[("rt", i, 0)])
        P.add("dve", lambda e: e.tensor_tensor(out=t2, in0=ps_b[0:64, :], in1=sin2[:, :], op=ALU.mult),
              reads=[kb, "tables"], writes=[("rt", i, 1)])
        P.add("dve", lambda e: e.tensor_tensor(out=out_ap, in0=t1, in1=t2, op=ALU.add),
              reads=[("rt", i, 0), ("rt", i, 1)], writes=[out_key])

    def pair_group(tiles_a, tiles_b, rhs_of_k, key_of_k, out_ap, out_key):
        hold = {}

        def ev_a(ps, pkey):
            hold["a"] = (ps, pkey)

        def ev_b(ps, pkey):
            pa, ka = hold["a"]
            rope_apply(pa, ka, ps, pkey, out_ap, out_key)
        B.group(tiles_a, rhs_of_k, key_of_k, ev_a, mw=64)
        B.group(tiles_b, rhs_of_k, key_of_k, ev_b, mw=64)

    def body(b):
        ts = slice(b * TB, (b + 1) * TB)
        for c0 in range(0, DC, 4):
            P.dma("sp", x_sb[:, c0:c0 + 4, :], xTv[:, c0:c0 + 4, ts], writes=[("x", c) for c in range(c0, c0 + 4)])
        P.dma("sp", pos_sb[:, :], pos[:, ts], writes=["pos"])
        emit_rope_tables(B, pos_sb[:, :], "pos", rc, TB, cos2[:, :], sin2[:, :], "tables")
        if stop <= 1:
            return
        B.norm_stats(x_sb, [("x", c) for c in range(DC)], h_sb, [("h", c) for c in range(DC)], DC, D)
        B.norm_apply(h_sb, [("h", c) for c in range(DC)], x_sb, [("x", c) for c in range(DC)],
                     lambda c: g_sb[:, c:c + 1], "g", DC)
        hk = lambda kk: h_sb[:, kk, :]
        hkey = lambda kk: ("h", kk)
        for m in range(4):
            B.group(wplan["dq"][m], hk, hkey, lambda ps, pkey, m=m: P.add(
                "act", lambda e: e.copy(out=cqp[:, m, :], in_=ps[:, :]), reads=[pkey], writes=[("cqp", m)]))
        for m in range(4):
            B.group(wplan["dkv"][m], hk, hkey, lambda ps, pkey, m=m: P.add(
                "act", lambda e: e.copy(out=ckvp[:, m, :], in_=ps[:, :]), reads=[pkey], writes=[("ckvp", m)]))
        if stop <= 2:
            return
        pair_group(wplan["kr"][0], wplan["kr_sw"][0], hk, hkey, st_kr[:, :], "st_kr")
        P.dma("sp", kr_o[:, ts], st_kr[:, :], reads=["st_kr"])
        if stop <= 3:
            return
        B.norm_stats(cqp, [("cqp", c) for c in range(4)], cqn, [("cqn", c) for c in range(4)], 4, 512)
        B.norm_apply(cqn, [("cqn", c) for c in range(4)], cqp, [("cqp", c) for c in range(4)],
                     lambda c: g_sb[:, DC + c:DC + c + 1], "g", 4)
        B.norm_stats(ckvp, [("ckvp", c) for c in range(4)], ckvn, [("ckvn", c) for c in range(4)], 4, 512)
        B.norm_apply(ckvn, [("ckvn", c) for c in range(4)], ckvp, [("ckvp", c) for c in range(4)],
                     lambda c: g_sb[:, DC + 4 + c:DC + 4 + c + 1], "g", 4)
        if stop <= 4:
            return
        qk = lambda kk: cqn[:, kk, :]
        qkey = lambda kk: ("cqn", kk)
        kk_ = lambda kk: ckvn[:, kk, :]
        kkey = lambda kk: ("ckvn", kk)
        for h in range(HEADS):
            B.group(wplan["uq_n"][h], qk, qkey, lambda ps, pkey, h=h: P.add(
                "act", lambda e: e.copy(out=st_qn[:, h, :], in_=ps[:, :]), reads=[pkey], writes=[("st_qn", h)]))
        P.dma("sp", qn_o[:, :, ts], st_qn[:, :, :], reads=[("st_qn", h) for h in range(HEADS)])
        if stop <= 5:
            return
        for h in range(HEADS):
            pair_group(wplan["uq_r"][h], wplan["uq_rs"][h], qk, qkey, st_qr[:, h, :], ("st_qr", h))
        P.dma("sp", qr_o[:, :, ts], st_qr[:, :, :], reads=[("st_qr", h) for h in range(HEADS)])
        if stop <= 6:
            return
        for h in range(HEADS):
            B.group(wplan["uk"][h], kk_, kkey, lambda ps, pkey, h=h: P.add(
                "act", lambda e: e.copy(out=st_kn[:, h, :], in_=ps[:, :]), reads=[pkey], writes=[("st_kn", h)]))
        P.dma("sp", kn_o[:, :, ts], st_kn[:, :, :], reads=[("st_kn", h) for h in range(HEADS)])
        if stop <= 7:
            return
        for cg in range(4):
            wap, wkey = B.ring.fetch(wplan["uv"][cg])
            for tt in range(4):
                ps, pkey = B.next_ps()
                for k in range(4):
                    P.add("pe", lambda e, ps=ps, k=k, tt=tt, wap=wap: e.matmul(
                        ps[:, :], ckvn[:, k, tt * 128:(tt + 1) * 128], wap[:, k * 512:(k + 1) * 512],
                        start=(k == 0), stop=(k == 3)), reads=[wkey, ("ckvn", k)], writes=[pkey])
                P.add("act", lambda e, ps=ps, tt=tt, cg=cg: e.copy(out=st_v[:, tt, cg * 512:(cg + 1) * 512],
                                                                    in_=ps[:, :]),
                      reads=[pkey], writes=[("st_v", tt, cg)])
        if stop <= 8:
            return
        for hf in range(2):
            P.dma("sp", v_ov[:, b * 4:(b + 1) * 4, hf * 1024:(hf + 1) * 1024], st_v[:, :, hf * 1024:(hf + 1) * 1024],
                  reads=[("st_v", tt, cg) for tt in range(4) for cg in range(2 * hf, 2 * hf + 2)])
    for b in range(NB):
        body(b)
    P.emit()
    return nc


def prep_mla_proj_weights(w_dq, w_dkv, w_uq, w_ukv):
    ws = WStream()
    plan = {}
    plan["dq"] = ws.add_matrix(w_dq)
    plan["dkv"] = ws.add_matrix(w_dkv[:, :512])
    kr = w_dkv[:, 512:576]
    plan["kr"] = ws.add_matrix(kr, mw=64)
    plan["kr_sw"] = ws.add_matrix(np.concatenate([kr[:, 32:], kr[:, :32]], axis=1), mw=64)
    uq = w_uq.reshape(512, HEADS, 192)
    plan["uq_n"] = ws.add_matrix(np.ascontiguousarray(uq[:, :, :128]).reshape(512, HEADS * 128))
    plan["uq_r"] = []
    plan["uq_rs"] = []
    for h in range(HEADS):
        r = uq[:, h, 128:192]
        plan["uq_r"].append(ws.add_matrix(r, mw=64)[0])
        plan["uq_rs"].append(ws.add_matrix(np.concatenate([r[:, 32:], r[:, :32]], axis=1), mw=64)[0])
    ukv = w_ukv.reshape(512, HEADS, 256)
    plan["uk"] = ws.add_matrix(np.ascontiguousarray(ukv[:, :, :128]).reshape(512, HEADS * 128))
    wv = np.ascontiguousarray(ukv[:, :, 128:]).reshape(512, HEADS * 128)
    plan["uv"] = []
    for cg in range(4):
        t = wv[:, cg * 512:(cg + 1) * 512].reshape(4, 128, 512).transpose(1, 0, 2).reshape(128, 2048)
        plan["uv"].append(ws.add_raw(t))
    return ws.array(), ws.meta, plan


ATT_SCALE = 192.0 ** -0.5


def attn_masks():
    k = np.arange(128)[:, None, None] + 128 * np.arange(4)[None, :, None]
    q = np.arange(512)[None, None, :]
    return ((k // 64) <= (q // 64)).astype(np.float32).astype(NPBF16)


def build_attn(S, NH):
    nc = bass.Bass("TRN2", target_bir_lowering=False)
    NQ = S // TB
    NKP = S // 1024
    qn = nc.dram_tensor("qn", [NH, 128, S], BF16, kind="ExternalInput").ap()
    qr = nc.dram_tensor("qr", [NH, 64, S], BF16, kind="ExternalInput").ap()
    kn = nc.dram_tensor("kn", [NH, 128, S], BF16, kind="ExternalInput").ap()
    kr = nc.dram_tensor("kr", [64, S], BF16, kind="ExternalInput").ap()
    v = nc.dram_tensor("v", [NH, 128, S // 128, 128], BF16, kind="ExternalInput").ap()
    msk = nc.dram_tensor("masks", [128, 4, 512], BF16, kind="ExternalInput").ap()
    o = nc.dram_tensor("o", [NH, 128, S], BF16, kind="ExternalOutput").ap()

    emit_attn(nc, S, NH, qn, qr, kn, kr, v, msk, o)
    return nc


def emit_attn(nc, S, NH, qn, qr, kn, kr, v, msk, o):
    NQ = S // TB
    NKP = S // 1024
    B = Bld(nc, None, None, nps=4, with_ss=False)
    P = B.P
    C = B.C
    ps_o = [P.ps(f"ps_o{i}") for i in range(2)]
    ps_d = [P.ps(f"ps_d{i}") for i in range(2)]
    kn_sb = [P.sb(f"kn_sb{i}", [128, S], BF16) for i in range(2)]
    v_sb = [P.sb(f"v_sb{i}", [128, S // 128, 128], BF16) for i in range(2)]
    kr_sb = P.sb("kr_sb", [128, S], BF16)
    m_sb = P.sb("m_sb", [128, 4, 512], BF16)
    qn_sb = [P.sb(f"qn_sb{i}", [128, TB], BF16) for i in range(2)]
    qr_sb = [P.sb(f"qr_sb{i}", [128, TB], BF16) for i in range(2)]
    NPT = 6
    pt = [P.sb(f"pt{i}", [128, TB], BF16) for i in range(NPT)]
    rec = [P.sb(f"rec{i}", [128, TB], F32) for i in range(2)]
    ost = [P.sb(f"ost{i}", [128, TB], BF16) for i in range(2)]
    acc = [P.sb(f"acc{i}", [128, TB], F32) for i in range(2)]
    ones_f = P.sb("ones_f", [128, 128], F32)
    P.add("dve", lambda e: e.memset(ones_f[:, :], 1.0), writes=["ones_f"])

    for j in range(4):
        P.dma("sp", m_sb[:, j, :], msk[:, j, :], writes=[("m", j)])
    for p in range(NKP):
        P.add("pool", lambda e, p=p: e.memset(kr_sb[:, p * 1024:(p + 1) * 1024], 0.0), writes=[("kr", p)])
    for i in range(2):
        P.add("pool", lambda e, i=i: e.memset(qr_sb[i][:, :], 0.0), writes=[("qr", i)])
    for p in range(NKP):
        P.dma("sp", kr_sb[0:64, p * 1024:(p + 1) * 1024], kr[:, p * 1024:(p + 1) * 1024], writes=[("kr", p)])
    pti = [0]
    qi = [0]
    def qblock(h, hp, qb, i):
        qs = slice(qb * TB, (qb + 1) * TB)
        P.dma("sp", qn_sb[i][:, :], qn[h, :, qs], writes=[("qn", i)])
        P.dma("sp", qr_sb[i][0:64, :], qr[h, :, qs], writes=[("qr", i)])
        nkt = 4 * (qb + 1)
        tiles = {}

        def S_(kt):
            ps, pkey = B.next_ps()
            j = pti[0] % NPT
            pti[0] += 1
            tiles[kt] = j
            ks = slice(kt * 128, (kt + 1) * 128)
            P.add("pe", lambda e: e.matmul(ps[:, :], kn_sb[hp][:, ks], qn_sb[i][:, :], start=True, stop=False),
                  reads=[("kn", hp, kt // 8), ("qn", i)], writes=[pkey])
            P.add("pe", lambda e: e.matmul(ps[:, :], kr_sb[:, ks], qr_sb[i][:, :], start=False, stop=True),
                  reads=[("kr", kt // 8), ("qr", i)], writes=[pkey])
            P.add("act", lambda e: e.activation(out=pt[j][:, :], in_=ps[:, :], func=AF.Exp, scale=ATT_SCALE),
                  reads=[pkey], writes=[("pt", j)])
            if kt >= 4 * qb:
                jj = kt - 4 * qb
                P.add("dve", lambda e: e.tensor_tensor(out=pt[j][:, :], in0=pt[j][:, :], in1=m_sb[:, jj, :],
                                                       op=ALU.mult),
                      reads=[("pt", j), ("m", jj)], writes=[("pt", j)])

        def PV_(kt):
            j = tiles[kt]
            P.add("pe", lambda e: e.matmul(ps_o[i][:, :], v_sb[hp][:, kt, :], pt[j][:, :], start=(kt == 0),
                                           stop=(kt == nkt - 1)),
                  reads=[("v", hp, kt // 8), ("pt", j)], writes=[("ps_o", i)])
            if kt == 0:
                P.add("dve", lambda e: e.tensor_copy(out=acc[i][:, :], in_=pt[j][:, :]),
                      reads=[("pt", j)], writes=[("acc", i)])
            else:
                P.add("dve", lambda e: e.tensor_tensor(out=acc[i][:, :], in0=acc[i][:, :], in1=pt[j][:, :],
                                                       op=ALU.add),
                      reads=[("pt", j), ("acc", i)], writes=[("acc", i)])

        LA = 3
        for kt in range(min(LA, nkt)):
            S_(kt)
        for kt in range(nkt):
            PV_(kt)
            if kt + LA < nkt:
                S_(kt + LA)
        P.add("pe", lambda e: e.matmul(ps_d[i][:, :], ones_f[:, :], acc[i][:, :], start=True, stop=True),
              reads=["ones_f", ("acc", i)], writes=[("ps_d", i)])
        P.add("dve", lambda e, i=i: e.reciprocal(out=rec[i][:, :], in_=ps_d[i][:, :]),
              reads=[("ps_d", i)], writes=[("rec", i)])
        P.add("dve", lambda e, i=i: e.tensor_tensor(out=ost[i][:, :], in0=ps_o[i][:, :], in1=rec[i][:, :],
                                                    op=ALU.mult),
              reads=[("ps_o", i), ("rec", i)], writes=[("ost", i)])
        P.dma("sp", o[h, :, qs], ost[i][:, :], reads=[("ost", i)])
    for h in range(NH):
        hp = h % 2
        for p in range(NKP):
            P.dma("sp", kn_sb[hp][:, p * 1024:(p + 1) * 1024], kn[h, :, p * 1024:(p + 1) * 1024],
                  writes=[("kn", hp, p)])
            P.dma("sp", v_sb[hp][:, p * 8:(p + 1) * 8, :], v[h, :, p * 8:(p + 1) * 8, :], writes=[("v", hp, p)])
        for qb in range(NQ):
            qblock(h, hp, qb, qi[0] % 2)
            qi[0] += 1
    P.emit()


CH = 84
NJ = 4


def build_lru(NT, NBATCH, wmeta, wplan):
    nc = bass.Bass("TRN2", target_bir_lowering=False)
    NB = NT // TB
    NBB = NB // NBATCH
    xT = nc.dram_tensor("xT", [D, NT], F32, kind="ExternalInput").ap()
    wdram = nc.dram_tensor("wstream", [128, wmeta[-1][0] + wmeta[-1][1]], F32, kind="ExternalInput").ap()
    wgd = nc.dram_tensor("wg", [CH, 2 * NJ * 2 * CH], F32, kind="ExternalInput").ap()
    vecd = nc.dram_tensor("vec", [CH, NJ, 10], F32, kind="ExternalInput").ap()
    gd = nc.dram_tensor("gpre", [128, DC], F32, kind="ExternalInput").ap()
    hy = nc.dram_tensor("hy", [NJ, CH, NT], BF16, kind="ExternalOutput").ap()
    xTv = xT.rearrange("(c p) t -> p c t", p=128)
    hyv = hy.rearrange("j p t -> p j t")

    B = Bld(nc, wdram, wmeta)
    P = B.P
    x_sb = P.sb("x_sb", [128, DC, TB], F32)
    h_sb = P.sb("h_sb", [128, DC, TB], BF16)
    g_sb = P.sb("g_sb", [128, DC], F32)
    wg_sb = P.sb("wg_sb", [128, 2, NJ, 2, CH], BF16)
    vec = P.sb("vec_sb", [128, NJ, 10], F32)
    cc = P.sb("cc", [128, 4, NJ], F32)
    one_c = P.sb("one_c", [128, 1], F32)
    y_sb = P.sb("y_sb", [128, NJ, TB], F32)
    uxb = P.sb("uxb", [128, NJ, TB + 4], F32)
    u_sb = P.sb("u_sb", [128, NJ, TB], F32)
    u_bf = P.sb("u_bf", [128, NJ, TB], BF16)
    r_sb = P.sb("r_sb", [128, NJ, TB], F32)
    ig_sb = P.sb("ig_sb", [128, NJ, TB], F32)
    a_sb = P.sb("a_sb", [128, NJ, TB], F32)
    m_sb = P.sb("m_sb", [128, NJ, TB], F32)
    inp_sb = P.sb("inp_sb", [128, NJ, TB], F32)
    hs_sb = P.sb("hs_sb", [128, NJ, TB], F32)
    ost = P.sb("ost", [128, NJ, TB], BF16)
    state = P.sb("state", [128, NJ], F32)

    P.dma("sp", g_sb[:, :], gd[:, :], writes=["g"])
    P.dma("sp", vec[0:CH, :, :], vecd[:, :, :], writes=["vec"])
    P.dma("pool", wg_sb[0:CH, :, :, :, :].rearrange("p a b c d -> p (a b c d)"), wgd[:, :], writes=["wg"])
    P.add("dve", lambda e: e.memset(one_c[:, :], 1.0), writes=["one_c"])
    e_ = cc[0:CH, 0, :]
    t_ = cc[0:CH, 1, :]
    P.add("act", lambda e: e.activation(out=e_, in_=vec[0:CH, :, 5], func=AF.Exp, scale=-1.0), reads=["vec"],
          writes=["cc"])
    coef = [1.0 / 5, -1.0 / 4, 1.0 / 3, -1.0 / 2, 1.0]
    P.add("dve", lambda e: e.tensor_scalar(out=t_, in0=e_, scalar1=-1.0 / 6, scalar2=coef[0], op0=ALU.mult,
                                           op1=ALU.add), reads=["cc"], writes=["cc"])
    for cf in coef[1:]:
        P.add("dve", lambda e: e.tensor_tensor(out=t_, in0=t_, in1=e_, op=ALU.mult), reads=["cc"], writes=["cc"])
        P.add("dve", lambda e, cf=cf: e.tensor_scalar(out=t_, in0=t_, scalar1=cf, scalar2=None, op0=ALU.add),
              reads=["cc"], writes=["cc"])
    P.add("dve", lambda e: e.tensor_tensor(out=t_, in0=t_, in1=e_, op=ALU.mult), reads=["cc"], writes=["cc"])
    P.add("dve", lambda e: e.tensor_scalar(out=cc[0:CH, 2, :], in0=t_, scalar1=-8.0, scalar2=None, op0=ALU.mult),
          reads=["cc"], writes=["cc"])
    P.add("dve", lambda e: e.tensor_scalar(out=cc[0:CH, 3, :], in0=t_, scalar1=-16.0, scalar2=None, op0=ALU.mult),
          reads=["cc"], writes=["cc"])

    def body(b):
        ts = slice(b * TB, (b + 1) * TB)
        if b % NBB == 0:
            P.add("dve", lambda e: e.memset(state[:, :], 0.0), writes=[("state", j) for j in range(NJ)])
            P.add("dve", lambda e: e.memset(uxb[:, :, 0:3], 0.0), writes=[("ux", j) for j in range(NJ)])
        for c0 in range(0, DC, 4):
            P.dma("sp", x_sb[:, c0:c0 + 4, :], xTv[:, c0:c0 + 4, ts], writes=[("x", c) for c in range(c0, c0 + 4)])
        B.norm_stats(x_sb, [("x", c) for c in range(DC)], h_sb, [("h", c) for c in range(DC)], DC, D)
        B.norm_apply(h_sb, [("h", c) for c in range(DC)], x_sb, [("x", c) for c in range(DC)],
                     lambda c: g_sb[:, c:c + 1], "g", DC)
        hk = lambda kk: h_sb[:, kk, :]
        hkey = lambda kk: ("h", kk)
        for j in range(NJ):
            B.group(wplan["wy"][j], hk, hkey, lambda ps, pkey, j=j: P.add(
                "act", lambda e: e.activation(out=y_sb[0:CH, j, :], in_=ps[0:CH, :], func=AF.Gelu,
                                              bias=vec[0:CH, j, 0:1]),
                reads=[pkey, "vec"], writes=[("y", j)]), mw=CH)
        for j in range(NJ):
            B.group(wplan["wx"][j], hk, hkey, lambda ps, pkey, j=j: P.add(
                "act", lambda e: e.activation(out=uxb[0:CH, j, 3:3 + TB], in_=ps[0:CH, :], func=AF.Identity,
                                              bias=vec[0:CH, j, 1:2]),
                reads=[pkey, "vec"], writes=[("ux", j)]), mw=CH)
        for j in range(NJ):
            P.add("dve", lambda e, j=j: e.tensor_scalar(out=u_sb[0:CH, j, :], in0=uxb[0:CH, j, 0:TB],
                                                        scalar1=vec[0:CH, j, 6:7], scalar2=vec[0:CH, j, 2:3],
                                                        op0=ALU.mult, op1=ALU.add),
                  reads=[("ux", j), "vec"], writes=[("u", j)])
            for i in range(1, 4):
                P.add("dve", lambda e, j=j, i=i: e.scalar_tensor_tensor(
                    out=u_sb[0:CH, j, :], in0=uxb[0:CH, j, i:i + TB], scalar=vec[0:CH, j, 6 + i:7 + i],
                    in1=u_sb[0:CH, j, :], op0=ALU.mult, op1=ALU.add),
                    reads=[("ux", j), "vec", ("u", j)], writes=[("u", j)])
            P.add("dve", lambda e, j=j: e.tensor_copy(out=uxb[0:CH, j, 0:3], in_=uxb[0:CH, j, TB:TB + 3]),
                  reads=[("ux", j)], writes=[("ux", j)])
            P.add("act", lambda e, j=j: e.copy(out=u_bf[0:CH, j, :], in_=u_sb[0:CH, j, :]),
                  reads=[("u", j)], writes=[("ubf", j)])
        for gi, dst, bidx in ((0, r_sb, 3), (1, ig_sb, 4)):
            for j in range(NJ):
                ps, pkey = B.next_ps()
                for i in range(2):
                    P.add("pe", lambda e, ps=ps, gi=gi, j=j, i=i: e.matmul(
                        ps[0:CH, :], wg_sb[0:CH, gi, j, i, :], u_bf[0:CH, 2 * (j // 2) + i, :], start=(i == 0),
                        stop=(i == 1)), reads=["wg", ("ubf", 2 * (j // 2) + i)], writes=[pkey])
                P.add("act", lambda e, ps=ps, j=j, dst=dst, bidx=bidx: e.activation(
                    out=dst[0:CH, j, :], in_=ps[0:CH, :], func=AF.Sigmoid, bias=vec[0:CH, j, bidx:bidx + 1]),
                    reads=[pkey, "vec"], writes=[("gate", gi, j)])
        for j in range(NJ):
            P.add("act", lambda e, j=j: e.activation(out=a_sb[0:CH, j, :], in_=r_sb[0:CH, j, :], func=AF.Exp,
                                                     scale=cc[0:CH, 2, j:j + 1]),
                  reads=[("gate", 0, j), "cc"], writes=[("a", j)])
            P.add("act", lambda e, j=j: e.activation(out=m_sb[0:CH, j, :], in_=r_sb[0:CH, j, :], func=AF.Exp,
                                                     scale=cc[0:CH, 3, j:j + 1]),
                  reads=[("gate", 0, j), "cc"], writes=[("m", j)])
        for j in range(NJ):
            P.add("dve", lambda e, j=j: e.tensor_scalar(out=m_sb[0:CH, j, :], in0=m_sb[0:CH, j, :], scalar1=-1.0,
                                                        scalar2=1.0, op0=ALU.mult, op1=ALU.add),
                  reads=[("m", j)], writes=[("m", j)])
        for j in range(NJ):
            P.add("act", lambda e, j=j: e.activation(out=m_sb[0:CH, j, :], in_=m_sb[0:CH, j, :], func=AF.Sqrt),
                  reads=[("m", j)], writes=[("m", j)])
        for j in range(NJ):
            P.add("dve", lambda e, j=j: e.tensor_tensor(out=inp_sb[0:CH, j, :], in0=ig_sb[0:CH, j, :],
                                                        in1=u_sb[0:CH, j, :], op=ALU.mult),
                  reads=[("gate", 1, j), ("u", j)], writes=[("inp", j)])
            P.add("dve", lambda e, j=j: e.tensor_tensor(out=inp_sb[0:CH, j, :], in0=inp_sb[0:CH, j, :],
                                                        in1=m_sb[0:CH, j, :], op=ALU.mult),
                  reads=[("inp", j), ("m", j)], writes=[("inp", j)])
            P.add("dve", lambda e, j=j: e.tensor_tensor_scan(
                out=hs_sb[0:CH, j, :], data0=a_sb[0:CH, j, :], data1=inp_sb[0:CH, j, :],
                initial=state[0:CH, j:j + 1], op0=ALU.mult, op1=ALU.add),
                reads=[("a", j), ("inp", j), ("state", j)], writes=[("hs", j)])
            P.add("dve", lambda e, j=j: e.tensor_copy(out=state[0:CH, j:j + 1], in_=hs_sb[0:CH, j, TB - 1:TB]),
                  reads=[("hs", j)], writes=[("state", j)])
            P.add("dve", lambda e, j=j: e.tensor_tensor(out=ost[0:CH, j, :], in0=hs_sb[0:CH, j, :],
                                                        in1=y_sb[0:CH, j, :], op=ALU.mult),
                  reads=[("hs", j), ("y", j)], writes=[("ost", j)])
        P.dma("sp", hyv[:, :, ts], ost[0:CH, :, :], reads=[("ost", j) for j in range(NJ)])

    for b in range(NB):
        body(b)
    P.emit()
    return nc


def prep_lru_weights(core, w_y, b_y, w_x, b_x, conv_w, conv_b, w_ga, b_ga, w_gi, b_gi, lam):
    c0 = core * NJ * CH
    c1 = c0 + NJ * CH
    ws = WStream()
    plan = {"wy": ws.add_matrix(np.ascontiguousarray(w_y[:, c0:c1]), mw=CH),
            "wx": ws.add_matrix(np.ascontiguousarray(w_x[:, c0:c1]), mw=CH)}
    wg = np.zeros((CH, 2, NJ, 2, CH), np.float32)
    for gi, w in enumerate((w_ga, w_gi)):
        for j in range(NJ):
            n = 2 * core + j // 2
            for i in range(2):
                wg[:, gi, j, i, :] = w[n, i * CH:(i + 1) * CH, (j % 2) * CH:(j % 2 + 1) * CH]
    vec = np.zeros((CH, NJ, 10), np.float32)
    for idx, vv in enumerate((b_y, b_x, conv_b, b_ga, b_gi, lam)):
        vec[:, :, idx] = vv[c0:c1].reshape(NJ, CH).T
    for i in range(4):
        vec[:, :, 6 + i] = conv_w[i, c0:c1].reshape(NJ, CH).T
    return ws.array(), ws.meta, plan, np.ascontiguousarray(wg.reshape(CH, -1)), vec


def _run(nc, in_maps):
    res = run_bass_kernel_spmd(nc, in_maps, core_ids=list(range(NCORES)))
    return res.results


def kernel_unfused(x, positions, mix_pre_g, mix_post_g, mlp_pre_g, mlp_post_g,
           mla_w_dq, mla_g_q, mla_w_uq, mla_w_dkv, mla_g_kv, mla_w_ukv, mla_w_o,
           lru_w_y, lru_b_y, lru_w_x, lru_b_x, lru_conv_w, lru_conv_b,
           lru_w_ga, lru_b_ga, lru_w_gi, lru_b_gi, lru_lam, lru_w_out, lru_b_out,
           mlp_w1, mlp_w2):
    f32 = lambda a: np.asarray(a, np.float32)
    x = f32(x)
    positions = np.asarray(positions, np.int32)
    NBATCH, S, _ = x.shape
    T = NBATCH * S // NCORES
    QPB = S // T
    NT = NBATCH * S
    NH = HEADS * NBATCH // NCORES
    xs = x.reshape(NCORES, T, D)
    xT = [np.ascontiguousarray(xs[c].T) for c in range(NCORES)]
    rc = rope_consts()
    masks = attn_masks()
    zero_b = np.zeros((128, DC), np.float32)
    depth = mix_pre_g.shape[0]

    def post_mlp(i, mT, w_o, b_o, KIN):
        warr, wmeta, wplan = prep_post_mlp_weights(f32(w_o), f32(mlp_w1[i]), f32(mlp_w2[i]))
        nc = build_post_mlp(T, KIN, b_o is not None, wmeta, wplan)
        gpost, gpre, gpost2 = gvec(mix_post_g[i]), gvec(mlp_pre_g[i]), gvec(mlp_post_g[i])
        bo = gvec(b_o) if b_o is not None else zero_b
        maps = [{"xT": xT[c], "mT": mT[c], "wstream": warr, "gpost": gpost, "gpre": gpre, "gpost2": gpost2,
                 "bout": bo} for c in range(NCORES)]
        r = _run(nc, maps)
        return [np.asarray(r[c]["oT"]) for c in range(NCORES)]

    for i in range(depth):
        j = i // 2
        if i % 2 == 0:
            warr, wmeta, wplan = prep_mla_proj_weights(f32(mla_w_dq[j]), f32(mla_w_dkv[j]), f32(mla_w_uq[j]),
                                                       f32(mla_w_ukv[j]))
            nc = build_mla_proj(T, wmeta, wplan)
            gv = np.ascontiguousarray(np.concatenate([gvec(mix_pre_g[i]), gvec(mla_g_q[j]), gvec(mla_g_kv[j])],
                                                     axis=1))
            maps = []
            for c in range(NCORES):
                b, q = divmod(c, QPB)
                pos = np.ascontiguousarray(np.broadcast_to(positions[b, q * T:(q + 1) * T][None, :], (64, T)))
                maps.append({"xT": xT[c], "pos": pos, "rc": rc, "wstream": warr, "gvecs": gv})
            r1 = _run(nc, maps)
            del maps
            maps = []
            for c in range(NCORES):
                b, hg = divmod(c, QPB)
                hs = slice(hg * NH, (hg + 1) * NH)
                cores = [b * QPB + q for q in range(QPB)]
                qn = np.concatenate([np.asarray(r1[cc]["qn"])[:, hs, :] for cc in cores], axis=2)
                qr = np.concatenate([np.asarray(r1[cc]["qr"])[:, hs, :] for cc in cores], axis=2)
                kn = np.concatenate([np.asarray(r1[cc]["kn"])[:, hs, :] for cc in cores], axis=2)
                kr = np.concatenate([np.asarray(r1[cc]["kr"]) for cc in cores], axis=1)
                v = np.concatenate([np.asarray(r1[cc]["v"])[:, hg * NH * 128:(hg + 1) * NH * 128] for cc in cores],
                                   axis=0)
                v = v.reshape(S // 128, 128, NH, 128).transpose(2, 1, 0, 3)
                maps.append({"qn": np.ascontiguousarray(qn.transpose(1, 0, 2)),
                             "qr": np.ascontiguousarray(qr.transpose(1, 0, 2)),
                             "kn": np.ascontiguousarray(kn.transpose(1, 0, 2)),
                             "kr": np.ascontiguousarray(kr), "v": np.ascontiguousarray(v), "masks": masks})
            del r1
            nc = build_attn(S, NH)
            r2 = _run(nc, maps)
            del maps
            mT = []
            for c in range(NCORES):
                b, q = divmod(c, QPB)
                parts = [np.asarray(r2[b * QPB + hg]["o"])[:, :, q * T:(q + 1) * T].reshape(NH * 128, T)
                         for hg in range(QPB)]
                mT.append(np.ascontiguousarray(np.concatenate(parts, axis=0)))
            del r2
            xT = post_mlp(i, mT, mla_w_o[j], None, D)
        else:
            xfull = np.ascontiguousarray(np.concatenate(xT, axis=1))
            gp = gvec(mix_pre_g[i])
            maps = []
            wm = wp = None
            for c in range(NCORES):
                warr, wm, wp, wg, vec = prep_lru_weights(
                    c, f32(lru_w_y[j]), f32(lru_b_y[j]), f32(lru_w_x[j]), f32(lru_b_x[j]), f32(lru_conv_w[j]),
                    f32(lru_conv_b[j]), f32(lru_w_ga[j]), f32(lru_b_ga[j]), f32(lru_w_gi[j]), f32(lru_b_gi[j]),
                    f32(lru_lam[j]))
                maps.append({"xT": xfull, "wstream": warr, "wg": wg, "vec": vec, "gpre": gp})
            nc = build_lru(NT, NBATCH, wm, wp)
            r4 = _run(nc, maps)
            del maps, xfull
            mfull = np.concatenate([np.asarray(r4[c]["hy"]).reshape(NJ * CH, NT) for c in range(NCORES)], axis=0)
            del r4
            mT = [np.ascontiguousarray(mfull[:, c * T:(c + 1) * T]) for c in range(NCORES)]
            del mfull
            xT = post_mlp(i, mT, lru_w_out[j], lru_b_out[j], DRNN)
    out = np.stack([xT[c].T for c in range(NCORES)], axis=0).reshape(NBATCH, S, D)
    return np.ascontiguousarray(out.astype(np.float32))


GROUPS = [[0, 1, 2, 3], [4, 5, 6, 7]]
TPG = 4
NHC = HEADS // TPG
FFC = DFF // TPG // 128
LJ = 8


def _blk(ap3, b):
    return ap3[b].rearrange("(c p) t -> p c t", p=128)


class Phase:
    def __init__(self, nc, wdram, wmeta, gdram, nps=6, nslots=6):
        self.B = Bld(nc, wdram, wmeta, nps=nps, nslots=nslots)
        P = self.B.P
        self.P = P
        self.x_sb = P.sb("x_sb", [128, DC, TB], F32)
        self.y_sb = P.sb("y_sb", [128, DC, TB], F32)
        self.h_sb = P.sb("h_sb", [128, DC, TB], BF16)
        self.g_sb = P.sb("g_sb", [128, 3, DC], F32)
        self.tmp2 = P.sb("tmp2", [128, 2, TB], F32)
        P.dma("sp", self.g_sb[:, :, :], gdram[:, :, :], writes=["g"])
        self.xk = [("x", c) for c in range(DC)]
        self.yk = [("y", c) for c in range(DC)]
        self.hk = [("h", c) for c in range(DC)]

    def prologue(self, b, x_src, red, has_bias, x_dst, pre_norm=True):
        P, B = self.P, self.B
        x_sb, y_sb, h_sb, g_sb, tmp2 = self.x_sb, self.y_sb, self.h_sb, self.g_sb, self.tmp2
        for c0 in range(0, DC, 4):
            P.dma("sp", x_sb[:, c0:c0 + 4, :], x_src[:, c0:c0 + 4, :], writes=self.xk[c0:c0 + 4])
        if red is not None:
            rv = _blk(red, b)
            for c0 in range(0, DC, 4):
                P.dma("sp", y_sb[:, c0:c0 + 4, :], rv[:, c0:c0 + 4, :], writes=self.yk[c0:c0 + 4])
            if has_bias:
                for c in range(DC):
                    P.add("act", lambda e, c=c: e.activation(out=y_sb[:, c, :], in_=y_sb[:, c, :], func=AF.Identity,
                                                             bias=g_sb[:, 2, c:c + 1]),
                          reads=[("y", c), "g"], writes=[("y", c)])
            B.norm_stats(y_sb, self.yk, h_sb, self.hk, DC, D)
            for c in range(DC):
                t2 = tmp2[:, c % 2, :]
                P.add("dve", lambda e, c=c, t2=t2: e.scalar_tensor_tensor(
                    out=t2, in0=y_sb[:, c, :], scalar=g_sb[:, 0, c:c + 1], in1=B.rstd[:, :], op0=ALU.mult,
                    op1=ALU.mult), reads=[("y", c), "rstd", "g"], writes=[("tmp2", c % 2)])
                P.add("dve", lambda e, c=c, t2=t2: e.tensor_tensor(out=x_sb[:, c, :], in0=x_sb[:, c, :], in1=t2,
                                                                   op=ALU.add),
                      reads=[("tmp2", c % 2), ("x", c)], writes=[("x", c)])
            if x_dst is not None:
                for c0 in range(0, DC, 4):
                    P.dma("sp", x_dst[:, c0:c0 + 4, :], x_sb[:, c0:c0 + 4, :], reads=self.xk[c0:c0 + 4])
        if pre_norm:
            B.norm_stats(x_sb, self.xk, h_sb, self.hk, DC, D)
            B.norm_apply(h_sb, self.hk, x_sb, self.xk, lambda c: g_sb[:, 1, c:c + 1], "g", DC)

    def store_partial(self, ps, pkey, part, b, m, pst, idx):
        P = self.P
        i = idx % 2
        P.add("act", lambda e: e.copy(out=pst[:, i, :], in_=ps[:, :]), reads=[pkey], writes=[("pst", i)])
        P.dma("sp", part[b, m * 128:(m + 1) * 128, :], pst[:, i, :], reads=[("pst", i)])


def xsrc_view(xin, xres, b, from_input):
    if from_input:
        return xin.rearrange("(c p) t -> p c t", p=128)[:, :, b * TB:(b + 1) * TB]
    return _blk(xres, b)


def phase_allreduce(nc, part, red, NB, uid):
    G = Prog.G
    if G.get("cc_nc") is not nc:
        G["cc_nc"] = nc
        G["cc"] = nc.alloc_semaphore(name="cc_global")
        G["ccv"] = 0
    cc = G["cc"]
    with nc.Block() as blk:
        @blk.gpsimd
        def _(g):
            for b in range(NB):
                g.collective_compute("AllReduce", ALU.add, replica_groups=GROUPS, ins=[part[b, :, :]],
                                     outs=[red[b, :, :]]).then_inc(cc)
                G["ccv"] += 1
                g.wait_ge(cc, G["ccv"])


def phase_mlp(nc, S, xin, xres, part, red, from_input, has_bias, wdram, wmeta, wplan, gdram):
    NB = S // TB
    ph = Phase(nc, wdram, wmeta, gdram)
    P, B = ph.P, ph.B
    h1 = P.sb("h1", [128, FFC, TB], BF16)
    pst = P.sb("pst", [128, 2, TB], F32)

    def body(b):
        ph.prologue(b, xsrc_view(xin, xres, b, from_input), red, has_bias, _blk(xres, b))
        for m in range(FFC):
            def evac(ps, pkey, m=m):
                t2 = ph.tmp2[:, m % 2, :]
                P.add("act", lambda e: e.activation(out=t2, in_=ps[:, :], func=AF.Relu), reads=[pkey],
                      writes=[("tmp2", m % 2)])
                P.add("dve", lambda e: e.tensor_tensor(out=h1[:, m, :], in0=t2, in1=t2, op=ALU.mult),
                      reads=[("tmp2", m % 2)], writes=[("h1", m)])
            B.group(wplan["w1"][m], lambda kk: ph.h_sb[:, kk, :], lambda kk: ("h", kk), evac)
        for m in range(DC):
            B.group(wplan["w2"][m], lambda kk: h1[:, kk, :], lambda kk: ("h1", kk),
                    lambda ps, pkey, m=m: ph.store_partial(ps, pkey, part, b, m, pst, m))
    for b in range(NB):
        body(b)
    P.emit()


def phase_wo(nc, S, o_s, part, wdram, wmeta, wplan):
    NB = S // TB
    B = Bld(nc, wdram, wmeta, with_ss=False)
    P = B.P
    o_sb = P.sb("o_sb", [128, NHC, TB], BF16)
    pst = P.sb("pst", [128, 2, TB], F32)
    ov = o_s.rearrange("h p t -> p h t")

    def body(b):
        P.dma("sp", o_sb[:, :, :], ov[:, :, b * TB:(b + 1) * TB], writes=[("o", h) for h in range(NHC)])
        for m in range(DC):
            def evac(ps, pkey, m=m):
                i = m % 2
                P.add("act", lambda e: e.copy(out=pst[:, i, :], in_=ps[:, :]), reads=[pkey], writes=[("pst", i)])
                P.dma("sp", part[b, m * 128:(m + 1) * 128, :], pst[:, i, :], reads=[("pst", i)])
            B.group(wplan["wo"][m], lambda kk: o_sb[:, kk, :], lambda kk: ("o", kk), evac)
    for b in range(NB):
        body(b)
    P.emit()


def phase_final(nc, S, xres, red, outT, gdram):
    NB = S // TB
    ph = Phase(nc, None, None, gdram)
    ov = outT.rearrange("(c p) t -> p c t", p=128)
    for b in range(NB):
        ph.prologue(b, _blk(xres, b), red, False, ov[:, :, b * TB:(b + 1) * TB], pre_norm=False)
    ph.P.emit()


def phase_mla_proj(nc, S, xin, xres, red, from_input, first, pos, rcd, wdram, wmeta, wplan, gdram, gqkv,
                   qn_o, qr_o, kn_o, kr_o, v_o):
    NB = S // TB
    ph = Phase(nc, wdram, wmeta, gdram)
    P, B = ph.P, ph.B
    h_sb = ph.h_sb
    cqp = P.sb("cqp", [128, 4, TB], F32)
    ckvp = P.sb("ckvp", [128, 4, TB], F32)
    cqn = P.sb("cqn", [128, 4, TB], BF16)
    ckvn = P.sb("ckvn", [128, 4, TB], BF16)
    gq_sb = P.sb("gq_sb", [128, 8], F32)
    rc = P.sb("rc_sb", [64, 2], F32)
    pos_sb = P.sb("pos_sb", [64, TB], I32)
    B.rope_ws = P.sb("rope_ws", [64, 4, TB], F32)
    B.rope_wi = P.sb("rope_wi", [64, TB], I32)
    cos2 = P.sb("cos2", [64, TB], F32)
    sin2 = P.sb("sin2", [64, TB], F32)
    rt = P.sb("rt", [64, 2, 2, TB], F32)
    st_qn = P.sb("st_qn", [128, NHC, TB], BF16)
    st_qr = P.sb("st_qr", [64, NHC, TB], BF16)
    st_kn = P.sb("st_kn", [128, NHC, TB], BF16)
    st_kr = P.sb("st_kr", [64, TB], BF16)
    st_v = P.sb("st_v", [128, 4, NHC * 128], BF16)
    P.dma("sp", gq_sb[:, :], gqkv[:, :], writes=["gq"])
    P.dma("sp", rc[:, :], rcd[:, :], writes=["rc"])
    qn_v = qn_o.rearrange("h p t -> p h t")
    qr_v = qr_o.rearrange("h p t -> p h t")
    kn_v = kn_o.rearrange("h p t -> p h t")
    rti = [0]

    def rope_apply(ps_a, ka, ps_b, kb, out_ap, out_key):
        i = rti[0] % 2
        rti[0] += 1
        t1, t2 = rt[:, i, 0, :], rt[:, i, 1, :]
        P.add("dve", lambda e: e.tensor_tensor(out=t1, in0=ps_a[0:64, :], in1=cos2[:, :], op=ALU.mult),
              reads=[ka, "tables"], writes=[("rt", i, 0)])
        P.add("dve", lambda e: e.tensor_tensor(out=t2, in0=ps_b[0:64, :], in1=sin2[:, :], op=ALU.mult),
              reads=[kb, "tables"], writes=[("rt", i, 1)])
        P.add("dve", lambda e: e.tensor_tensor(out=out_ap, in0=t1, in1=t2, op=ALU.add),
              reads=[("rt", i, 0), ("rt", i, 1)], writes=[out_key])

    def pair_group(tiles_a, tiles_b, rhs_of_k, key_of_k, out_ap, out_key):
        hold = {}

        def ev_a(ps, pkey):
            hold["a"] = (ps, pkey)

        def ev_b(ps, pkey):
            pa, ka = hold["a"]
            rope_apply(pa, ka, ps, pkey, out_ap, out_key)
        B.group(tiles_a, rhs_of_k, key_of_k, ev_a, mw=64)
        B.group(tiles_b, rhs_of_k, key_of_k, ev_b, mw=64)

    def body(b):
        ts = slice(b * TB, (b + 1) * TB)
        ph.prologue(b, xsrc_view(xin, xres, b, from_input), None if first else red, False,
                    None if first else _blk(xres, b))
        P.dma("sp", pos_sb[:, :], pos[:, ts], writes=["pos"])
        emit_rope_tables(B, pos_sb[:, :], "pos", rc, TB, cos2[:, :], sin2[:, :], "tables")
        hk = lambda kk: h_sb[:, kk, :]
        hkey = lambda kk: ("h", kk)
        for m in range(4):
            B.group(wplan["dq"][m], hk, hkey, lambda ps, pkey, m=m: P.add(
                "act", lambda e: e.copy(out=cqp[:, m, :], in_=ps[:, :]), reads=[pkey], writes=[("cqp", m)]))
        for m in range(4):
            B.group(wplan["dkv"][m], hk, hkey, lambda ps, pkey, m=m: P.add(
                "act", lambda e: e.copy(out=ckvp[:, m, :], in_=ps[:, :]), reads=[pkey], writes=[("ckvp", m)]))
        pair_group(wplan["kr"][0], wplan["kr_sw"][0], hk, hkey, st_kr[:, :], "st_kr")
        P.dma("sp", kr_o[:, ts], st_kr[:, :], reads=["st_kr"])
        B.norm_stats(cqp, [("cqp", c) for c in range(4)], cqn, [("cqn", c) for c in range(4)], 4, 512)
        B.norm_apply(cqn, [("cqn", c) for c in range(4)], cqp, [("cqp", c) for c in range(4)],
                     lambda c: gq_sb[:, c:c + 1], "gq", 4)
        B.norm_stats(ckvp, [("ckvp", c) for c in range(4)], ckvn, [("ckvn", c) for c in range(4)], 4, 512)
        B.norm_apply(ckvn, [("ckvn", c) for c in range(4)], ckvp, [("ckvp", c) for c in range(4)],
                     lambda c: gq_sb[:, 4 + c:5 + c], "gq", 4)
        qk = lambda kk: cqn[:, kk, :]
        qkey = lambda kk: ("cqn", kk)
        kk_ = lambda kk: ckvn[:, kk, :]
        kkey = lambda kk: ("ckvn", kk)
        for h in range(NHC):
            B.group(wplan["uq_n"][h], qk, qkey, lambda ps, pkey, h=h: P.add(
                "act", lambda e: e.copy(out=st_qn[:, h, :], in_=ps[:, :]), reads=[pkey], writes=[("st_qn", h)]))
        P.dma("sp", qn_v[:, :, ts], st_qn[:, :, :], reads=[("st_qn", h) for h in range(NHC)])
        for h in range(NHC):
            pair_group(wplan["uq_r"][h], wplan["uq_rs"][h], qk, qkey, st_qr[:, h, :], ("st_qr", h))
        P.dma("sp", qr_v[:, :, ts], st_qr[:, :, :], reads=[("st_qr", h) for h in range(NHC)])
        for h in range(NHC):
            B.group(wplan["uk"][h], kk_, kkey, lambda ps, pkey, h=h: P.add(
                "act", lambda e: e.copy(out=st_kn[:, h, :], in_=ps[:, :]), reads=[pkey], writes=[("st_kn", h)]))
        P.dma("sp", kn_v[:, :, ts], st_kn[:, :, :], reads=[("st_kn", h) for h in range(NHC)])
        wap, wkey = B.ring.fetch(wplan["uv"])
        for tt in range(4):
            ps, pkey = B.next_ps()
            for k in range(4):
                P.add("pe", lambda e, ps=ps, k=k, tt=tt: e.matmul(
                    ps[:, :], ckvn[:, k, tt * 128:(tt + 1) * 128], wap[:, k * 512:(k + 1) * 512],
                    start=(k == 0), stop=(k == 3)), reads=[wkey, ("ckvn", k)], writes=[pkey])
            P.add("act", lambda e, ps=ps, tt=tt: e.copy(out=st_v[:, tt, :], in_=ps[:, :]),
                  reads=[pkey], writes=[("st_v", tt)])
        for h in range(NHC):
            P.dma("sp", v_o[h, :, b * 4:(b + 1) * 4, :], st_v[:, :, h * 128:(h + 1) * 128],
                  reads=[("st_v", tt) for tt in range(4)])
    for b in range(NB):
        body(b)
    P.emit()


def prep_mla_core(r, w_dq, w_dkv, w_uq, w_ukv, w_o):
    ws = WStream()
    plan = {}
    plan["dq"] = ws.add_matrix(w_dq)
    plan["dkv"] = ws.add_matrix(w_dkv[:, :512])
    kr = w_dkv[:, 512:576]
    plan["kr"] = ws.add_matrix(kr, mw=64)
    plan["kr_sw"] = ws.add_matrix(np.concatenate([kr[:, 32:], kr[:, :32]], axis=1), mw=64)
    uq = w_uq.reshape(512, HEADS, 192)[:, r * NHC:(r + 1) * NHC, :]
    plan["uq_n"] = ws.add_matrix(np.ascontiguousarray(uq[:, :, :128]).reshape(512, NHC * 128))
    plan["uq_r"] = []
    plan["uq_rs"] = []
    for h in range(NHC):
        rr = uq[:, h, 128:192]
        plan["uq_r"].append(ws.add_matrix(rr, mw=64)[0])
        plan["uq_rs"].append(ws.add_matrix(np.concatenate([rr[:, 32:], rr[:, :32]], axis=1), mw=64)[0])
    ukv = w_ukv.reshape(512, HEADS, 256)[:, r * NHC:(r + 1) * NHC, :]
    plan["uk"] = ws.add_matrix(np.ascontiguousarray(ukv[:, :, :128]).reshape(512, NHC * 128))
    wv = np.ascontiguousarray(ukv[:, :, 128:]).reshape(512, NHC * 128)
    plan["uv"] = ws.add_raw(wv.reshape(4, 128, 512).transpose(1, 0, 2).reshape(128, 2048))
    wso = WStream()
    plano = {"wo": wso.add_matrix(np.ascontiguousarray(w_o[r * NHC * 128:(r + 1) * NHC * 128, :]))}
    return ws.array(), ws.meta, plan, wso.array(), wso.meta, plano


def phase_lru(nc, S, xin, xres, part, red, from_input, wdram, wmeta, wplan, gdram, wgd, vecd):
    NB = S // TB
    ph = Phase(nc, wdram, wmeta, gdram, nslots=4)
    P, B = ph.P, ph.B
    h_sb = ph.h_sb
    HJ = 4
    wg_sb = P.sb("wg_sb", [128, 2, LJ, 2, CH], BF16)
    vec = P.sb("vec_sb", [128, LJ, 10], F32)
    cc = P.sb("cc", [128, 4, LJ], F32)
    yl = P.sb("yl", [128, HJ, TB], F32)
    uxb = P.sb("uxb", [128, HJ, TB + 4], F32)
    u_sb = P.sb("u_sb", [128, HJ, TB], F32)
    u_bf = P.sb("u_bf", [128, HJ, TB], BF16)
    r_sb = P.sb("r_sb", [128, HJ, TB], F32)
    ig_sb = P.sb("ig_sb", [128, HJ, TB], F32)
    a_sb = P.sb("a_sb", [128, HJ, TB], F32)
    m_sb = P.sb("m_sb", [128, HJ, TB], F32)
    hs_sb = P.sb("hs_sb", [128, HJ, TB], F32)
    ost = P.sb("ost", [128, LJ, TB], BF16)
    pst = P.sb("pst", [128, 2, TB], F32)
    state = P.sb("state", [128, LJ], F32)
    halo = P.sb("halo", [128, LJ, 4], F32)
    P.dma("sp", vec[0:CH, :, :], vecd[:, :, :], writes=["vec"])
    P.dma("pool", wg_sb[0:CH, :, :, :, :].rearrange("p a b c d -> p (a b c d)"), wgd[:, :], writes=["wg"])
    e_ = cc[0:CH, 0, :]
    t_ = cc[0:CH, 1, :]
    P.add("act", lambda e: e.activation(out=e_, in_=vec[0:CH, :, 5], func=AF.Exp, scale=-1.0), reads=["vec"],
          writes=["cc"])
    coef = [1.0 / 5, -1.0 / 4, 1.0 / 3, -1.0 / 2, 1.0]
    P.add("dve", lambda e: e.tensor_scalar(out=t_, in0=e_, scalar1=-1.0 / 6, scalar2=coef[0], op0=ALU.mult,
                                           op1=ALU.add), reads=["cc"], writes=["cc"])
    for cf in coef[1:]:
        P.add("dve", lambda e: e.tensor_tensor(out=t_, in0=t_, in1=e_, op=ALU.mult), reads=["cc"], writes=["cc"])
        P.add("dve", lambda e, cf=cf: e.tensor_scalar(out=t_, in0=t_, scalar1=cf, scalar2=None, op0=ALU.add),
              reads=["cc"], writes=["cc"])
    P.add("dve", lambda e: e.tensor_tensor(out=t_, in0=t_, in1=e_, op=ALU.mult), reads=["cc"], writes=["cc"])
    P.add("dve", lambda e: e.tensor_scalar(out=cc[0:CH, 2, :], in0=t_, scalar1=-8.0, scalar2=None, op0=ALU.mult),
          reads=["cc"], writes=["cc"])
    P.add("dve", lambda e: e.tensor_scalar(out=cc[0:CH, 3, :], in0=t_, scalar1=-16.0, scalar2=None, op0=ALU.mult),
          reads=["cc"], writes=["cc"])
    P.add("dve", lambda e: e.memset(state[:, :], 0.0), writes=[("state", j) for j in range(LJ)])
    P.add("dve", lambda e: e.memset(halo[:, :, :], 0.0), writes=[("halo", j) for j in range(LJ)])
    hk = lambda kk: h_sb[:, kk, :]
    hkey = lambda kk: ("h", kk)

    def half(b, hf):
        J0 = hf * HJ
        for j in range(HJ):
            J = J0 + j
            B.group(wplan["wy"][J], hk, hkey, lambda ps, pkey, j=j, J=J: P.add(
                "act", lambda e: e.activation(out=yl[0:CH, j, :], in_=ps[0:CH, :], func=AF.Gelu,
                                              bias=vec[0:CH, J, 0:1]),
                reads=[pkey, "vec"], writes=[("yl", j)]), mw=CH)
        for j in range(HJ):
            J = J0 + j
            B.group(wplan["wx"][J], hk, hkey, lambda ps, pkey, j=j, J=J: P.add(
                "act", lambda e: e.activation(out=uxb[0:CH, j, 3:3 + TB], in_=ps[0:CH, :], func=AF.Identity,
                                              bias=vec[0:CH, J, 1:2]),
                reads=[pkey, "vec"], writes=[("ux", j)]), mw=CH)
        for j in range(HJ):
            J = J0 + j
            P.add("dve", lambda e, j=j, J=J: e.tensor_copy(out=uxb[0:CH, j, 0:3], in_=halo[0:CH, J, 0:3]),
                  reads=[("halo", J), ("ux", j)], writes=[("ux", j)])
            P.add("dve", lambda e, j=j, J=J: e.tensor_scalar(out=u_sb[0:CH, j, :], in0=uxb[0:CH, j, 0:TB],
                                                             scalar1=vec[0:CH, J, 6:7], scalar2=vec[0:CH, J, 2:3],
                                                             op0=ALU.mult, op1=ALU.add),
                  reads=[("ux", j), "vec"], writes=[("u", j)])
            for i in range(1, 4):
                P.add("dve", lambda e, j=j, J=J, i=i: e.scalar_tensor_tensor(
                    out=u_sb[0:CH, j, :], in0=uxb[0:CH, j, i:i + TB], scalar=vec[0:CH, J, 6 + i:7 + i],
                    in1=u_sb[0:CH, j, :], op0=ALU.mult, op1=ALU.add),
                    reads=[("ux", j), "vec", ("u", j)], writes=[("u", j)])
            P.add("dve", lambda e, j=j, J=J: e.tensor_copy(out=halo[0:CH, J, 0:3], in_=uxb[0:CH, j, TB:TB + 3]),
                  reads=[("ux", j)], writes=[("halo", J)])
            P.add("act", lambda e, j=j: e.copy(out=u_bf[0:CH, j, :], in_=u_sb[0:CH, j, :]),
                  reads=[("u", j)], writes=[("ubf", j)])
        for gi, dst, bidx in ((0, r_sb, 3), (1, ig_sb, 4)):
            for j in range(HJ):
                J = J0 + j
                ps, pkey = B.next_ps()
                for i in range(2):
                    P.add("pe", lambda e, ps=ps, gi=gi, j=j, J=J, i=i: e.matmul(
                        ps[0:CH, :], wg_sb[0:CH, gi, J, i, :], u_bf[0:CH, 2 * (j // 2) + i, :], start=(i == 0),
                        stop=(i == 1)), reads=["wg", ("ubf", 2 * (j // 2) + i)], writes=[pkey])
                P.add("act", lambda e, ps=ps, j=j, J=J, dst=dst, bidx=bidx: e.activation(
                    out=dst[0:CH, j, :], in_=ps[0:CH, :], func=AF.Sigmoid, bias=vec[0:CH, J, bidx:bidx + 1]),
                    reads=[pkey, "vec"], writes=[("gate", gi, j)])
        for j in range(HJ):
            J = J0 + j
            P.add("act", lambda e, j=j, J=J: e.activation(out=a_sb[0:CH, j, :], in_=r_sb[0:CH, j, :], func=AF.Exp,
                                                          scale=cc[0:CH, 2, J:J + 1]),
                  reads=[("gate", 0, j), "cc"], writes=[("a", j)])
            P.add("act", lambda e, j=j, J=J: e.activation(out=m_sb[0:CH, j, :], in_=r_sb[0:CH, j, :], func=AF.Exp,
                                                          scale=cc[0:CH, 3, J:J + 1]),
                  reads=[("gate", 0, j), "cc"], writes=[("m", j)])
        for j in range(HJ):
            P.add("dve", lambda e, j=j: e.tensor_scalar(out=m_sb[0:CH, j, :], in0=m_sb[0:CH, j, :], scalar1=-1.0,
                                                        scalar2=1.0, op0=ALU.mult, op1=ALU.add),
                  reads=[("m", j)], writes=[("m", j)])
        for j in range(HJ):
            P.add("act", lambda e, j=j: e.activation(out=m_sb[0:CH, j, :], in_=m_sb[0:CH, j, :], func=AF.Sqrt),
                  reads=[("m", j)], writes=[("m", j)])
        for j in range(HJ):
            J = J0 + j
            P.add("dve", lambda e, j=j: e.tensor_tensor(out=ig_sb[0:CH, j, :], in0=ig_sb[0:CH, j, :],
                                                        in1=u_sb[0:CH, j, :], op=ALU.mult),
                  reads=[("gate", 1, j), ("u", j)], writes=[("gate", 1, j)])
            P.add("dve", lambda e, j=j: e.tensor_tensor(out=ig_sb[0:CH, j, :], in0=ig_sb[0:CH, j, :],
                                                        in1=m_sb[0:CH, j, :], op=ALU.mult),
                  reads=[("gate", 1, j), ("m", j)], writes=[("gate", 1, j)])
            P.add("dve", lambda e, j=j, J=J: e.tensor_tensor_scan(
                out=hs_sb[0:CH, j, :], data0=a_sb[0:CH, j, :], data1=ig_sb[0:CH, j, :],
                initial=state[0:CH, J:J + 1], op0=ALU.mult, op1=ALU.add),
                reads=[("a", j), ("gate", 1, j), ("state", J)], writes=[("hs", j)])
            P.add("dve", lambda e, j=j, J=J: e.tensor_copy(out=state[0:CH, J:J + 1], in_=hs_sb[0:CH, j, TB - 1:TB]),
                  reads=[("hs", j)], writes=[("state", J)])
            P.add("dve", lambda e, j=j, J=J: e.tensor_tensor(out=ost[0:CH, J, :], in0=hs_sb[0:CH, j, :],
                                                             in1=yl[0:CH, j, :], op=ALU.mult),
                  reads=[("hs", j), ("yl", j)], writes=[("ost", J)])

    def body(b):
        ph.prologue(b, xsrc_view(xin, xres, b, from_input), red, False, _blk(xres, b))
        half(b, 0)
        half(b, 1)
        for m in range(DC):
            ps, pkey = B.next_ps()
            wap, wkey = B.ring.fetch(wplan["wout"][m])
            for J in range(LJ):
                P.add("pe", lambda e, ps=ps, J=J, wap=wap: e.matmul(
                    ps[:, :], wap[0:CH, J * 128:(J + 1) * 128], ost[0:CH, J, :], start=(J == 0), stop=(J == LJ - 1)),
                    reads=[wkey, ("ost", J)], writes=[pkey])
            ph.store_partial(ps, pkey, part, b, m, pst, m)
    for b in range(NB):
        body(b)
    P.emit()


def prep_lru_core(r, w_y, b_y, w_x, b_x, conv_w, conv_b, w_ga, b_ga, w_gi, b_gi, lam, w_out):
    c0 = r * LJ * CH
    c1 = c0 + LJ * CH
    ws = WStream()
    plan = {"wy": ws.add_matrix(np.ascontiguousarray(w_y[:, c0:c1]), mw=CH),
            "wx": ws.add_matrix(np.ascontiguousarray(w_x[:, c0:c1]), mw=CH), "wout": []}
    wo = w_out[c0:c1, :].reshape(LJ, CH, DC, 128)
    for m in range(DC):
        t = np.zeros((128, LJ, 128), np.float32)
        t[:CH] = wo[:, :, m, :].transpose(1, 0, 2)
        plan["wout"].append(ws.add_raw(t.reshape(128, LJ * 128)))
    wg = np.zeros((CH, 2, LJ, 2, CH), np.float32)
    for gi, w in enumerate((w_ga, w_gi)):
        for J in range(LJ):
            n = (LJ // 2) * r + J // 2
            for i in range(2):
                wg[:, gi, J, i, :] = w[n, i * CH:(i + 1) * CH, (J % 2) * CH:(J % 2 + 1) * CH]
    vec = np.zeros((CH, LJ, 10), np.float32)
    for idx, vv in enumerate((b_y, b_x, conv_b, b_ga, b_gi, lam)):
        vec[:, :, idx] = vv[c0:c1].reshape(LJ, CH).T
    for i in range(4):
        vec[:, :, 6 + i] = conv_w[i, c0:c1].reshape(LJ, CH).T
    return ws.array(), ws.meta, plan, np.ascontiguousarray(wg.reshape(CH, -1)), vec


def prep_mlp_core(r, w1, w2):
    ws = WStream()
    n = DFF // TPG
    plan = {"w1": ws.add_matrix(np.ascontiguousarray(w1[:, r * n:(r + 1) * n])),
            "w2": ws.add_matrix(np.ascontiguousarray(w2[r * n:(r + 1) * n, :]))}
    return ws.array(), ws.meta, plan


def g3(gpost, gpre, bias):
    z = np.zeros((128, DC), np.float32)
    return np.ascontiguousarray(np.stack([gvec(gpost) if gpost is not None else z,
                                          gvec(gpre) if gpre is not None else z,
                                          gvec(bias) if bias is not None else z], axis=1))


def build_fused(S, depth, metas):
    nc = bass.Bass("TRN2", target_bir_lowering=False)
    NB = S // TB

    def din(name, shape, dt=F32):
        return nc.dram_tensor(name, list(shape), dt, kind="ExternalInput").ap()

    def wsz(meta):
        return meta[-1][0] + meta[-1][1]
    xin = din("xT", [D, S])
    pos = din("pos", [64, S], I32)
    rcd = din("rc", [64, 2])
    msk = din("masks", [128, 4, 512], BF16)
    outT = nc.dram_tensor("outT", [D, S], F32, kind="ExternalOutput").ap()
    xres = nc.dram_tensor("xres", [NB, D, TB], F32)
    part = nc.dram_tensor("part", [NB, D, TB], F32)
    red = nc.dram_tensor("red", [NB, D, TB], F32)
    qn_s = nc.dram_tensor("qn_s", [NHC, 128, S], BF16).ap()
    qr_s = nc.dram_tensor("qr_s", [NHC, 64, S], BF16).ap()
    kn_s = nc.dram_tensor("kn_s", [NHC, 128, S], BF16).ap()
    kr_s = nc.dram_tensor("kr_s", [64, S], BF16).ap()
    v_s = nc.dram_tensor("v_s", [NHC, 128, S // 128, 128], BF16).ap()
    o_s = nc.dram_tensor("o_s", [NHC, 128, S], BF16).ap()
    first = True
    from_input = True
    arid = [0]

    def allreduce():
        arid[0] += 1
        phase_allreduce(nc, part, red, NB, arid[0])
    prev_bias = False
    for i in range(depth):
        j = i // 2
        if i % 2 == 0:
            wm, wp, wmo, wpo = metas[f"mla{j}"]
            phase_mla_proj(nc, S, xin, xres, red, from_input, first, pos, rcd, din(f"w_mla{j}", [128, wsz(wm)]), wm,
                           wp, din(f"g_mix{i}", [128, 3, DC]), din(f"gq{j}", [128, 8]), qn_s, qr_s, kn_s, kr_s, v_s)
            emit_attn(nc, S, NHC, qn_s, qr_s, kn_s, kr_s, v_s, msk, o_s)
            phase_wo(nc, S, o_s, part, din(f"w_o{j}", [128, wsz(wmo)]), wmo, wpo)
            prev_bias = False
        else:
            wm, wp = metas[f"lru{j}"]
            phase_lru(nc, S, xin, xres, part, red, from_input, din(f"w_lru{j}", [128, wsz(wm)]), wm, wp,
                      din(f"g_mix{i}", [128, 3, DC]), din(f"wg{j}", [CH, 2 * LJ * 2 * CH]),
                      din(f"vec{j}", [CH, LJ, 10]))
            prev_bias = True
        allreduce()
        wm, wp = metas[f"mlp{i}"]
        phase_mlp(nc, S, xin, xres, part, red, from_input, prev_bias, din(f"w_mlp{i}", [128, wsz(wm)]), wm, wp,
                  din(f"g_mlp{i}", [128, 3, DC]))
        first = False
        from_input = False
        allreduce()
    phase_final(nc, S, xres, red, outT, din("g_fin", [128, 3, DC]))
    return nc


def kernel(x, positions, mix_pre_g, mix_post_g, mlp_pre_g, mlp_post_g,
           mla_w_dq, mla_g_q, mla_w_uq, mla_w_dkv, mla_g_kv, mla_w_ukv, mla_w_o,
           lru_w_y, lru_b_y, lru_w_x, lru_b_x, lru_conv_w, lru_conv_b,
           lru_w_ga, lru_b_ga, lru_w_gi, lru_b_gi, lru_lam, lru_w_out, lru_b_out,
           mlp_w1, mlp_w2):
    f32 = lambda a: np.asarray(a, np.float32)
    x = f32(x)
    positions = np.asarray(positions, np.int32)
    NBATCH, S, _ = x.shape
    assert NBATCH * TPG == NCORES
    depth = mix_pre_g.shape[0]
    rc = rope_consts()
    masks = attn_masks()
    maps = [dict() for _ in range(NCORES)]
    metas = {}
    xTb = [np.ascontiguousarray(x[b].T) for b in range(NBATCH)]
    posb = [np.ascontiguousarray(np.broadcast_to(positions[b][None, :], (64, S))) for b in range(NBATCH)]
    for c in range(NCORES):
        b, r = divmod(c, TPG)
        m = maps[c]
        m["xT"] = xTb[b]
        m["pos"] = posb[b]
        m["rc"] = rc
        m["masks"] = masks
        for i in range(depth):
            j = i // 2
            if i % 2 == 0:
                wa, wm, wp, wao, wmo, wpo = prep_mla_core(r, f32(mla_w_dq[j]), f32(mla_w_dkv[j]), f32(mla_w_uq[j]),
                                                          f32(mla_w_ukv[j]), f32(mla_w_o[j]))
                metas[f"mla{j}"] = (wm, wp, wmo, wpo)
                m[f"w_mla{j}"] = wa
                m[f"w_o{j}"] = wao
                m[f"gq{j}"] = np.ascontiguousarray(np.concatenate([gvec(mla_g_q[j]), gvec(mla_g_kv[j])], axis=1))
                m[f"g_mix{i}"] = g3(mlp_post_g[i - 1] if i > 0 else None, mix_pre_g[i], None)
                m[f"g_mlp{i}"] = g3(mix_post_g[i], mlp_pre_g[i], None)
            else:
                wa, wm, wp, wg, vec = prep_lru_core(
                    r, f32(lru_w_y[j]), f32(lru_b_y[j]), f32(lru_w_x[j]), f32(lru_b_x[j]), f32(lru_conv_w[j]),
                    f32(lru_conv_b[j]), f32(lru_w_ga[j]), f32(lru_b_ga[j]), f32(lru_w_gi[j]), f32(lru_b_gi[j]),
                    f32(lru_lam[j]), f32(lru_w_out[j]))
                metas[f"lru{j}"] = (wm, wp)
                m[f"w_lru{j}"] = wa
                m[f"wg{j}"] = wg
                m[f"vec{j}"] = vec
                m[f"g_mix{i}"] = g3(mlp_post_g[i - 1], mix_pre_g[i], None)
                m[f"g_mlp{i}"] = g3(mix_post_g[i], mlp_pre_g[i], lru_b_out[j])
            wa, wm, wp = prep_mlp_core(r, f32(mlp_w1[i]), f32(mlp_w2[i]))
            metas[f"mlp{i}"] = (wm, wp)
            m[f"w_mlp{i}"] = wa
        m["g_fin"] = g3(mlp_post_g[depth - 1], None, None)
    nc = build_fused(S, depth, metas)
    res = run_bass_kernel_spmd(nc, maps, core_ids=list(range(NCORES))).results
    out = np.stack([np.asarray(res[b * TPG]["outT"]).T for b in range(NBATCH)], axis=0)
    return np.ascontiguousarray(out.astype(np.float32))
```

```python
import contextlib
import numpy as np
import ml_dtypes
import concourse.bass as bass
import concourse.mybir as mybir
from concourse.bass_utils import run_bass_kernel_spmd

F32 = mybir.dt.float32
BF16 = mybir.dt.bfloat16
I32 = mybir.dt.int32
ALU = mybir.AluOpType
AF = mybir.ActivationFunctionType
NPBF16 = ml_dtypes.bfloat16

D = 2048
DC = D // 128
DFF = 8192
FC = DFF // 128
EPS = 1e-6
TB = 512
NCORES = 8
HEADS = 16
DRNN = 2688
RC = DRNN // 128


class Op:
    __slots__ = ("eng", "fn", "deps", "idx", "signal", "cnt", "is_dma", "dsem", "dval", "gid", "ext", "is_cc", "ccval")


class Prog:
    ENGS = ["pe", "act", "dve", "pool", "sp"]
    EPOCH = 12000
    NDMA_SEM = 8

    _uid = [0]
    G = {}

    def __init__(self, nc):
        self.nc = nc
        Prog._uid[0] += 1
        self.pfx = f"p{Prog._uid[0]}_"
        self.ops = {e: [] for e in self.ENGS}
        self.last_w = {}
        self.readers = {}
        self.stack = contextlib.ExitStack()
        self.ngid = 0

    def sb(self, name, shape, dt):
        return self.stack.enter_context(self.nc.sbuf_tensor(self.pfx + name, list(shape), dt))

    def ps(self, name, shape=(128, 512), dt=F32):
        return self.stack.enter_context(self.nc.psum_tensor(self.pfx + name, list(shape), dt))

    def add(self, eng, fn, reads=(), writes=(), dma=False, ext=None, cc=False):
        op = Op()
        op.ext = ext
        op.is_cc = cc
        op.ccval = None
        op.eng = eng
        op.fn = fn
        op.is_dma = dma
        op.signal = False
        op.gid = self.ngid
        self.ngid += 1
        deps = {}
        for k in reads:
            w = self.last_w.get(k)
            if w is not None:
                deps[w.gid] = w
        for k in writes:
            w = self.last_w.get(k)
            if w is not None:
                deps[w.gid] = w
            for r in self.readers.get(k, ()):
                deps[r.gid] = r
        for k in reads:
            self.readers.setdefault(k, []).append(op)
        for k in writes:
            self.last_w[k] = op
            self.readers[k] = []
        op.deps = list(deps.values())
        op.idx = len(self.ops[eng])
        self.ops[eng].append(op)
        return op

    def dma(self, eng, out, in_, reads=(), writes=(), ext=None):
        return self.add(eng, lambda e: e.dma_start(out=out, in_=in_), reads, writes, dma=True, ext=ext)

    def collective(self, in_ap, out_ap, reads=()):
        return self.add("pool", lambda e: e.collective_compute("AllReduce", ALU.add, replica_groups=GROUPS,
                                                               ins=[in_ap], outs=[out_ap]), reads, (), cc=True)

    def emit(self):
        nc = self.nc
        for e in self.ENGS:
            for op in self.ops[e]:
                best = {}
                dd = []
                for d in op.deps:
                    if d.is_dma:
                        dd.append(d)
                        continue
                    if d.eng == op.eng and op.eng == "pe" and not op.is_dma:
                        continue
                    b = best.get(d.eng)
                    if b is None or d.idx > b.idx:
                        best[d.eng] = d
                for d in best.values():
                    d.signal = True
                    dd.append(d)
                op.deps = dd
        G = Prog.G
        if G.get("nc") is not nc:
            G.clear()
            G["nc"] = nc
            G["sems"] = {e: [] for e in self.ENGS}
            G["cnt"] = {e: 0 for e in self.ENGS}
            G["dsets"] = {e: [] for e in self.ENGS}
            G["dvals"] = {e: None for e in self.ENGS}
            G["dj"] = {e: 0 for e in self.ENGS}
            G["nalloc"] = 0

        def new_sem(name):
            G["nalloc"] += 1
            return nc.alloc_semaphore(name=f"g{G['nalloc']}_{name}")
        base = dict(G["cnt"])
        if "cc" not in G:
            G["cc"] = nc.alloc_semaphore(name="cc_global")
            G["ccv"] = 0
        for op in self.ops["pool"]:
            if op.is_cc:
                G["ccv"] += 1
                op.ccval = G["ccv"]
        ccsem = G["cc"]
        for e in self.ENGS:
            c = G["cnt"][e]
            for op in self.ops[e]:
                if op.signal and not op.is_dma:
                    c += 1
                    op.cnt = c
            G["cnt"][e] = c
            need = (c + self.EPOCH - 1) // self.EPOCH
            while len(G["sems"][e]) < max(1, need):
                G["sems"][e].append(new_sem(f"s_{e}"))
        sems = G["sems"]
        DLIM = 30000
        for e in self.ENGS:
            for op in self.ops[e]:
                if op.is_dma:
                    if G["dvals"][e] is None or max(G["dvals"][e]) + 16 > DLIM:
                        G["dsets"][e].append([new_sem(f"d_{e}") for _ in range(self.NDMA_SEM)])
                        G["dvals"][e] = [0] * self.NDMA_SEM
                    sidx = G["dj"][e] % self.NDMA_SEM
                    G["dj"][e] += 1
                    G["dvals"][e][sidx] += 16
                    op.dsem = G["dsets"][e][-1][sidx]
                    op.dval = G["dvals"][e][sidx]
        EP = self.EPOCH
        self.stats = {}

        def run(eng_name, eng):
            known = dict(base)
            known_dma = set()
            nw = 0
            for op in self.ops[eng_name]:
                for d in op.deps:
                    if d.is_dma:
                        if d.gid in known_dma:
                            continue
                        eng.wait_ge(d.dsem, d.dval)
                        known_dma.add(d.gid)
                        nw += 1
                    else:
                        if d.cnt <= known[d.eng]:
                            continue
                        ep = (d.cnt - 1) // EP
                        eng.wait_ge(sems[d.eng][ep], d.cnt - ep * EP)
                        known[d.eng] = d.cnt
                        nw += 1
                if op.ext is not None:
                    eng.wait_ge(ccsem, op.ext)
                if op.is_cc:
                    if op.ccval > 1:
                        eng.wait_ge(ccsem, op.ccval - 1)
                    op.fn(eng).then_inc(ccsem)
                elif op.is_dma:
                    if op.dval > 16:
                        eng.wait_ge(op.dsem, op.dval - 16)
                    op.fn(eng).then_inc(op.dsem, 16)
                else:
                    ins = op.fn(eng)
                    if op.signal:
                        ep = (op.cnt - 1) // EP
                        ins.then_inc(sems[eng_name][ep], 1)
            last = {}
            for op in self.ops[eng_name]:
                if op.is_dma:
                    last[id(op.dsem)] = op
            for op in last.values():
                eng.wait_ge(op.dsem, op.dval)
            self.stats[eng_name] = (len(self.ops[eng_name]), nw)

        with nc.Block() as block:
            @block.tensor
            def _(pe):
                run("pe", pe)

            @block.scalar
            def _(act):
                run("act", act)

            @block.vector
            def _(dve):
                run("dve", dve)

            @block.gpsimd
            def _(pool):
                run("pool", pool)

            @block.sync
            def _(sp):
                run("sp", sp)
        self.stack.close()


class WStream:
    def __init__(self):
        self.tiles = []
        self.meta = []
        self.off = 0

    def add_matrix(self, w, mw=128):
        K, M = w.shape
        kc = K // 128
        kmax = 2048 // mw
        out = []
        wt = w.reshape(kc, 128, M // mw, mw)
        for m in range(M // mw):
            lst = []
            for k0 in range(0, kc, kmax):
                nk = min(kmax, kc - k0)
                t = np.ascontiguousarray(wt[k0:k0 + nk, :, m, :].transpose(1, 0, 2))
                lst.append((len(self.tiles), nk))
                self.tiles.append(t.reshape(128, nk * mw))
                self.meta.append((self.off, nk * mw))
                self.off += nk * mw
            out.append(lst)
        return out

    def add_raw(self, t):
        n = t.shape[1]
        tid = len(self.tiles)
        self.tiles.append(np.ascontiguousarray(t))
        self.meta.append((self.off, n))
        self.off += n
        return tid

    def array(self):
        return np.ascontiguousarray(np.concatenate(self.tiles, axis=1).astype(np.float32))


class WRing:
    def __init__(self, P, wdram, meta, nslots=6):
        self.P = P
        self.wdram = wdram
        self.meta = meta
        self.nslots = nslots
        self.buf = P.sb("wring", [128, nslots, 16 * 128], BF16)
        self.next = 0
        self.loaded = {}

    def fetch(self, tid):
        slot = self.next % self.nslots
        self.next += 1
        off, ne = self.meta[tid]
        key = ("w", slot)
        self.P.dma("pool", self.buf[:, slot, 0:ne], self.wdram[:, off:off + ne], writes=[key])
        pend = getattr(self.P, "pending_cc", None)
        if pend:
            for it in pend:
                it[0] -= 1
            while pend and pend[0][0] <= 0:
                pend.pop(0)[1]()
        return self.buf[:, slot, :], key


def gvec(g):
    g = np.asarray(g, np.float32)
    return np.ascontiguousarray(g.reshape(-1, 128).T)


class Ctx:
    pass


def emit_rmsnorm_stats(P, C, src, src_keys, sq, sq_keys, nch, ps_ss, ps_key, rstd, rstd_key, tmp, tmp_key,
                       dim, ntok=TB, sq_eng="act"):
    for c in range(nch):
        if sq_eng == "act":
            P.add("act", lambda e, c=c: e.activation(out=sq[:, c, :ntok], in_=src[:, c, :ntok], func=AF.Square),
                  reads=[src_keys[c]], writes=[sq_keys[c]])
        else:
            P.add(sq_eng, lambda e, c=c: e.tensor_tensor(out=sq[:, c, :ntok], in0=src[:, c, :ntok],
                                                         in1=src[:, c, :ntok], op=ALU.mult),
                  reads=[src_keys[c]], writes=[sq_keys[c]])
    for c in range(nch):
        P.add("pe", lambda e, c=c: e.matmul(ps_ss[:, :ntok], C.ones[:, :], sq[:, c, :ntok], start=(c == 0),
                                            stop=(c == nch - 1)),
              reads=[sq_keys[c], "ones"], writes=[ps_key])
    P.add("act", lambda e: e.activation(out=tmp[:, :ntok], in_=ps_ss[:, :ntok], func=AF.Sqrt, scale=1.0 / dim,
                                        bias=C.eps[:, 0:1]),
          reads=[ps_key, "eps"], writes=[tmp_key])
    P.add("dve", lambda e: e.reciprocal(out=rstd[:, :ntok], in_=tmp[:, :ntok]),
          reads=[tmp_key], writes=[rstd_key])


def build_post_mlp(T, KIN, has_bias, wmeta, wplan):
    nc = bass.Bass("TRN2", target_bir_lowering=False)
    KC = KIN // 128
    NB = T // TB
    xT = nc.dram_tensor("xT", [D, T], F32, kind="ExternalInput").ap()
    mT = nc.dram_tensor("mT", [KIN, T], BF16, kind="ExternalInput").ap()
    wdram = nc.dram_tensor("wstream", [128, wmeta[-1][0] + wmeta[-1][1]], F32, kind="ExternalInput").ap()
    gpost = nc.dram_tensor("gpost", [128, DC], F32, kind="ExternalInput").ap()
    gpre = nc.dram_tensor("gpre", [128, DC], F32, kind="ExternalInput").ap()
    gpost2 = nc.dram_tensor("gpost2", [128, DC], F32, kind="ExternalInput").ap()
    bout = nc.dram_tensor("bout", [128, DC], F32, kind="ExternalInput").ap()
    oT = nc.dram_tensor("oT", [D, T], F32, kind="ExternalOutput").ap()
    xTv = xT.rearrange("(c p) t -> p c t", p=128)
    oTv = oT.rearrange("(c p) t -> p c t", p=128)
    mTv = mT.rearrange("(c p) t -> p c t", p=128)

    P = Prog(nc)
    C = Ctx()
    C.ones = P.sb("ones", [128, 128], BF16)
    x_sb = P.sb("x_sb", [128, DC, TB], F32)
    y_sb = P.sb("y_sb", [128, DC, TB], F32)
    h_sb = P.sb("h_sb", [128, DC, TB], BF16)
    big = P.sb("big", [128, FC, TB], BF16)
    g_sb = P.sb("g_sb", [128, 4, DC], F32)
    rstd = P.sb("rstd", [128, TB], F32)
    tmp = P.sb("tmp", [128, TB], F32)
    tmp2 = P.sb("tmp2", [128, 2, TB], F32)
    NPS = 6
    pss = [P.ps(f"ps{i}") for i in range(NPS)]
    ps_ss = P.ps("ps_ss")
    ring = WRing(P, wdram, wmeta, nslots=6)

    C.eps = P.sb("eps", [128, 1], F32)
    P.add("dve", lambda e: e.memset(C.ones[:, :], 1.0), writes=["ones"])
    P.add("dve", lambda e: e.memset(C.eps[:, :], EPS), writes=["eps"])
    for i, g in enumerate([gpost, gpre, gpost2, bout]):
        P.dma("sp", g_sb[:, i, :], g[:, :], writes=[("g", i)])

    psi = [0]

    def next_ps():
        i = psi[0] % NPS
        psi[0] += 1
        return pss[i], ("ps", i)

    def matmul_group(wtiles, rhs_of_k, rhs_keys_of_k, evac):
        ps, pkey = next_ps()
        kbase = 0
        total = sum(nk for _, nk in wtiles)
        for tid, nk in wtiles:
            wap, wkey = ring.fetch(tid)
            for k in range(nk):
                kk = kbase + k
                P.add("pe", lambda e, wap=wap, k=k, kk=kk: e.matmul(
                    ps[:, :], wap[:, k * 128:(k + 1) * 128], rhs_of_k(kk), start=(kk == 0), stop=(kk == total - 1)),
                    reads=[wkey, rhs_keys_of_k(kk)], writes=[pkey])
            kbase += nk
        evac(ps, pkey)

    def post_norm_residual(gi):
        emit_rmsnorm_stats(P, C, y_sb, [("y", c) for c in range(DC)], h_sb, [("h", c) for c in range(DC)], DC,
                           ps_ss, "ps_ss", rstd, "rstd", tmp, "tmp", D)
        for c in range(DC):
            t2 = tmp2[:, c % 2, :]
            P.add("dve", lambda e, c=c, t2=t2: e.scalar_tensor_tensor(
                out=t2, in0=y_sb[:, c, :], scalar=g_sb[:, gi, c:c + 1], in1=rstd[:, :], op0=ALU.mult, op1=ALU.mult),
                reads=[("y", c), "rstd", ("g", gi)], writes=[("tmp2", c % 2)])
            P.add("dve", lambda e, c=c, t2=t2: e.tensor_tensor(out=x_sb[:, c, :], in0=x_sb[:, c, :], in1=t2,
                                                                 op=ALU.add),
                  reads=[("tmp2", c % 2), ("x", c)], writes=[("x", c)])

    for b in range(NB):
        ts = slice(b * TB, (b + 1) * TB)
        for c0 in range(0, DC, 4):
            P.dma("sp", x_sb[:, c0:c0 + 4, :], xTv[:, c0:c0 + 4, ts], writes=[("x", c) for c in range(c0, c0 + 4)])
        for c0 in range(0, KC, 8):
            c1 = min(KC, c0 + 8)
            P.dma("sp", big[:, c0:c1, :], mTv[:, c0:c1, ts], writes=[("big", c) for c in range(c0, c1)])
        for m in range(DC):
            def evac(ps, pkey, m=m):
                if has_bias:
                    P.add("act", lambda e: e.activation(out=y_sb[:, m, :], in_=ps[:, :], func=AF.Identity,
                                                        bias=g_sb[:, 3, m:m + 1]),
                          reads=[pkey, ("g", 3)], writes=[("y", m)])
                else:
                    P.add("act", lambda e: e.copy(out=y_sb[:, m, :], in_=ps[:, :]), reads=[pkey], writes=[("y", m)])
            matmul_group(wplan["wo"][m], lambda kk: big[:, kk, :], lambda kk: ("big", kk), evac)
        post_norm_residual(0)
        emit_rmsnorm_stats(P, C, x_sb, [("x", c) for c in range(DC)], h_sb, [("h", c) for c in range(DC)], DC,
                           ps_ss, "ps_ss", rstd, "rstd", tmp, "tmp", D)
        for c in range(DC):
            P.add("dve", lambda e, c=c: e.scalar_tensor_tensor(
                out=h_sb[:, c, :], in0=x_sb[:, c, :], scalar=g_sb[:, 1, c:c + 1], in1=rstd[:, :], op0=ALU.mult,
                op1=ALU.mult), reads=[("x", c), "rstd", ("g", 1)], writes=[("h", c)])
        for m in range(FC):
            def evac(ps, pkey, m=m):
                t2 = tmp2[:, m % 2, :]
                P.add("act", lambda e: e.activation(out=t2, in_=ps[:, :], func=AF.Relu), reads=[pkey],
                      writes=[("tmp2", m % 2)])
                P.add("dve", lambda e: e.tensor_tensor(out=big[:, m, :], in0=t2, in1=t2, op=ALU.mult),
                      reads=[("tmp2", m % 2)], writes=[("big", m)])
            matmul_group(wplan["w1"][m], lambda kk: h_sb[:, kk, :], lambda kk: ("h", kk), evac)
        for m in range(DC):
            def evac(ps, pkey, m=m):
                P.add("act", lambda e: e.copy(out=y_sb[:, m, :], in_=ps[:, :]), reads=[pkey], writes=[("y", m)])
            matmul_group(wplan["w2"][m], lambda kk: big[:, kk, :], lambda kk: ("big", kk), evac)
        post_norm_residual(2)
        for c0 in range(0, DC, 4):
            P.dma("sp", oTv[:, c0:c0 + 4, ts], x_sb[:, c0:c0 + 4, :], reads=[("x", c) for c in range(c0, c0 + 4)])
    P.emit()
    return nc


def prep_post_mlp_weights(w_o, w1, w2):
    ws = WStream()
    plan = {"wo": ws.add_matrix(w_o), "w1": ws.add_matrix(w1), "w2": ws.add_matrix(w2)}
    return ws.array(), ws.meta, plan


class Bld:
    def __init__(self, nc, wdram, wmeta, nps=6, nslots=6, with_ss=True):
        self.P = Prog(nc)
        P = self.P
        self.C = Ctx()
        self.C.ones = P.sb("ones", [128, 128], BF16)
        self.C.eps = P.sb("eps", [128, 1], F32)
        P.add("dve", lambda e: e.memset(self.C.ones[:, :], 1.0), writes=["ones"])
        P.add("dve", lambda e: e.memset(self.C.eps[:, :], EPS), writes=["eps"])
        self.pss = [P.ps(f"ps{i}") for i in range(nps)]
        self.nps = nps
        self.psi = 0
        if with_ss:
            self.ps_ss = P.ps("ps_ss")
            self.rstd = P.sb("rstd", [128, TB], F32)
            self.tmp = P.sb("tmp", [128, TB], F32)
        self.ring = WRing(P, wdram, wmeta, nslots=nslots) if wdram is not None else None

    def next_ps(self):
        i = self.psi % self.nps
        self.psi += 1
        return self.pss[i], ("ps", i)

    def group(self, wtiles, rhs_of_k, key_of_k, evac, mw=128, ncol=TB):
        P = self.P
        ps, pkey = self.next_ps()
        total = sum(nk for _, nk in wtiles)
        kbase = 0
        for tid, nk in wtiles:
            wap, wkey = self.ring.fetch(tid)
            for k in range(nk):
                kk = kbase + k
                P.add("pe", lambda e, wap=wap, k=k, kk=kk: e.matmul(
                    ps[0:mw, :ncol], wap[:, k * mw:(k + 1) * mw], rhs_of_k(kk), start=(kk == 0),
                    stop=(kk == total - 1)), reads=[wkey, key_of_k(kk)], writes=[pkey])
            kbase += nk
        evac(ps, pkey)

    def norm_stats(self, src, src_keys, sq, sq_keys, nch, dim, ntok=TB):
        emit_rmsnorm_stats(self.P, self.C, src, src_keys, sq, sq_keys, nch, self.ps_ss, "ps_ss", self.rstd, "rstd",
                           self.tmp, "tmp", dim, ntok)

    def norm_apply(self, dst, dst_keys, src, src_keys, g_ap_of_c, g_key, nch, eng="dve"):
        for c in range(nch):
            self.P.add(eng, lambda e, c=c: e.scalar_tensor_tensor(
                out=dst[:, c, :], in0=src[:, c, :], scalar=g_ap_of_c(c), in1=self.rstd[:, :], op0=ALU.mult,
                op1=ALU.mult), reads=[src_keys[c], "rstd", g_key], writes=[dst_keys[c]])


TWO_PI = 6.283185307179586
CW1 = 6.28125
CW2 = TWO_PI - CW1
PI_LO = 3.1415925


def emit_rope_tables(B, pos_i, pos_key, rc, n, cos2, sin2, tkey):
    P = B.P
    W = B.rope_ws
    ang, kfl, r, m = W[:, 0, :n], W[:, 1, :n], W[:, 2, :n], W[:, 3, :n]
    ki = B.rope_wi[:, :n]
    K = "ropews"
    P.add("dve", lambda e: e.tensor_copy(out=kfl, in_=pos_i), reads=[pos_key], writes=[K])
    P.add("dve", lambda e: e.tensor_scalar(out=ang, in0=kfl, scalar1=rc[:, 0:1], scalar2=None, op0=ALU.mult),
          reads=[K, "rc"], writes=[K])
    P.add("dve", lambda e: e.tensor_scalar(out=ki, in0=ang, scalar1=1.0 / TWO_PI, scalar2=None, op0=ALU.mult),
          reads=[K], writes=[K])
    P.add("dve", lambda e: e.tensor_copy(out=kfl, in_=ki), reads=[K], writes=[K])
    P.add("dve", lambda e: e.scalar_tensor_tensor(out=r, in0=kfl, scalar=-CW1, in1=ang, op0=ALU.mult, op1=ALU.add),
          reads=[K], writes=[K])
    P.add("dve", lambda e: e.scalar_tensor_tensor(out=r, in0=kfl, scalar=-CW2, in1=r, op0=ALU.mult, op1=ALU.add),
          reads=[K], writes=[K])
    P.add("dve", lambda e: e.tensor_scalar(out=m, in0=r, scalar1=np.pi, scalar2=-TWO_PI, op0=ALU.is_gt,
                                           op1=ALU.mult), reads=[K], writes=[K])
    P.add("dve", lambda e: e.tensor_tensor(out=r, in0=r, in1=m, op=ALU.add), reads=[K], writes=[K])
    P.add("dve", lambda e: e.tensor_scalar(out=r, in0=r, scalar1=PI_LO, scalar2=-PI_LO, op0=ALU.min, op1=ALU.max),
          reads=[K], writes=[K])
    P.add("act", lambda e: e.activation(out=m, in_=r, func=AF.Sin), reads=[K], writes=[K])
    P.add("act", lambda e: e.activation(out=ang, in_=r, func=AF.Sin, scale=0.5), reads=[K], writes=[K])
    P.add("dve", lambda e: e.tensor_scalar(out=sin2, in0=m, scalar1=rc[:, 1:2], scalar2=None, op0=ALU.mult),
          reads=[K, "rc"], writes=[tkey])
    P.add("dve", lambda e: e.tensor_tensor(out=ang, in0=ang, in1=ang, op=ALU.mult), reads=[K], writes=[K])
    P.add("dve", lambda e: e.tensor_scalar(out=cos2, in0=ang, scalar1=-2.0, scalar2=1.0, op0=ALU.mult, op1=ALU.add),
          reads=[K], writes=[tkey])


def rope_consts():
    half = 32
    inv = (10000.0 ** (-(np.arange(half, dtype=np.float32) / np.float32(half)))).astype(np.float32)
    rc = np.zeros((64, 2), np.float32)
    rc[:, 0] = np.concatenate([inv, inv])
    rc[:32, 1] = -1.0
    rc[32:, 1] = 1.0
    return rc


def build_mla_proj(T, wmeta, wplan, stop=99):
    nc = bass.Bass("TRN2", target_bir_lowering=False)
    NB = T // TB
    xT = nc.dram_tensor("xT", [D, T], F32, kind="ExternalInput").ap()
    pos = nc.dram_tensor("pos", [64, T], I32, kind="ExternalInput").ap()
    rcd = nc.dram_tensor("rc", [64, 2], F32, kind="ExternalInput").ap()
    wdram = nc.dram_tensor("wstream", [128, wmeta[-1][0] + wmeta[-1][1]], F32, kind="ExternalInput").ap()
    gd = nc.dram_tensor("gvecs", [128, DC + 8], F32, kind="ExternalInput").ap()
    qn_o = nc.dram_tensor("qn", [128, HEADS, T], BF16, kind="ExternalOutput").ap()
    qr_o = nc.dram_tensor("qr", [64, HEADS, T], BF16, kind="ExternalOutput").ap()
    kn_o = nc.dram_tensor("kn", [128, HEADS, T], BF16, kind="ExternalOutput").ap()
    kr_o = nc.dram_tensor("kr", [64, T], BF16, kind="ExternalOutput").ap()
    v_o = nc.dram_tensor("v", [T, D], BF16, kind="ExternalOutput").ap()
    xTv = xT.rearrange("(c p) t -> p c t", p=128)
    v_ov = v_o.rearrange("(t p) c -> p t c", p=128)

    B = Bld(nc, wdram, wmeta)
    P = B.P
    x_sb = P.sb("x_sb", [128, DC, TB], F32)
    h_sb = P.sb("h_sb", [128, DC, TB], BF16)
    cqp = P.sb("cqp", [128, 4, TB], F32)
    ckvp = P.sb("ckvp", [128, 4, TB], F32)
    cqn = P.sb("cqn", [128, 4, TB], BF16)
    ckvn = P.sb("ckvn", [128, 4, TB], BF16)
    g_sb = P.sb("g_sb", [128, DC + 8], F32)
    rc = P.sb("rc_sb", [64, 2], F32)
    pos_sb = P.sb("pos_sb", [64, TB], I32)
    B.rope_ws = P.sb("rope_ws", [64, 4, TB], F32)
    B.rope_wi = P.sb("rope_wi", [64, TB], I32)
    cos2 = P.sb("cos2", [64, TB], F32)
    sin2 = P.sb("sin2", [64, TB], F32)
    rt = P.sb("rt", [64, 2, 2, TB], F32)
    st_qn = P.sb("st_qn", [128, HEADS, TB], BF16)
    st_qr = P.sb("st_qr", [64, HEADS, TB], BF16)
    st_kn = P.sb("st_kn", [128, HEADS, TB], BF16)
    st_kr = P.sb("st_kr", [64, TB], BF16)
    st_v = P.sb("st_v", [128, 4, D], BF16)

    P.dma("sp", g_sb[:, :], gd[:, :], writes=["g"])
    P.dma("sp", rc[:, :], rcd[:, :], writes=["rc"])

    rti = [0]

    def rope_apply(ps_a, ka, ps_b, kb, out_ap, out_key):
        i = rti[0] % 2
        rti[0] += 1
        t1, t2 = rt[:, i, 0, :], rt[:, i, 1, :]
        P.add("dve", lambda e: e.tensor_tensor(out=t1, in0=ps_a[0:64, :], in1=cos2[:, :], op=ALU.mult),
              reads=[ka, "tables"], writes=[("rt", i, 0)])
        P.add("dve", lambda e: e.tensor_tensor(out=t2, in0=ps_b[0:64, :], in1=sin2[:, :], op=ALU.mult),
              reads=[kb, "tables"], writes=[("rt", i, 1)])
        P.add("dve", lambda e: e.tensor_tensor(out=out_ap, in0=t1, in1=t2, op=ALU.add),
              reads=[("rt", i, 0), ("rt", i, 1)], writes=[out_key])

    def pair_group(tiles_a, tiles_b, rhs_of_k, key_of_k, out_ap, out_key):
        hold = {}

        def ev_a(ps, pkey):
            hold["a"] = (ps, pkey)

        def ev_b(ps, pkey):
            pa, ka = hold["a"]
            rope_apply(pa, ka, ps, pkey, out_ap, out_key)
        B.group(tiles_a, rhs_of_k, key_of_k, ev_a, mw=64)
        B.group(tiles_b, rhs_of_k, key_of_k, ev_b, mw=64)

    def body(b):
        ts = slice(b * TB, (b + 1) * TB)
        for c0 in range(0, DC, 4):
            P.dma("sp", x_sb[:, c0:c0 + 4, :], xTv[:, c0:c0 + 4, ts], writes=[("x", c) for c in range(c0, c0 + 4)])
        P.dma("sp", pos_sb[:, :], pos[:, ts], writes=["pos"])
        emit_rope_tables(B, pos_sb[:, :], "pos", rc, TB, cos2[:, :], sin2[:, :], "tables")
        if stop <= 1:
            return
        B.norm_stats(x_sb, [("x", c) for c in range(DC)], h_sb, [("h", c) for c in range(DC)], DC, D)
        B.norm_apply(h_sb, [("h", c) for c in range(DC)], x_sb, [("x", c) for c in range(DC)],
                     lambda c: g_sb[:, c:c + 1], "g", DC)
        hk = lambda kk: h_sb[:, kk, :]
        hkey = lambda kk: ("h", kk)
        for m in range(4):
            B.group(wplan["dq"][m], hk, hkey, lambda ps, pkey, m=m: P.add(
                "act", lambda e: e.copy(out=cqp[:, m, :], in_=ps[:, :]), reads=[pkey], writes=[("cqp", m)]))
        for m in range(4):
            B.group(wplan["dkv"][m], hk, hkey, lambda ps, pkey, m=m: P.add(
                "act", lambda e: e.copy(out=ckvp[:, m, :], in_=ps[:, :]), reads=[pkey], writes=[("ckvp", m)]))
        if stop <= 2:
            return
        pair_group(wplan["kr"][0], wplan["kr_sw"][0], hk, hkey, st_kr[:, :], "st_kr")
        P.dma("sp", kr_o[:, ts], st_kr[:, :], reads=["st_kr"])
        if stop <= 3:
            return
        B.norm_stats(cqp, [("cqp", c) for c in range(4)], cqn, [("cqn", c) for c in range(4)], 4, 512)
        B.norm_apply(cqn, [("cqn", c) for c in range(4)], cqp, [("cqp", c) for c in range(4)],
                     lambda c: g_sb[:, DC + c:DC + c + 1], "g", 4)
        B.norm_stats(ckvp, [("ckvp", c) for c in range(4)], ckvn, [("ckvn", c) for c in range(4)], 4, 512)
        B.norm_apply(ckvn, [("ckvn", c) for c in range(4)], ckvp, [("ckvp", c) for c in range(4)],
                     lambda c: g_sb[:, DC + 4 + c:DC + 4 + c + 1], "g", 4)
        if stop <= 4:
            return
        qk = lambda kk: cqn[:, kk, :]
        qkey = lambda kk: ("cqn", kk)
        kk_ = lambda kk: ckvn[:, kk, :]
        kkey = lambda kk: ("ckvn", kk)
        for h in range(HEADS):
            B.group(wplan["uq_n"][h], qk, qkey, lambda ps, pkey, h=h: P.add(
                "act", lambda e: e.copy(out=st_qn[:, h, :], in_=ps[:, :]), reads=[pkey], writes=[("st_qn", h)]))
        P.dma("sp", qn_o[:, :, ts], st_qn[:, :, :], reads=[("st_qn", h) for h in range(HEADS)])
        if stop <= 5:
            return
        for h in range(HEADS):
            pair_group(wplan["uq_r"][h], wplan["uq_rs"][h], qk, qkey, st_qr[:, h, :], ("st_qr", h))
        P.dma("sp", qr_o[:, :, ts], st_qr[:, :, :], reads=[("st_qr", h) for h in range(HEADS)])
        if stop <= 6:
            return
        for h in range(HEADS):
            B.group(wplan["uk"][h], kk_, kkey, lambda ps, pkey, h=h: P.add(
                "act", lambda e: e.copy(out=st_kn[:, h, :], in_=ps[:, :]), reads=[pkey], writes=[("st_kn", h)]))
        P.dma("sp", kn_o[:, :, ts], st_kn[:, :, :], reads=[("st_kn", h) for h in range(HEADS)])
        if stop <= 7:
            return
        for cg in range(4):
            wap, wkey = B.ring.fetch(wplan["uv"][cg])
            for tt in range(4):
                ps, pkey = B.next_ps()
                for k in range(4):
                    P.add("pe", lambda e, ps=ps, k=k, tt=tt, wap=wap: e.matmul(
                        ps[:, :], ckvn[:, k, tt * 128:(tt + 1) * 128], wap[:, k * 512:(k + 1) * 512],
                        start=(k == 0), stop=(k == 3)), reads=[wkey, ("ckvn", k)], writes=[pkey])
                P.add("act", lambda e, ps=ps, tt=tt, cg=cg: e.copy(out=st_v[:, tt, cg * 512:(cg + 1) * 512],
                                                                    in_=ps[:, :]),
                      reads=[pkey], writes=[("st_v", tt, cg)])
        if stop <= 8:
            return
        for hf in range(2):
            P.dma("sp", v_ov[:, b * 4:(b + 1) * 4, hf * 1024:(hf + 1) * 1024], st_v[:, :, hf * 1024:(hf + 1) * 1024],
                  reads=[("st_v", tt, cg) for tt in range(4) for cg in range(2 * hf, 2 * hf + 2)])
    for b in range(NB):
        body(b)
    P.emit()
    return nc


def prep_mla_proj_weights(w_dq, w_dkv, w_uq, w_ukv):
    ws = WStream()
    plan = {}
    plan["dq"] = ws.add_matrix(w_dq)
    plan["dkv"] = ws.add_matrix(w_dkv[:, :512])
    kr = w_dkv[:, 512:576]
    plan["kr"] = ws.add_matrix(kr, mw=64)
    plan["kr_sw"] = ws.add_matrix(np.concatenate([kr[:, 32:], kr[:, :32]], axis=1), mw=64)
    uq = w_uq.reshape(512, HEADS, 192)
    plan["uq_n"] = ws.add_matrix(np.ascontiguousarray(uq[:, :, :128]).reshape(512, HEADS * 128))
    plan["uq_r"] = []
    plan["uq_rs"] = []
    for h in range(HEADS):
        r = uq[:, h, 128:192]
        plan["uq_r"].append(ws.add_matrix(r, mw=64)[0])
        plan["uq_rs"].append(ws.add_matrix(np.concatenate([r[:, 32:], r[:, :32]], axis=1), mw=64)[0])
    ukv = w_ukv.reshape(512, HEADS, 256)
    plan["uk"] = ws.add_matrix(np.ascontiguousarray(ukv[:, :, :128]).reshape(512, HEADS * 128))
    wv = np.ascontiguousarray(ukv[:, :, 128:]).reshape(512, HEADS * 128)
    plan["uv"] = []
    for cg in range(4):
        t = wv[:, cg * 512:(cg + 1) * 512].reshape(4, 128, 512).transpose(1, 0, 2).reshape(128, 2048)
        plan["uv"].append(ws.add_raw(t))
    return ws.array(), ws.meta, plan


ATT_SCALE = 192.0 ** -0.5


def attn_masks():
    k = np.arange(128)[:, None, None] + 128 * np.arange(4)[None, :, None]
    q = np.arange(512)[None, None, :]
    return ((k // 64) <= (q // 64)).astype(np.float32).astype(NPBF16)


def build_attn(S, NH):
    nc = bass.Bass("TRN2", target_bir_lowering=False)
    NQ = S // TB
    NKP = S // 1024
    qn = nc.dram_tensor("qn", [NH, 128, S], BF16, kind="ExternalInput").ap()
    qr = nc.dram_tensor("qr", [NH, 64, S], BF16, kind="ExternalInput").ap()
    kn = nc.dram_tensor("kn", [NH, 128, S], BF16, kind="ExternalInput").ap()
    kr = nc.dram_tensor("kr", [64, S], BF16, kind="ExternalInput").ap()
    v = nc.dram_tensor("v", [NH, 128, S // 128, 128], BF16, kind="ExternalInput").ap()
    msk = nc.dram_tensor("masks", [128, 4, 512], BF16, kind="ExternalInput").ap()
    o = nc.dram_tensor("o", [NH, 128, S], BF16, kind="ExternalOutput").ap()

    emit_attn(nc, S, NH, qn, qr, kn, kr, v, msk, o)
    return nc


def emit_attn(nc, S, NH, qn, qr, kn, kr, v, msk, o):
    NQ = S // TB
    NKP = S // 1024
    B = Bld(nc, None, None, nps=4, with_ss=False)
    P = B.P
    C = B.C
    ps_o = [P.ps(f"ps_o{i}") for i in range(2)]
    ps_d = [P.ps(f"ps_d{i}") for i in range(2)]
    kn_sb = [P.sb(f"kn_sb{i}", [128, S], BF16) for i in range(2)]
    v_sb = [P.sb(f"v_sb{i}", [128, S // 128, 128], BF16) for i in range(2)]
    kr_sb = P.sb("kr_sb", [128, S], BF16)
    m_sb = P.sb("m_sb", [128, 4, 512], BF16)
    qn_sb = [P.sb(f"qn_sb{i}", [128, TB], BF16) for i in range(2)]
    qr_sb = [P.sb(f"qr_sb{i}", [128, TB], BF16) for i in range(2)]
    NPT = 6
    pt = [P.sb(f"pt{i}", [128, TB], BF16) for i in range(NPT)]
    rec = [P.sb(f"rec{i}", [128, TB], F32) for i in range(2)]
    ost = [P.sb(f"ost{i}", [128, TB], BF16) for i in range(2)]
    acc = [P.sb(f"acc{i}", [128, TB], F32) for i in range(2)]
    ones_f = P.sb("ones_f", [128, 128], F32)
    P.add("dve", lambda e: e.memset(ones_f[:, :], 1.0), writes=["ones_f"])

    for j in range(4):
        P.dma("sp", m_sb[:, j, :], msk[:, j, :], writes=[("m", j)])
    for p in range(NKP):
        P.add("pool", lambda e, p=p: e.memset(kr_sb[:, p * 1024:(p + 1) * 1024], 0.0), writes=[("kr", p)])
    for i in range(2):
        P.add("pool", lambda e, i=i: e.memset(qr_sb[i][:, :], 0.0), writes=[("qr", i)])
    for p in range(NKP):
        P.dma("sp", kr_sb[0:64, p * 1024:(p + 1) * 1024], kr[:, p * 1024:(p + 1) * 1024], writes=[("kr", p)])
    pti = [0]
    qi = [0]
    def qblock(h, hp, qb, i):
        qs = slice(qb * TB, (qb + 1) * TB)
        P.dma("sp", qn_sb[i][:, :], qn[h, :, qs], writes=[("qn", i)])
        P.dma("sp", qr_sb[i][0:64, :], qr[h, :, qs], writes=[("qr", i)])
        nkt = 4 * (qb + 1)
        tiles = {}

        def S_(kt):
            ps, pkey = B.next_ps()
            j = pti[0] % NPT
            pti[0] += 1
            tiles[kt] = j
            ks = slice(kt * 128, (kt + 1) * 128)
            P.add("pe", lambda e: e.matmul(ps[:, :], kn_sb[hp][:, ks], qn_sb[i][:, :], start=True, stop=False),
                  reads=[("kn", hp, kt // 8), ("qn", i)], writes=[pkey])
            P.add("pe", lambda e: e.matmul(ps[:, :], kr_sb[:, ks], qr_sb[i][:, :], start=False, stop=True),
                  reads=[("kr", kt // 8), ("qr", i)], writes=[pkey])
            P.add("act", lambda e: e.activation(out=pt[j][:, :], in_=ps[:, :], func=AF.Exp, scale=ATT_SCALE),
                  reads=[pkey], writes=[("pt", j)])
            if kt >= 4 * qb:
                jj = kt - 4 * qb
                P.add("dve", lambda e: e.tensor_tensor(out=pt[j][:, :], in0=pt[j][:, :], in1=m_sb[:, jj, :],
                                                       op=ALU.mult),
                      reads=[("pt", j), ("m", jj)], writes=[("pt", j)])

        def PV_(kt):
            j = tiles[kt]
            P.add("pe", lambda e: e.matmul(ps_o[i][:, :], v_sb[hp][:, kt, :], pt[j][:, :], start=(kt == 0),
                                           stop=(kt == nkt - 1)),
                  reads=[("v", hp, kt // 8), ("pt", j)], writes=[("ps_o", i)])
            if kt == 0:
                P.add("dve", lambda e: e.tensor_copy(out=acc[i][:, :], in_=pt[j][:, :]),
                      reads=[("pt", j)], writes=[("acc", i)])
            else:
                P.add("dve", lambda e: e.tensor_tensor(out=acc[i][:, :], in0=acc[i][:, :], in1=pt[j][:, :],
                                                       op=ALU.add),
                      reads=[("pt", j), ("acc", i)], writes=[("acc", i)])

        LA = 3
        for kt in range(min(LA, nkt)):
            S_(kt)
        for kt in range(nkt):
            PV_(kt)
            if kt + LA < nkt:
                S_(kt + LA)
        P.add("pe", lambda e: e.matmul(ps_d[i][:, :], ones_f[:, :], acc[i][:, :], start=True, stop=True),
              reads=["ones_f", ("acc", i)], writes=[("ps_d", i)])
        P.add("dve", lambda e, i=i: e.reciprocal(out=rec[i][:, :], in_=ps_d[i][:, :]),
              reads=[("ps_d", i)], writes=[("rec", i)])
        P.add("dve", lambda e, i=i: e.tensor_tensor(out=ost[i][:, :], in0=ps_o[i][:, :], in1=rec[i][:, :],
                                                    op=ALU.mult),
              reads=[("ps_o", i), ("rec", i)], writes=[("ost", i)])
        P.dma("sp", o[h, :, qs], ost[i][:, :], reads=[("ost", i)])
    for h in range(NH):
        hp = h % 2
        for p in range(NKP):
            P.dma("sp", kn_sb[hp][:, p * 1024:(p + 1) * 1024], kn[h, :, p * 1024:(p + 1) * 1024],
                  writes=[("kn", hp, p)])
            P.dma("sp", v_sb[hp][:, p * 8:(p + 1) * 8, :], v[h, :, p * 8:(p + 1) * 8, :], writes=[("v", hp, p)])
        for qb in range(NQ):
            qblock(h, hp, qb, qi[0] % 2)
            qi[0] += 1
    P.emit()


CH = 84
NJ = 4


def build_lru(NT, NBATCH, wmeta, wplan):
    nc = bass.Bass("TRN2", target_bir_lowering=False)
    NB = NT // TB
    NBB = NB // NBATCH
    xT = nc.dram_tensor("xT", [D, NT], F32, kind="ExternalInput").ap()
    wdram = nc.dram_tensor("wstream", [128, wmeta[-1][0] + wmeta[-1][1]], F32, kind="ExternalInput").ap()
    wgd = nc.dram_tensor("wg", [CH, 2 * NJ * 2 * CH], F32, kind="ExternalInput").ap()
    vecd = nc.dram_tensor("vec", [CH, NJ, 10], F32, kind="ExternalInput").ap()
    gd = nc.dram_tensor("gpre", [128, DC], F32, kind="ExternalInput").ap()
    hy = nc.dram_tensor("hy", [NJ, CH, NT], BF16, kind="ExternalOutput").ap()
    xTv = xT.rearrange("(c p) t -> p c t", p=128)
    hyv = hy.rearrange("j p t -> p j t")

    B = Bld(nc, wdram, wmeta)
    P = B.P
    x_sb = P.sb("x_sb", [128, DC, TB], F32)
    h_sb = P.sb("h_sb", [128, DC, TB], BF16)
    g_sb = P.sb("g_sb", [128, DC], F32)
    wg_sb = P.sb("wg_sb", [128, 2, NJ, 2, CH], BF16)
    vec = P.sb("vec_sb", [128, NJ, 10], F32)
    cc = P.sb("cc", [128, 4, NJ], F32)
    one_c = P.sb("one_c", [128, 1], F32)
    y_sb = P.sb("y_sb", [128, NJ, TB], F32)
    uxb = P.sb("uxb", [128, NJ, TB + 4], F32)
    u_sb = P.sb("u_sb", [128, NJ, TB], F32)
    u_bf = P.sb("u_bf", [128, NJ, TB], BF16)
    r_sb = P.sb("r_sb", [128, NJ, TB], F32)
    ig_sb = P.sb("ig_sb", [128, NJ, TB], F32)
    a_sb = P.sb("a_sb", [128, NJ, TB], F32)
    m_sb = P.sb("m_sb", [128, NJ, TB], F32)
    inp_sb = P.sb("inp_sb", [128, NJ, TB], F32)
    hs_sb = P.sb("hs_sb", [128, NJ, TB], F32)
    ost = P.sb("ost", [128, NJ, TB], BF16)
    state = P.sb("state", [128, NJ], F32)

    P.dma("sp", g_sb[:, :], gd[:, :], writes=["g"])
    P.dma("sp", vec[0:CH, :, :], vecd[:, :, :], writes=["vec"])
    P.dma("pool", wg_sb[0:CH, :, :, :, :].rearrange("p a b c d -> p (a b c d)"), wgd[:, :], writes=["wg"])
    P.add("dve", lambda e: e.memset(one_c[:, :], 1.0), writes=["one_c"])
    e_ = cc[0:CH, 0, :]
    t_ = cc[0:CH, 1, :]
    P.add("act", lambda e: e.activation(out=e_, in_=vec[0:CH, :, 5], func=AF.Exp, scale=-1.0), reads=["vec"],
          writes=["cc"])
    coef = [1.0 / 5, -1.0 / 4, 1.0 / 3, -1.0 / 2, 1.0]
    P.add("dve", lambda e: e.tensor_scalar(out=t_, in0=e_, scalar1=-1.0 / 6, scalar2=coef[0], op0=ALU.mult,
                                           op1=ALU.add), reads=["cc"], writes=["cc"])
    for cf in coef[1:]:
        P.add("dve", lambda e: e.tensor_tensor(out=t_, in0=t_, in1=e_, op=ALU.mult), reads=["cc"], writes=["cc"])
        P.add("dve", lambda e, cf=cf: e.tensor_scalar(out=t_, in0=t_, scalar1=cf, scalar2=None, op0=ALU.add),
              reads=["cc"], writes=["cc"])
    P.add("dve", lambda e: e.tensor_tensor(out=t_, in0=t_, in1=e_, op=ALU.mult), reads=["cc"], writes=["cc"])
    P.add("dve", lambda e: e.tensor_scalar(out=cc[0:CH, 2, :], in0=t_, scalar1=-8.0, scalar2=None, op0=ALU.mult),
          reads=["cc"], writes=["cc"])
    P.add("dve", lambda e: e.tensor_scalar(out=cc[0:CH, 3, :], in0=t_, scalar1=-16.0, scalar2=None, op0=ALU.mult),
          reads=["cc"], writes=["cc"])

    def body(b):
        ts = slice(b * TB, (b + 1) * TB)
        if b % NBB == 0:
            P.add("dve", lambda e: e.memset(state[:, :], 0.0), writes=[("state", j) for j in range(NJ)])
            P.add("dve", lambda e: e.memset(uxb[:, :, 0:3], 0.0), writes=[("ux", j) for j in range(NJ)])
        for c0 in range(0, DC, 4):
            P.dma("sp", x_sb[:, c0:c0 + 4, :], xTv[:, c0:c0 + 4, ts], writes=[("x", c) for c in range(c0, c0 + 4)])
        B.norm_stats(x_sb, [("x", c) for c in range(DC)], h_sb, [("h", c) for c in range(DC)], DC, D)
        B.norm_apply(h_sb, [("h", c) for c in range(DC)], x_sb, [("x", c) for c in range(DC)],
                     lambda c: g_sb[:, c:c + 1], "g", DC)
        hk = lambda kk: h_sb[:, kk, :]
        hkey = lambda kk: ("h", kk)
        for j in range(NJ):
            B.group(wplan["wy"][j], hk, hkey, lambda ps, pkey, j=j: P.add(
                "act", lambda e: e.activation(out=y_sb[0:CH, j, :], in_=ps[0:CH, :], func=AF.Gelu,
                                              bias=vec[0:CH, j, 0:1]),
                reads=[pkey, "vec"], writes=[("y", j)]), mw=CH)
        for j in range(NJ):
            B.group(wplan["wx"][j], hk, hkey, lambda ps, pkey, j=j: P.add(
                "act", lambda e: e.activation(out=uxb[0:CH, j, 3:3 + TB], in_=ps[0:CH, :], func=AF.Identity,
                                              bias=vec[0:CH, j, 1:2]),
                reads=[pkey, "vec"], writes=[("ux", j)]), mw=CH)
        for j in range(NJ):
            P.add("dve", lambda e, j=j: e.tensor_scalar(out=u_sb[0:CH, j, :], in0=uxb[0:CH, j, 0:TB],
                                                        scalar1=vec[0:CH, j, 6:7], scalar2=vec[0:CH, j, 2:3],
                                                        op0=ALU.mult, op1=ALU.add),
                  reads=[("ux", j), "vec"], writes=[("u", j)])
            for i in range(1, 4):
                P.add("dve", lambda e, j=j, i=i: e.scalar_tensor_tensor(
                    out=u_sb[0:CH, j, :], in0=uxb[0:CH, j, i:i + TB], scalar=vec[0:CH, j, 6 + i:7 + i],
                    in1=u_sb[0:CH, j, :], op0=ALU.mult, op1=ALU.add),
                    reads=[("ux", j), "vec", ("u", j)], writes=[("u", j)])
            P.add("dve", lambda e, j=j: e.tensor_copy(out=uxb[0:CH, j, 0:3], in_=uxb[0:CH, j, TB:TB + 3]),
                  reads=[("ux", j)], writes=[("ux", j)])
            P.add("act", lambda e, j=j: e.copy(out=u_bf[0:CH, j, :], in_=u_sb[0:CH, j, :]),
                  reads=[("u", j)], writes=[("ubf", j)])
        for gi, dst, bidx in ((0, r_sb, 3), (1, ig_sb, 4)):
            for j in range(NJ):
                ps, pkey = B.next_ps()
                for i in range(2):
                    P.add("pe", lambda e, ps=ps, gi=gi, j=j, i=i: e.matmul(
                        ps[0:CH, :], wg_sb[0:CH, gi, j, i, :], u_bf[0:CH, 2 * (j // 2) + i, :], start=(i == 0),
                        stop=(i == 1)), reads=["wg", ("ubf", 2 * (j // 2) + i)], writes=[pkey])
                P.add("act", lambda e, ps=ps, j=j, dst=dst, bidx=bidx: e.activation(
                    out=dst[0:CH, j, :], in_=ps[0:CH, :], func=AF.Sigmoid, bias=vec[0:CH, j, bidx:bidx + 1]),
                    reads=[pkey, "vec"], writes=[("gate", gi, j)])
        for j in range(NJ):
            P.add("act", lambda e, j=j: e.activation(out=a_sb[0:CH, j, :], in_=r_sb[0:CH, j, :], func=AF.Exp,
                                                     scale=cc[0:CH, 2, j:j + 1]),
                  reads=[("gate", 0, j), "cc"], writes=[("a", j)])
            P.add("act", lambda e, j=j: e.activation(out=m_sb[0:CH, j, :], in_=r_sb[0:CH, j, :], func=AF.Exp,
                                                     scale=cc[0:CH, 3, j:j + 1]),
                  reads=[("gate", 0, j), "cc"], writes=[("m", j)])
        for j in range(NJ):
            P.add("dve", lambda e, j=j: e.tensor_scalar(out=m_sb[0:CH, j, :], in0=m_sb[0:CH, j, :], scalar1=-1.0,
                                                        scalar2=1.0, op0=ALU.mult, op1=ALU.add),
                  reads=[("m", j)], writes=[("m", j)])
        for j in range(NJ):
            P.add("act", lambda e, j=j: e.activation(out=m_sb[0:CH, j, :], in_=m_sb[0:CH, j, :], func=AF.Sqrt),
                  reads=[("m", j)], writes=[("m", j)])
        for j in range(NJ):
            P.add("dve", lambda e, j=j: e.tensor_tensor(out=inp_sb[0:CH, j, :], in0=ig_sb[0:CH, j, :],
                                                        in1=u_sb[0:CH, j, :], op=ALU.mult),
                  reads=[("gate", 1, j), ("u", j)], writes=[("inp", j)])
            P.add("dve", lambda e, j=j: e.tensor_tensor(out=inp_sb[0:CH, j, :], in0=inp_sb[0:CH, j, :],
                                                        in1=m_sb[0:CH, j, :], op=ALU.mult),
                  reads=[("inp", j), ("m", j)], writes=[("inp", j)])
            P.add("dve", lambda e, j=j: e.tensor_tensor_scan(
                out=hs_sb[0:CH, j, :], data0=a_sb[0:CH, j, :], data1=inp_sb[0:CH, j, :],
                initial=state[0:CH, j:j + 1], op0=ALU.mult, op1=ALU.add),
                reads=[("a", j), ("inp", j), ("state", j)], writes=[("hs", j)])
            P.add("dve", lambda e, j=j: e.tensor_copy(out=state[0:CH, j:j + 1], in_=hs_sb[0:CH, j, TB - 1:TB]),
                  reads=[("hs", j)], writes=[("state", j)])
            P.add("dve", lambda e, j=j: e.tensor_tensor(out=ost[0:CH, j, :], in0=hs_sb[0:CH, j, :],
                                                        in1=y_sb[0:CH, j, :], op=ALU.mult),
                  reads=[("hs", j), ("y", j)], writes=[("ost", j)])
        P.dma("sp", hyv[:, :, ts], ost[0:CH, :, :], reads=[("ost", j) for j in range(NJ)])

    for b in range(NB):
        body(b)
    P.emit()
    return nc


def prep_lru_weights(core, w_y, b_y, w_x, b_x, conv_w, conv_b, w_ga, b_ga, w_gi, b_gi, lam):
    c0 = core * NJ * CH
    c1 = c0 + NJ * CH
    ws = WStream()
    plan = {"wy": ws.add_matrix(np.ascontiguousarray(w_y[:, c0:c1]), mw=CH),
            "wx": ws.add_matrix(np.ascontiguousarray(w_x[:, c0:c1]), mw=CH)}
    wg = np.zeros((CH, 2, NJ, 2, CH), np.float32)
    for gi, w in enumerate((w_ga, w_gi)):
        for j in range(NJ):
            n = 2 * core + j // 2
            for i in range(2):
                wg[:, gi, j, i, :] = w[n, i * CH:(i + 1) * CH, (j % 2) * CH:(j % 2 + 1) * CH]
    vec = np.zeros((CH, NJ, 10), np.float32)
    for idx, vv in enumerate((b_y, b_x, conv_b, b_ga, b_gi, lam)):
        vec[:, :, idx] = vv[c0:c1].reshape(NJ, CH).T
    for i in range(4):
        vec[:, :, 6 + i] = conv_w[i, c0:c1].reshape(NJ, CH).T
    return ws.array(), ws.meta, plan, np.ascontiguousarray(wg.reshape(CH, -1)), vec


def _run(nc, in_maps):
    res = run_bass_kernel_spmd(nc, in_maps, core_ids=list(range(NCORES)))
    return res.results


def kernel_unfused(x, positions, mix_pre_g, mix_post_g, mlp_pre_g, mlp_post_g,
           mla_w_dq, mla_g_q, mla_w_uq, mla_w_dkv, mla_g_kv, mla_w_ukv, mla_w_o,
           lru_w_y, lru_b_y, lru_w_x, lru_b_x, lru_conv_w, lru_conv_b,
           lru_w_ga, lru_b_ga, lru_w_gi, lru_b_gi, lru_lam, lru_w_out, lru_b_out,
           mlp_w1, mlp_w2):
    f32 = lambda a: np.asarray(a, np.float32)
    x = f32(x)
    positions = np.asarray(positions, np.int32)
    NBATCH, S, _ = x.shape
    T = NBATCH * S // NCORES
    QPB = S // T
    NT = NBATCH * S
    NH = HEADS * NBATCH // NCORES
    xs = x.reshape(NCORES, T, D)
    xT = [np.ascontiguousarray(xs[c].T) for c in range(NCORES)]
    rc = rope_consts()
    masks = attn_masks()
    zero_b = np.zeros((128, DC), np.float32)
    depth = mix_pre_g.shape[0]

    def post_mlp(i, mT, w_o, b_o, KIN):
        warr, wmeta, wplan = prep_post_mlp_weights(f32(w_o), f32(mlp_w1[i]), f32(mlp_w2[i]))
        nc = build_post_mlp(T, KIN, b_o is not None, wmeta, wplan)
        gpost, gpre, gpost2 = gvec(mix_post_g[i]), gvec(mlp_pre_g[i]), gvec(mlp_post_g[i])
        bo = gvec(b_o) if b_o is not None else zero_b
        maps = [{"xT": xT[c], "mT": mT[c], "wstream": warr, "gpost": gpost, "gpre": gpre, "gpost2": gpost2,
                 "bout": bo} for c in range(NCORES)]
        r = _run(nc, maps)
        return [np.asarray(r[c]["oT"]) for c in range(NCORES)]

    for i in range(depth):
        j = i // 2
        if i % 2 == 0:
            warr, wmeta, wplan = prep_mla_proj_weights(f32(mla_w_dq[j]), f32(mla_w_dkv[j]), f32(mla_w_uq[j]),
                                                       f32(mla_w_ukv[j]))
            nc = build_mla_proj(T, wmeta, wplan)
            gv = np.ascontiguousarray(np.concatenate([gvec(mix_pre_g[i]), gvec(mla_g_q[j]), gvec(mla_g_kv[j])],
                                                     axis=1))
            maps = []
            for c in range(NCORES):
                b, q = divmod(c, QPB)
                pos = np.ascontiguousarray(np.broadcast_to(positions[b, q * T:(q + 1) * T][None, :], (64, T)))
                maps.append({"xT": xT[c], "pos": pos, "rc": rc, "wstream": warr, "gvecs": gv})
            r1 = _run(nc, maps)
            del maps
            maps = []
            for c in range(NCORES):
                b, hg = divmod(c, QPB)
                hs = slice(hg * NH, (hg + 1) * NH)
                cores = [b * QPB + q for q in range(QPB)]
                qn = np.concatenate([np.asarray(r1[cc]["qn"])[:, hs, :] for cc in cores], axis=2)
                qr = np.concatenate([np.asarray(r1[cc]["qr"])[:, hs, :] for cc in cores], axis=2)
                kn = np.concatenate([np.asarray(r1[cc]["kn"])[:, hs, :] for cc in cores], axis=2)
                kr = np.concatenate([np.asarray(r1[cc]["kr"]) for cc in cores], axis=1)
                v = np.concatenate([np.asarray(r1[cc]["v"])[:, hg * NH * 128:(hg + 1) * NH * 128] for cc in cores],
                                   axis=0)
                v = v.reshape(S // 128, 128, NH, 128).transpose(2, 1, 0, 3)
                maps.append({"qn": np.ascontiguousarray(qn.transpose(1, 0, 2)),
                             "qr": np.ascontiguousarray(qr.transpose(1, 0, 2)),
                             "kn": np.ascontiguousarray(kn.transpose(1, 0, 2)),
                             "kr": np.ascontiguousarray(kr), "v": np.ascontiguousarray(v), "masks": masks})
            del r1
            nc = build_attn(S, NH)
            r2 = _run(nc, maps)
            del maps
            mT = []
            for c in range(NCORES):
                b, q = divmod(c, QPB)
                parts = [np.asarray(r2[b * QPB + hg]["o"])[:, :, q * T:(q + 1) * T].reshape(NH * 128, T)
                         for hg in range(QPB)]
                mT.append(np.ascontiguousarray(np.concatenate(parts, axis=0)))
            del r2
            xT = post_mlp(i, mT, mla_w_o[j], None, D)
        else:
            xfull = np.ascontiguousarray(np.concatenate(xT, axis=1))
            gp = gvec(mix_pre_g[i])
            maps = []
            wm = wp = None
            for c in range(NCORES):
                warr, wm, wp, wg, vec = prep_lru_weights(
                    c, f32(lru_w_y[j]), f32(lru_b_y[j]), f32(lru_w_x[j]), f32(lru_b_x[j]), f32(lru_conv_w[j]),
                    f32(lru_conv_b[j]), f32(lru_w_ga[j]), f32(lru_b_ga[j]), f32(lru_w_gi[j]), f32(lru_b_gi[j]),
                    f32(lru_lam[j]))
                maps.append({"xT": xfull, "wstream": warr, "wg": wg, "vec": vec, "gpre": gp})
            nc = build_lru(NT, NBATCH, wm, wp)
            r4 = _run(nc, maps)
            del maps, xfull
            mfull = np.concatenate([np.asarray(r4[c]["hy"]).reshape(NJ * CH, NT) for c in range(NCORES)], axis=0)
            del r4
            mT = [np.ascontiguousarray(mfull[:, c * T:(c + 1) * T]) for c in range(NCORES)]
            del mfull
            xT = post_mlp(i, mT, lru_w_out[j], lru_b_out[j], DRNN)
    out = np.stack([xT[c].T for c in range(NCORES)], axis=0).reshape(NBATCH, S, D)
    return np.ascontiguousarray(out.astype(np.float32))


GROUPS = [[0, 1, 2, 3], [4, 5, 6, 7]]
TPG = 4
NHC = HEADS // TPG
FFC = DFF // TPG // 128
LJ = 8


def _blk(ap3, b):
    return ap3[b].rearrange("(c p) t -> p c t", p=128)


class Phase:
    def __init__(self, nc, wdram, wmeta, gdram, nps=6, nslots=6):
        self.B = Bld(nc, wdram, wmeta, nps=nps, nslots=nslots)
        P = self.B.P
        self.P = P
        self.x_sb = P.sb("x_sb", [128, DC, TB], F32)
        self.y_sb = P.sb("y_sb", [128, DC, TB], F32)
        self.h_sb = P.sb("h_sb", [128, DC, TB], BF16)
        self.g_sb = P.sb("g_sb", [128, 3, DC], F32)
        self.tmp2 = P.sb("tmp2", [128, 2, TB], F32)
        P.dma("sp", self.g_sb[:, :, :], gdram[:, :, :], writes=["g"])
        self.xk = [("x", c) for c in range(DC)]
        self.yk = [("y", c) for c in range(DC)]
        self.hk = [("h", c) for c in range(DC)]

    def prologue(self, b, x_src, red, has_bias, x_dst, pre_norm=True, ready=None):
        P, B = self.P, self.B
        x_sb, y_sb, h_sb, g_sb, tmp2 = self.x_sb, self.y_sb, self.h_sb, self.g_sb, self.tmp2
        for c0 in range(0, DC, 4):
            P.dma("sp", x_sb[:, c0:c0 + 4, :], x_src[:, c0:c0 + 4, :], writes=self.xk[c0:c0 + 4])
        if red is not None:
            rv = _blk(red, b)
            for c0 in range(0, DC, 4):
                P.dma("sp", y_sb[:, c0:c0 + 4, :], rv[:, c0:c0 + 4, :], writes=self.yk[c0:c0 + 4],
                      ext=(ready[b] if (ready is not None and c0 == 0) else None))
            if has_bias:
                for c in range(DC):
                    P.add("act", lambda e, c=c: e.activation(out=y_sb[:, c, :], in_=y_sb[:, c, :], func=AF.Identity,
                                                             bias=g_sb[:, 2, c:c + 1]),
                          reads=[("y", c), "g"], writes=[("y", c)])
            B.norm_stats(y_sb, self.yk, h_sb, self.hk, DC, D)
            for c in range(DC):
                t2 = tmp2[:, c % 2, :]
                P.add("dve", lambda e, c=c, t2=t2: e.scalar_tensor_tensor(
                    out=t2, in0=y_sb[:, c, :], scalar=g_sb[:, 0, c:c + 1], in1=B.rstd[:, :], op0=ALU.mult,
                    op1=ALU.mult), reads=[("y", c), "rstd", "g"], writes=[("tmp2", c % 2)])
                P.add("dve", lambda e, c=c, t2=t2: e.tensor_tensor(out=x_sb[:, c, :], in0=x_sb[:, c, :], in1=t2,
                                                                   op=ALU.add),
                      reads=[("tmp2", c % 2), ("x", c)], writes=[("x", c)])
            if x_dst is not None:
                for c0 in range(0, DC, 4):
                    P.dma("sp", x_dst[:, c0:c0 + 4, :], x_sb[:, c0:c0 + 4, :], reads=self.xk[c0:c0 + 4])
        if pre_norm:
            B.norm_stats(x_sb, self.xk, h_sb, self.hk, DC, D)
            B.norm_apply(h_sb, self.hk, x_sb, self.xk, lambda c: g_sb[:, 1, c:c + 1], "g", DC)

    def store_partial(self, ps, pkey, part, b, m, pst, idx):
        P = self.P
        i = idx % 2
        P.add("act", lambda e: e.copy(out=pst[:, i, :], in_=ps[:, :]), reads=[pkey], writes=[("pst", i)])
        P.dma("sp", part[b, m * 128:(m + 1) * 128, :], pst[:, i, :], reads=[("pst", i)], writes=[("partblk", b, m)])


def schedule_cc(P, part, red, b, ccops, delay=5):
    def go():
        ccops[b] = P.collective(part[b, :, :], red[b, :, :], reads=[("partblk", b, m) for m in range(DC)])
    if not hasattr(P, "pending_cc"):
        P.pending_cc = []
    P.pending_cc.append([delay, go])


def flush_cc(P):
    pend = getattr(P, "pending_cc", [])
    while pend:
        pend.pop(0)[1]()


def xsrc_view(xin, xres, b, from_input):
    if from_input:
        return xin.rearrange("(c p) t -> p c t", p=128)[:, :, b * TB:(b + 1) * TB]
    return _blk(xres, b)


def phase_allreduce(nc, part, red, NB, uid):
    G = Prog.G
    if G.get("cc_nc") is not nc:
        G["cc_nc"] = nc
        G["cc"] = nc.alloc_semaphore(name="cc_global")
        G["ccv"] = 0
    cc = G["cc"]
    with nc.Block() as blk:
        @blk.gpsimd
        def _(g):
            for b in range(NB):
                g.collective_compute("AllReduce", ALU.add, replica_groups=GROUPS, ins=[part[b, :, :]],
                                     outs=[red[b, :, :]]).then_inc(cc)
                G["ccv"] += 1
                g.wait_ge(cc, G["ccv"])


def phase_mlp(nc, S, xin, xres, part, red, from_input, has_bias, wdram, wmeta, wplan, gdram, ready, redo):
    NB = S // TB
    ph = Phase(nc, wdram, wmeta, gdram)
    P, B = ph.P, ph.B
    h1 = P.sb("h1", [128, FFC, TB], BF16)
    pst = P.sb("pst", [128, 2, TB], F32)

    def body(b):
        ph.prologue(b, xsrc_view(xin, xres, b, from_input), red, has_bias, _blk(xres, b), ready=ready)
        for m in range(FFC):
            def evac(ps, pkey, m=m):
                t2 = ph.tmp2[:, m % 2, :]
                P.add("act", lambda e: e.activation(out=t2, in_=ps[:, :], func=AF.Relu), reads=[pkey],
                      writes=[("tmp2", m % 2)])
                P.add("dve", lambda e: e.tensor_tensor(out=h1[:, m, :], in0=t2, in1=t2, op=ALU.mult),
                      reads=[("tmp2", m % 2)], writes=[("h1", m)])
            B.group(wplan["w1"][m], lambda kk: ph.h_sb[:, kk, :], lambda kk: ("h", kk), evac)
        for m in range(DC):
            B.group(wplan["w2"][m], lambda kk: h1[:, kk, :], lambda kk: ("h1", kk),
                    lambda ps, pkey, m=m: ph.store_partial(ps, pkey, part, b, m, pst, m))
        schedule_cc(P, part, redo, b, ccops)
    ccops = {}
    for b in range(NB):
        body(b)
    flush_cc(P)
    P.emit()
    return {b: op.ccval for b, op in ccops.items()}


def phase_wo(nc, S, o_s, part, wdram, wmeta, wplan, redo):
    NB = S // TB
    B = Bld(nc, wdram, wmeta, with_ss=False)
    P = B.P
    o_sb = P.sb("o_sb", [128, NHC, TB], BF16)
    pst = P.sb("pst", [128, 2, TB], F32)
    ov = o_s.rearrange("h p t -> p h t")

    def body(b):
        P.dma("sp", o_sb[:, :, :], ov[:, :, b * TB:(b + 1) * TB], writes=[("o", h) for h in range(NHC)])
        for m in range(DC):
            def evac(ps, pkey, m=m):
                i = m % 2
                P.add("act", lambda e: e.copy(out=pst[:, i, :], in_=ps[:, :]), reads=[pkey], writes=[("pst", i)])
                P.dma("sp", part[b, m * 128:(m + 1) * 128, :], pst[:, i, :], reads=[("pst", i)],
                      writes=[("partblk", b, m)])
            B.group(wplan["wo"][m], lambda kk: o_sb[:, kk, :], lambda kk: ("o", kk), evac)
        schedule_cc(P, part, redo, b, ccops)
    ccops = {}
    for b in range(NB):
        body(b)
    flush_cc(P)
    P.emit()
    return {b: op.ccval for b, op in ccops.items()}


def phase_final(nc, S, xres, red, outT, gdram, ready):
    NB = S // TB
    ph = Phase(nc, None, None, gdram)
    ov = outT.rearrange("(c p) t -> p c t", p=128)
    for b in range(NB):
        ph.prologue(b, _blk(xres, b), red, False, ov[:, :, b * TB:(b + 1) * TB], pre_norm=False, ready=ready)
    ph.P.emit()


def phase_mla_proj(nc, S, xin, xres, red, from_input, first, pos, rcd, wdram, wmeta, wplan, gdram, gqkv,
                   qn_o, qr_o, kn_o, kr_o, v_o, ready):
    NB = S // TB
    ph = Phase(nc, wdram, wmeta, gdram)
    P, B = ph.P, ph.B
    h_sb = ph.h_sb
    cqp = P.sb("cqp", [128, 4, TB], F32)
    ckvp = P.sb("ckvp", [128, 4, TB], F32)
    cqn = P.sb("cqn", [128, 4, TB], BF16)
    ckvn = P.sb("ckvn", [128, 4, TB], BF16)
    gq_sb = P.sb("gq_sb", [128, 8], F32)
    rc = P.sb("rc_sb", [64, 2], F32)
    pos_sb = P.sb("pos_sb", [64, TB], I32)
    B.rope_ws = P.sb("rope_ws", [64, 4, TB], F32)
    B.rope_wi = P.sb("rope_wi", [64, TB], I32)
    cos2 = P.sb("cos2", [64, TB], F32)
    sin2 = P.sb("sin2", [64, TB], F32)
    rt = P.sb("rt", [64, 2, 2, TB], F32)
    st_qn = P.sb("st_qn", [128, NHC, TB], BF16)
    st_qr = P.sb("st_qr", [64, NHC, TB], BF16)
    st_kn = P.sb("st_kn", [128, NHC, TB], BF16)
    st_kr = P.sb("st_kr", [64, TB], BF16)
    st_v = P.sb("st_v", [128, 4, NHC * 128], BF16)
    P.dma("sp", gq_sb[:, :], gqkv[:, :], writes=["gq"])
    P.dma("sp", rc[:, :], rcd[:, :], writes=["rc"])
    qn_v = qn_o.rearrange("h p t -> p h t")
    qr_v = qr_o.rearrange("h p t -> p h t")
    kn_v = kn_o.rearrange("h p t -> p h t")
    rti = [0]

    def rope_apply(ps_a, ka, ps_b, kb, out_ap, out_key):
        i = rti[0] % 2
        rti[0] += 1
        t1, t2 = rt[:, i, 0, :], rt[:, i, 1, :]
        P.add("dve", lambda e: e.tensor_tensor(out=t1, in0=ps_a[0:64, :], in1=cos2[:, :], op=ALU.mult),
              reads=[ka, "tables"], writes=[("rt", i, 0)])
        P.add("dve", lambda e: e.tensor_tensor(out=t2, in0=ps_b[0:64, :], in1=sin2[:, :], op=ALU.mult),
              reads=[kb, "tables"], writes=[("rt", i, 1)])
        P.add("dve", lambda e: e.tensor_tensor(out=out_ap, in0=t1, in1=t2, op=ALU.add),
              reads=[("rt", i, 0), ("rt", i, 1)], writes=[out_key])

    def pair_group(tiles_a, tiles_b, rhs_of_k, key_of_k, out_ap, out_key):
        hold = {}

        def ev_a(ps, pkey):
            hold["a"] = (ps, pkey)

        def ev_b(ps, pkey):
            pa, ka = hold["a"]
            rope_apply(pa, ka, ps, pkey, out_ap, out_key)
        B.group(tiles_a, rhs_of_k, key_of_k, ev_a, mw=64)
        B.group(tiles_b, rhs_of_k, key_of_k, ev_b, mw=64)

    def body(b):
        ts = slice(b * TB, (b + 1) * TB)
        ph.prologue(b, xsrc_view(xin, xres, b, from_input), None if first else red, False,
                    None if first else _blk(xres, b), ready=ready)
        P.dma("sp", pos_sb[:, :], pos[:, ts], writes=["pos"])
        emit_rope_tables(B, pos_sb[:, :], "pos", rc, TB, cos2[:, :], sin2[:, :], "tables")
        hk = lambda kk: h_sb[:, kk, :]
        hkey = lambda kk: ("h", kk)
        for m in range(4):
            B.group(wplan["dq"][m], hk, hkey, lambda ps, pkey, m=m: P.add(
                "act", lambda e: e.copy(out=cqp[:, m, :], in_=ps[:, :]), reads=[pkey], writes=[("cqp", m)]))
        for m in range(4):
            B.group(wplan["dkv"][m], hk, hkey, lambda ps, pkey, m=m: P.add(
                "act", lambda e: e.copy(out=ckvp[:, m, :], in_=ps[:, :]), reads=[pkey], writes=[("ckvp", m)]))
        pair_group(wplan["kr"][0], wplan["kr_sw"][0], hk, hkey, st_kr[:, :], "st_kr")
        P.dma("sp", kr_o[:, ts], st_kr[:, :], reads=["st_kr"])
        B.norm_stats(cqp, [("cqp", c) for c in range(4)], cqn, [("cqn", c) for c in range(4)], 4, 512)
        B.norm_apply(cqn, [("cqn", c) for c in range(4)], cqp, [("cqp", c) for c in range(4)],
                     lambda c: gq_sb[:, c:c + 1], "gq", 4)
        B.norm_stats(ckvp, [("ckvp", c) for c in range(4)], ckvn, [("ckvn", c) for c in range(4)], 4, 512)
        B.norm_apply(ckvn, [("ckvn", c) for c in range(4)], ckvp, [("ckvp", c) for c in range(4)],
                     lambda c: gq_sb[:, 4 + c:5 + c], "gq", 4)
        qk = lambda kk: cqn[:, kk, :]
        qkey = lambda kk: ("cqn", kk)
        kk_ = lambda kk: ckvn[:, kk, :]
        kkey = lambda kk: ("ckvn", kk)
        for h in range(NHC):
            B.group(wplan["uq_n"][h], qk, qkey, lambda ps, pkey, h=h: P.add(
                "act", lambda e: e.copy(out=st_qn[:, h, :], in_=ps[:, :]), reads=[pkey], writes=[("st_qn", h)]))
        P.dma("sp", qn_v[:, :, ts], st_qn[:, :, :], reads=[("st_qn", h) for h in range(NHC)])
        for h in range(NHC):
            pair_group(wplan["uq_r"][h], wplan["uq_rs"][h], qk, qkey, st_qr[:, h, :], ("st_qr", h))
        P.dma("sp", qr_v[:, :, ts], st_qr[:, :, :], reads=[("st_qr", h) for h in range(NHC)])
        for h in range(NHC):
            B.group(wplan["uk"][h], kk_, kkey, lambda ps, pkey, h=h: P.add(
                "act", lambda e: e.copy(out=st_kn[:, h, :], in_=ps[:, :]), reads=[pkey], writes=[("st_kn", h)]))
        P.dma("sp", kn_v[:, :, ts], st_kn[:, :, :], reads=[("st_kn", h) for h in range(NHC)])
        wap, wkey = B.ring.fetch(wplan["uv"])
        for tt in range(4):
            ps, pkey = B.next_ps()
            for k in range(4):
                P.add("pe", lambda e, ps=ps, k=k, tt=tt: e.matmul(
                    ps[:, :], ckvn[:, k, tt * 128:(tt + 1) * 128], wap[:, k * 512:(k + 1) * 512],
                    start=(k == 0), stop=(k == 3)), reads=[wkey, ("ckvn", k)], writes=[pkey])
            P.add("act", lambda e, ps=ps, tt=tt: e.copy(out=st_v[:, tt, :], in_=ps[:, :]),
                  reads=[pkey], writes=[("st_v", tt)])
        for h in range(NHC):
            P.dma("sp", v_o[h, :, b * 4:(b + 1) * 4, :], st_v[:, :, h * 128:(h + 1) * 128],
                  reads=[("st_v", tt) for tt in range(4)])
    for b in range(NB):
        body(b)
    P.emit()


def prep_mla_core(r, w_dq, w_dkv, w_uq, w_ukv, w_o):
    ws = WStream()
    plan = {}
    plan["dq"] = ws.add_matrix(w_dq)
    plan["dkv"] = ws.add_matrix(w_dkv[:, :512])
    kr = w_dkv[:, 512:576]
    plan["kr"] = ws.add_matrix(kr, mw=64)
    plan["kr_sw"] = ws.add_matrix(np.concatenate([kr[:, 32:], kr[:, :32]], axis=1), mw=64)
    uq = w_uq.reshape(512, HEADS, 192)[:, r * NHC:(r + 1) * NHC, :]
    plan["uq_n"] = ws.add_matrix(np.ascontiguousarray(uq[:, :, :128]).reshape(512, NHC * 128))
    plan["uq_r"] = []
    plan["uq_rs"] = []
    for h in range(NHC):
        rr = uq[:, h, 128:192]
        plan["uq_r"].append(ws.add_matrix(rr, mw=64)[0])
        plan["uq_rs"].append(ws.add_matrix(np.concatenate([rr[:, 32:], rr[:, :32]], axis=1), mw=64)[0])
    ukv = w_ukv.reshape(512, HEADS, 256)[:, r * NHC:(r + 1) * NHC, :]
    plan["uk"] = ws.add_matrix(np.ascontiguousarray(ukv[:, :, :128]).reshape(512, NHC * 128))
    wv = np.ascontiguousarray(ukv[:, :, 128:]).reshape(512, NHC * 128)
    plan["uv"] = ws.add_raw(wv.reshape(4, 128, 512).transpose(1, 0, 2).reshape(128, 2048))
    wso = WStream()
    plano = {"wo": wso.add_matrix(np.ascontiguousarray(w_o[r * NHC * 128:(r + 1) * NHC * 128, :]))}
    return ws.array(), ws.meta, plan, wso.array(), wso.meta, plano


def phase_lru(nc, S, xin, xres, part, red, from_input, wdram, wmeta, wplan, gdram, wgd, vecd, ready, redo):
    NB = S // TB
    ph = Phase(nc, wdram, wmeta, gdram, nslots=4)
    P, B = ph.P, ph.B
    h_sb = ph.h_sb
    HJ = 4
    wg_sb = P.sb("wg_sb", [128, 2, LJ, 2, CH], BF16)
    vec = P.sb("vec_sb", [128, LJ, 10], F32)
    cc = P.sb("cc", [128, 4, LJ], F32)
    yl = P.sb("yl", [128, HJ, TB], F32)
    uxb = P.sb("uxb", [128, HJ, TB + 4], F32)
    u_sb = P.sb("u_sb", [128, HJ, TB], F32)
    u_bf = P.sb("u_bf", [128, HJ, TB], BF16)
    r_sb = P.sb("r_sb", [128, HJ, TB], F32)
    ig_sb = P.sb("ig_sb", [128, HJ, TB], F32)
    a_sb = P.sb("a_sb", [128, HJ, TB], F32)
    m_sb = P.sb("m_sb", [128, HJ, TB], F32)
    hs_sb = P.sb("hs_sb", [128, HJ, TB], F32)
    ost = P.sb("ost", [128, LJ, TB], BF16)
    pst = P.sb("pst", [128, 2, TB], F32)
    state = P.sb("state", [128, LJ], F32)
    halo = P.sb("halo", [128, LJ, 4], F32)
    P.dma("sp", vec[0:CH, :, :], vecd[:, :, :], writes=["vec"])
    P.dma("pool", wg_sb[0:CH, :, :, :, :].rearrange("p a b c d -> p (a b c d)"), wgd[:, :], writes=["wg"])
    e_ = cc[0:CH, 0, :]
    t_ = cc[0:CH, 1, :]
    P.add("act", lambda e: e.activation(out=e_, in_=vec[0:CH, :, 5], func=AF.Exp, scale=-1.0), reads=["vec"],
          writes=["cc"])
    coef = [1.0 / 5, -1.0 / 4, 1.0 / 3, -1.0 / 2, 1.0]
    P.add("dve", lambda e: e.tensor_scalar(out=t_, in0=e_, scalar1=-1.0 / 6, scalar2=coef[0], op0=ALU.mult,
                                           op1=ALU.add), reads=["cc"], writes=["cc"])
    for cf in coef[1:]:
        P.add("dve", lambda e: e.tensor_tensor(out=t_, in0=t_, in1=e_, op=ALU.mult), reads=["cc"], writes=["cc"])
        P.add("dve", lambda e, cf=cf: e.tensor_scalar(out=t_, in0=t_, scalar1=cf, scalar2=None, op0=ALU.add),
              reads=["cc"], writes=["cc"])
    P.add("dve", lambda e: e.tensor_tensor(out=t_, in0=t_, in1=e_, op=ALU.mult), reads=["cc"], writes=["cc"])
    P.add("dve", lambda e: e.tensor_scalar(out=cc[0:CH, 2, :], in0=t_, scalar1=-8.0, scalar2=None, op0=ALU.mult),
          reads=["cc"], writes=["cc"])
    P.add("dve", lambda e: e.tensor_scalar(out=cc[0:CH, 3, :], in0=t_, scalar1=-16.0, scalar2=None, op0=ALU.mult),
          reads=["cc"], writes=["cc"])
    P.add("dve", lambda e: e.memset(state[:, :], 0.0), writes=[("state", j) for j in range(LJ)])
    P.add("dve", lambda e: e.memset(halo[:, :, :], 0.0), writes=[("halo", j) for j in range(LJ)])
    hk = lambda kk: h_sb[:, kk, :]
    hkey = lambda kk: ("h", kk)

    def half(b, hf):
        J0 = hf * HJ
        for j in range(HJ):
            J = J0 + j
            B.group(wplan["wy"][J], hk, hkey, lambda ps, pkey, j=j, J=J: P.add(
                "act", lambda e: e.activation(out=yl[0:CH, j, :], in_=ps[0:CH, :], func=AF.Gelu,
                                              bias=vec[0:CH, J, 0:1]),
                reads=[pkey, "vec"], writes=[("yl", j)]), mw=CH)
        for j in range(HJ):
            J = J0 + j
            B.group(wplan["wx"][J], hk, hkey, lambda ps, pkey, j=j, J=J: P.add(
                "act", lambda e: e.activation(out=uxb[0:CH, j, 3:3 + TB], in_=ps[0:CH, :], func=AF.Identity,
                                              bias=vec[0:CH, J, 1:2]),
                reads=[pkey, "vec"], writes=[("ux", j)]), mw=CH)
        for j in range(HJ):
            J = J0 + j
            P.add("dve", lambda e, j=j, J=J: e.tensor_copy(out=uxb[0:CH, j, 0:3], in_=halo[0:CH, J, 0:3]),
                  reads=[("halo", J), ("ux", j)], writes=[("ux", j)])
            P.add("dve", lambda e, j=j, J=J: e.tensor_scalar(out=u_sb[0:CH, j, :], in0=uxb[0:CH, j, 0:TB],
                                                             scalar1=vec[0:CH, J, 6:7], scalar2=vec[0:CH, J, 2:3],
                                                             op0=ALU.mult, op1=ALU.add),
                  reads=[("ux", j), "vec"], writes=[("u", j)])
            for i in range(1, 4):
                P.add("dve", lambda e, j=j, J=J, i=i: e.scalar_tensor_tensor(
                    out=u_sb[0:CH, j, :], in0=uxb[0:CH, j, i:i + TB], scalar=vec[0:CH, J, 6 + i:7 + i],
                    in1=u_sb[0:CH, j, :], op0=ALU.mult, op1=ALU.add),
                    reads=[("ux", j), "vec", ("u", j)], writes=[("u", j)])
            P.add("dve", lambda e, j=j, J=J: e.tensor_copy(out=halo[0:CH, J, 0:3], in_=uxb[0:CH, j, TB:TB + 3]),
                  reads=[("ux", j)], writes=[("halo", J)])
            P.add("act", lambda e, j=j: e.copy(out=u_bf[0:CH, j, :], in_=u_sb[0:CH, j, :]),
                  reads=[("u", j)], writes=[("ubf", j)])
        for gi, dst, bidx in ((0, r_sb, 3), (1, ig_sb, 4)):
            for j in range(HJ):
                J = J0 + j
                ps, pkey = B.next_ps()
                for i in range(2):
                    P.add("pe", lambda e, ps=ps, gi=gi, j=j, J=J, i=i: e.matmul(
                        ps[0:CH, :], wg_sb[0:CH, gi, J, i, :], u_bf[0:CH, 2 * (j // 2) + i, :], start=(i == 0),
                        stop=(i == 1)), reads=["wg", ("ubf", 2 * (j // 2) + i)], writes=[pkey])
                P.add("act", lambda e, ps=ps, j=j, J=J, dst=dst, bidx=bidx: e.activation(
                    out=dst[0:CH, j, :], in_=ps[0:CH, :], func=AF.Sigmoid, bias=vec[0:CH, J, bidx:bidx + 1]),
                    reads=[pkey, "vec"], writes=[("gate", gi, j)])
        for j in range(HJ):
            J = J0 + j
            P.add("act", lambda e, j=j, J=J: e.activation(out=a_sb[0:CH, j, :], in_=r_sb[0:CH, j, :], func=AF.Exp,
                                                          scale=cc[0:CH, 2, J:J + 1]),
                  reads=[("gate", 0, j), "cc"], writes=[("a", j)])
            P.add("act", lambda e, j=j, J=J: e.activation(out=m_sb[0:CH, j, :], in_=r_sb[0:CH, j, :], func=AF.Exp,
                                                          scale=cc[0:CH, 3, J:J + 1]),
                  reads=[("gate", 0, j), "cc"], writes=[("m", j)])
        for j in range(HJ):
            P.add("dve", lambda e, j=j: e.tensor_scalar(out=m_sb[0:CH, j, :], in0=m_sb[0:CH, j, :], scalar1=-1.0,
                                                        scalar2=1.0, op0=ALU.mult, op1=ALU.add),
                  reads=[("m", j)], writes=[("m", j)])
        for j in range(HJ):
            P.add("act", lambda e, j=j: e.activation(out=m_sb[0:CH, j, :], in_=m_sb[0:CH, j, :], func=AF.Sqrt),
                  reads=[("m", j)], writes=[("m", j)])
        for j in range(HJ):
            J = J0 + j
            P.add("dve", lambda e, j=j: e.tensor_tensor(out=ig_sb[0:CH, j, :], in0=ig_sb[0:CH, j, :],
                                                        in1=u_sb[0:CH, j, :], op=ALU.mult),
                  reads=[("gate", 1, j), ("u", j)], writes=[("gate", 1, j)])
            P.add("dve", lambda e, j=j: e.tensor_tensor(out=ig_sb[0:CH, j, :], in0=ig_sb[0:CH, j, :],
                                                        in1=m_sb[0:CH, j, :], op=ALU.mult),
                  reads=[("gate", 1, j), ("m", j)], writes=[("gate", 1, j)])
            P.add("dve", lambda e, j=j, J=J: e.tensor_tensor_scan(
                out=hs_sb[0:CH, j, :], data0=a_sb[0:CH, j, :], data1=ig_sb[0:CH, j, :],
                initial=state[0:CH, J:J + 1], op0=ALU.mult, op1=ALU.add),
                reads=[("a", j), ("gate", 1, j), ("state", J)], writes=[("hs", j)])
            P.add("dve", lambda e, j=j, J=J: e.tensor_copy(out=state[0:CH, J:J + 1], in_=hs_sb[0:CH, j, TB - 1:TB]),
                  reads=[("hs", j)], writes=[("state", J)])
            P.add("dve", lambda e, j=j, J=J: e.tensor_tensor(out=ost[0:CH, J, :], in0=hs_sb[0:CH, j, :],
                                                             in1=yl[0:CH, j, :], op=ALU.mult),
                  reads=[("hs", j), ("yl", j)], writes=[("ost", J)])

    def body(b):
        ph.prologue(b, xsrc_view(xin, xres, b, from_input), red, False, _blk(xres, b), ready=ready)
        half(b, 0)
        half(b, 1)
        for m in range(DC):
            ps, pkey = B.next_ps()
            wap, wkey = B.ring.fetch(wplan["wout"][m])
            for J in range(LJ):
                P.add("pe", lambda e, ps=ps, J=J, wap=wap: e.matmul(
                    ps[:, :], wap[0:CH, J * 128:(J + 1) * 128], ost[0:CH, J, :], start=(J == 0), stop=(J == LJ - 1)),
                    reads=[wkey, ("ost", J)], writes=[pkey])
            ph.store_partial(ps, pkey, part, b, m, pst, m)
        schedule_cc(P, part, redo, b, ccops)
    ccops = {}
    for b in range(NB):
        body(b)
    flush_cc(P)
    P.emit()
    return {b: op.ccval for b, op in ccops.items()}


def prep_lru_core(r, w_y, b_y, w_x, b_x, conv_w, conv_b, w_ga, b_ga, w_gi, b_gi, lam, w_out):
    c0 = r * LJ * CH
    c1 = c0 + LJ * CH
    ws = WStream()
    plan = {"wy": ws.add_matrix(np.ascontiguousarray(w_y[:, c0:c1]), mw=CH),
            "wx": ws.add_matrix(np.ascontiguousarray(w_x[:, c0:c1]), mw=CH), "wout": []}
    wo = w_out[c0:c1, :].reshape(LJ, CH, DC, 128)
    for m in range(DC):
        t = np.zeros((128, LJ, 128), np.float32)
        t[:CH] = wo[:, :, m, :].transpose(1, 0, 2)
        plan["wout"].append(ws.add_raw(t.reshape(128, LJ * 128)))
    wg = np.zeros((CH, 2, LJ, 2, CH), np.float32)
    for gi, w in enumerate((w_ga, w_gi)):
        for J in range(LJ):
            n = (LJ // 2) * r + J // 2
            for i in range(2):
                wg[:, gi, J, i, :] = w[n, i * CH:(i + 1) * CH, (J % 2) * CH:(J % 2 + 1) * CH]
    vec = np.zeros((CH, LJ, 10), np.float32)
    for idx, vv in enumerate((b_y, b_x, conv_b, b_ga, b_gi, lam)):
        vec[:, :, idx] = vv[c0:c1].reshape(LJ, CH).T
    for i in range(4):
        vec[:, :, 6 + i] = conv_w[i, c0:c1].reshape(LJ, CH).T
    return ws.array(), ws.meta, plan, np.ascontiguousarray(wg.reshape(CH, -1)), vec


def prep_mlp_core(r, w1, w2):
    ws = WStream()
    n = DFF // TPG
    plan = {"w1": ws.add_matrix(np.ascontiguousarray(w1[:, r * n:(r + 1) * n])),
            "w2": ws.add_matrix(np.ascontiguousarray(w2[r * n:(r + 1) * n, :]))}
    return ws.array(), ws.meta, plan


def g3(gpost, gpre, bias):
    z = np.zeros((128, DC), np.float32)
    return np.ascontiguousarray(np.stack([gvec(gpost) if gpost is not None else z,
                                          gvec(gpre) if gpre is not None else z,
                                          gvec(bias) if bias is not None else z], axis=1))


def build_fused(S, depth, metas):
    nc = bass.Bass("TRN2", target_bir_lowering=False)
    NB = S // TB

    def din(name, shape, dt=F32):
        return nc.dram_tensor(name, list(shape), dt, kind="ExternalInput").ap()

    def wsz(meta):
        return meta[-1][0] + meta[-1][1]
    xin = din("xT", [D, S])
    pos = din("pos", [64, S], I32)
    rcd = din("rc", [64, 2])
    msk = din("masks", [128, 4, 512], BF16)
    outT = nc.dram_tensor("outT", [D, S], F32, kind="ExternalOutput").ap()
    xres = nc.dram_tensor("xres", [NB, D, TB], F32)
    parts = [nc.dram_tensor(f"part{i}", [NB, D, TB], F32) for i in range(2)]
    reds = [nc.dram_tensor(f"red{i}", [NB, D, TB], F32) for i in range(2)]
    qn_s = nc.dram_tensor("qn_s", [NHC, 128, S], BF16).ap()
    qr_s = nc.dram_tensor("qr_s", [NHC, 64, S], BF16).ap()
    kn_s = nc.dram_tensor("kn_s", [NHC, 128, S], BF16).ap()
    kr_s = nc.dram_tensor("kr_s", [64, S], BF16).ap()
    v_s = nc.dram_tensor("v_s", [NHC, 128, S // 128, 128], BF16).ap()
    o_s = nc.dram_tensor("o_s", [NHC, 128, S], BF16).ap()
    first = True
    from_input = True
    k = 0
    ready = None
    prev_bias = False
    for i in range(depth):
        j = i // 2
        if i % 2 == 0:
            wm, wp, wmo, wpo = metas[f"mla{j}"]
            phase_mla_proj(nc, S, xin, xres, reds[(k - 1) % 2], from_input, first, pos, rcd,
                           din(f"w_mla{j}", [128, wsz(wm)]), wm, wp, din(f"g_mix{i}", [128, 3, DC]),
                           din(f"gq{j}", [128, 8]), qn_s, qr_s, kn_s, kr_s, v_s, ready)
            emit_attn(nc, S, NHC, qn_s, qr_s, kn_s, kr_s, v_s, msk, o_s)
            ready = phase_wo(nc, S, o_s, parts[k % 2], din(f"w_o{j}", [128, wsz(wmo)]), wmo, wpo, reds[k % 2])
            prev_bias = False
        else:
            wm, wp = metas[f"lru{j}"]
            ready = phase_lru(nc, S, xin, xres, parts[k % 2], reds[(k - 1) % 2], from_input,
                              din(f"w_lru{j}", [128, wsz(wm)]), wm, wp, din(f"g_mix{i}", [128, 3, DC]),
                              din(f"wg{j}", [CH, 2 * LJ * 2 * CH]), din(f"vec{j}", [CH, LJ, 10]), ready, reds[k % 2])
            prev_bias = True
        k += 1
        wm, wp = metas[f"mlp{i}"]
        ready = phase_mlp(nc, S, xin, xres, parts[k % 2], reds[(k - 1) % 2], from_input, prev_bias,
                          din(f"w_mlp{i}", [128, wsz(wm)]), wm, wp, din(f"g_mlp{i}", [128, 3, DC]), ready,
                          reds[k % 2])
        k += 1
        first = False
        from_input = False
    phase_final(nc, S, xres, reds[(k - 1) % 2], outT, din("g_fin", [128, 3, DC]), ready)
    return nc


def kernel(x, positions, mix_pre_g, mix_post_g, mlp_pre_g, mlp_post_g,
           mla_w_dq, mla_g_q, mla_w_uq, mla_w_dkv, mla_g_kv, mla_w_ukv, mla_w_o,
           lru_w_y, lru_b_y, lru_w_x, lru_b_x, lru_conv_w, lru_conv_b,
           lru_w_ga, lru_b_ga, lru_w_gi, lru_b_gi, lru_lam, lru_w_out, lru_b_out,
           mlp_w1, mlp_w2):
    f32 = lambda a: np.asarray(a, np.float32)
    x = f32(x)
    positions = np.asarray(positions, np.int32)
    NBATCH, S, _ = x.shape
    assert NBATCH * TPG == NCORES
    depth = mix_pre_g.shape[0]
    rc = rope_consts()
    masks = attn_masks()
    maps = [dict() for _ in range(NCORES)]
    metas = {}
    xTb = [np.ascontiguousarray(x[b].T) for b in range(NBATCH)]
    posb = [np.ascontiguousarray(np.broadcast_to(positions[b][None, :], (64, S))) for b in range(NBATCH)]
    for c in range(NCORES):
        b, r = divmod(c, TPG)
        m = maps[c]
        m["xT"] = xTb[b]
        m["pos"] = posb[b]
        m["rc"] = rc
        m["masks"] = masks
        for i in range(depth):
            j = i // 2
            if i % 2 == 0:
                wa, wm, wp, wao, wmo, wpo = prep_mla_core(r, f32(mla_w_dq[j]), f32(mla_w_dkv[j]), f32(mla_w_uq[j]),
                                                          f32(mla_w_ukv[j]), f32(mla_w_o[j]))
                metas[f"mla{j}"] = (wm, wp, wmo, wpo)
                m[f"w_mla{j}"] = wa
                m[f"w_o{j}"] = wao
                m[f"gq{j}"] = np.ascontiguousarray(np.concatenate([gvec(mla_g_q[j]), gvec(mla_g_kv[j])], axis=1))
                m[f"g_mix{i}"] = g3(mlp_post_g[i - 1] if i > 0 else None, mix_pre_g[i], None)
                m[f"g_mlp{i}"] = g3(mix_post_g[i], mlp_pre_g[i], None)
            else:
                wa, wm, wp, wg, vec = prep_lru_core(
                    r, f32(lru_w_y[j]), f32(lru_b_y[j]), f32(lru_w_x[j]), f32(lru_b_x[j]), f32(lru_conv_w[j]),
                    f32(lru_conv_b[j]), f32(lru_w_ga[j]), f32(lru_b_ga[j]), f32(lru_w_gi[j]), f32(lru_b_gi[j]),
                    f32(lru_lam[j]), f32(lru_w_out[j]))
                metas[f"lru{j}"] = (wm, wp)
                m[f"w_lru{j}"] = wa
                m[f"wg{j}"] = wg
                m[f"vec{j}"] = vec
                m[f"g_mix{i}"] = g3(mlp_post_g[i - 1], mix_pre_g[i], None)
                m[f"g_mlp{i}"] = g3(mix_post_g[i], mlp_pre_g[i], lru_b_out[j])
            wa, wm, wp = prep_mlp_core(r, f32(mlp_w1[i]), f32(mlp_w2[i]))
            metas[f"mlp{i}"] = (wm, wp)
            m[f"w_mlp{i}"] = wa
        m["g_fin"] = g3(mlp_post_g[depth - 1], None, None)
    nc = build_fused(S, depth, metas)
    res = run_bass_kernel_spmd(nc, maps, core_ids=list(range(NCORES))).results
    out = np.stack([np.asarray(res[b * TPG]["outT"]).T for b in range(NBATCH)], axis=0)
    return np.ascontiguousarray(out.astype(np.float32))
```

```python
import contextlib
import numpy as np
import ml_dtypes
import concourse.bass as bass
import concourse.mybir as mybir
from concourse.bass_utils import run_bass_kernel_spmd

F32 = mybir.dt.float32
BF16 = mybir.dt.bfloat16
I32 = mybir.dt.int32
ALU = mybir.AluOpType
AF = mybir.ActivationFunctionType
NPBF16 = ml_dtypes.bfloat16

D = 2048
DC = D // 128
DFF = 8192
FC = DFF // 128
EPS = 1e-6
TB = 512
NCORES = 8
HEADS = 16
DRNN = 2688
RC = DRNN // 128


class Op:
    __slots__ = ("eng", "fn", "deps", "idx", "signal", "cnt", "is_dma", "dsem", "dval", "gid", "ext", "is_cc", "ccval")


class Prog:
    ENGS = ["pe", "act", "dve", "pool", "sp"]
    EPOCH = 12000
    NDMA_SEM = 8
    CCWIN = 33

    _uid = [0]
    G = {}

    def __init__(self, nc):
        self.nc = nc
        Prog._uid[0] += 1
        self.pfx = f"p{Prog._uid[0]}_"
        self.ops = {e: [] for e in self.ENGS}
        self.last_w = {}
        self.readers = {}
        self.stack = contextlib.ExitStack()
        self.ngid = 0

    def sb(self, name, shape, dt):
        return self.stack.enter_context(self.nc.sbuf_tensor(self.pfx + name, list(shape), dt))

    def ps(self, name, shape=(128, 512), dt=F32):
        return self.stack.enter_context(self.nc.psum_tensor(self.pfx + name, list(shape), dt))

    def add(self, eng, fn, reads=(), writes=(), dma=False, ext=None, cc=False):
        op = Op()
        op.ext = ext
        op.is_cc = cc
        op.ccval = None
        op.eng = eng
        op.fn = fn
        op.is_dma = dma
        op.signal = False
        op.gid = self.ngid
        self.ngid += 1
        deps = {}
        for k in reads:
            w = self.last_w.get(k)
            if w is not None:
                deps[w.gid] = w
        for k in writes:
            w = self.last_w.get(k)
            if w is not None:
                deps[w.gid] = w
            for r in self.readers.get(k, ()):
                deps[r.gid] = r
        for k in reads:
            self.readers.setdefault(k, []).append(op)
        for k in writes:
            self.last_w[k] = op
            self.readers[k] = []
        op.deps = list(deps.values())
        op.idx = len(self.ops[eng])
        self.ops[eng].append(op)
        return op

    def dma(self, eng, out, in_, reads=(), writes=(), ext=None):
        return self.add(eng, lambda e: e.dma_start(out=out, in_=in_), reads, writes, dma=True, ext=ext)

    def collective(self, in_ap, out_ap, reads=()):
        return self.add("pool", lambda e: e.collective_compute("AllReduce", ALU.add, replica_groups=GROUPS,
                                                               ins=[in_ap], outs=[out_ap]), reads, (), cc=True)

    def emit(self):
        nc = self.nc
        for e in self.ENGS:
            for op in self.ops[e]:
                best = {}
                dd = []
                for d in op.deps:
                    if d.is_dma:
                        dd.append(d)
                        continue
                    if d.eng == op.eng and op.eng == "pe" and not op.is_dma:
                        continue
                    b = best.get(d.eng)
                    if b is None or d.idx > b.idx:
                        best[d.eng] = d
                for d in best.values():
                    d.signal = True
                    dd.append(d)
                op.deps = dd
        G = Prog.G
        if G.get("nc") is not nc:
            G.clear()
            G["nc"] = nc
            G["sems"] = {e: [] for e in self.ENGS}
            G["cnt"] = {e: 0 for e in self.ENGS}
            G["dsets"] = {e: [] for e in self.ENGS}
            G["dvals"] = {e: None for e in self.ENGS}
            G["dj"] = {e: 0 for e in self.ENGS}
            G["nalloc"] = 0

        def new_sem(name):
            G["nalloc"] += 1
            return nc.alloc_semaphore(name=f"g{G['nalloc']}_{name}")
        base = dict(G["cnt"])
        if "cc" not in G:
            G["cc"] = nc.alloc_semaphore(name="cc_global")
            G["ccv"] = 0
        for op in self.ops["pool"]:
            if op.is_cc:
                G["ccv"] += 1
                op.ccval = G["ccv"]
        ccsem = G["cc"]
        for e in self.ENGS:
            c = G["cnt"][e]
            for op in self.ops[e]:
                if op.signal and not op.is_dma:
                    c += 1
                    op.cnt = c
            G["cnt"][e] = c
            need = (c + self.EPOCH - 1) // self.EPOCH
            while len(G["sems"][e]) < max(1, need):
                G["sems"][e].append(new_sem(f"s_{e}"))
        sems = G["sems"]
        DLIM = 30000
        for e in self.ENGS:
            for op in self.ops[e]:
                if op.is_dma:
                    if G["dvals"][e] is None or max(G["dvals"][e]) + 16 > DLIM:
                        G["dsets"][e].append([new_sem(f"d_{e}") for _ in range(self.NDMA_SEM)])
                        G["dvals"][e] = [0] * self.NDMA_SEM
                    sidx = G["dj"][e] % self.NDMA_SEM
                    G["dj"][e] += 1
                    G["dvals"][e][sidx] += 16
                    op.dsem = G["dsets"][e][-1][sidx]
                    op.dval = G["dvals"][e][sidx]
        EP = self.EPOCH
        self.stats = {}

        def run(eng_name, eng):
            known = dict(base)
            known_dma = set()
            nw = 0
            for op in self.ops[eng_name]:
                for d in op.deps:
                    if d.is_dma:
                        if d.gid in known_dma:
                            continue
                        eng.wait_ge(d.dsem, d.dval)
                        known_dma.add(d.gid)
                        nw += 1
                    else:
                        if d.cnt <= known[d.eng]:
                            continue
                        ep = (d.cnt - 1) // EP
                        eng.wait_ge(sems[d.eng][ep], d.cnt - ep * EP)
                        known[d.eng] = d.cnt
                        nw += 1
                if op.ext is not None:
                    eng.wait_ge(ccsem, op.ext)
                if op.is_cc:
                    if op.ccval > self.CCWIN:
                        eng.wait_ge(ccsem, op.ccval - self.CCWIN)
                    op.fn(eng).then_inc(ccsem)
                elif op.is_dma:
                    if op.dval > 16:
                        eng.wait_ge(op.dsem, op.dval - 16)
                    op.fn(eng).then_inc(op.dsem, 16)
                else:
                    ins = op.fn(eng)
                    if op.signal:
                        ep = (op.cnt - 1) // EP
                        ins.then_inc(sems[eng_name][ep], 1)
            last = {}
            for op in self.ops[eng_name]:
                if op.is_dma:
                    last[id(op.dsem)] = op
            for op in last.values():
                eng.wait_ge(op.dsem, op.dval)
            self.stats[eng_name] = (len(self.ops[eng_name]), nw)

        with nc.Block() as block:
            @block.tensor
            def _(pe):
                run("pe", pe)

            @block.scalar
            def _(act):
                run("act", act)

            @block.vector
            def _(dve):
                run("dve", dve)

            @block.gpsimd
            def _(pool):
                run("pool", pool)

            @block.sync
            def _(sp):
                run("sp", sp)
        self.stack.close()


class WStream:
    def __init__(self):
        self.tiles = []
        self.meta = []
        self.off = 0

    def add_matrix(self, w, mw=128):
        K, M = w.shape
        kc = K // 128
        kmax = 2048 // mw
        out = []
        wt = w.reshape(kc, 128, M // mw, mw)
        for m in range(M // mw):
            lst = []
            for k0 in range(0, kc, kmax):
                nk = min(kmax, kc - k0)
                t = np.ascontiguousarray(wt[k0:k0 + nk, :, m, :].transpose(1, 0, 2))
                lst.append((len(self.tiles), nk))
                self.tiles.append(t.reshape(128, nk * mw))
                self.meta.append((self.off, nk * mw))
                self.off += nk * mw
            out.append(lst)
        return out

    def add_raw(self, t):
        n = t.shape[1]
        tid = len(self.tiles)
        self.tiles.append(np.ascontiguousarray(t))
        self.meta.append((self.off, n))
        self.off += n
        return tid

    def array(self):
        return np.ascontiguousarray(np.concatenate(self.tiles, axis=1).astype(np.float32))


class WRing:
    CONV = 4096

    def __init__(self, P, wdram, meta, nslots=6):
        self.P = P
        self.meta = meta
        self.nslots = nslots
        self.buf = P.sb("wring", [128, nslots, 16 * 128], BF16)
        self.next = 0
        total = meta[-1][0] + meta[-1][1]
        self.wbf = P.nc.dram_tensor(P.pfx + "wbf", [128, total], BF16).ap()
        for ci, c0 in enumerate(range(0, total, self.CONV)):
            c1 = min(total, c0 + self.CONV)
            P.dma("pool", self.wbf[:, c0:c1], wdram[:, c0:c1], writes=[("wconv", ci)])

    def fetch(self, tid):
        slot = self.next % self.nslots
        self.next += 1
        off, ne = self.meta[tid]
        key = ("w", slot)
        rk = [("wconv", ci) for ci in range(off // self.CONV, (off + ne - 1) // self.CONV + 1)]
        self.P.dma("pool", self.buf[:, slot, 0:ne], self.wbf[:, off:off + ne], reads=rk, writes=[key])
        pend = getattr(self.P, "pending_cc", None)
        if pend:
            for it in pend:
                it[0] -= 1
            while pend and pend[0][0] <= 0:
                pend.pop(0)[1]()
        return self.buf[:, slot, :], key


def gvec(g):
    g = np.asarray(g, np.float32)
    return np.ascontiguousarray(g.reshape(-1, 128).T)


class Ctx:
    pass


def emit_rmsnorm_stats(P, C, src, src_keys, sq, sq_keys, nch, ps_ss, ps_key, rstd, rstd_key, tmp, tmp_key,
                       dim, ntok=TB, sq_eng="act"):
    for c in range(nch):
        if sq_eng == "act":
            P.add("act", lambda e, c=c: e.activation(out=sq[:, c, :ntok], in_=src[:, c, :ntok], func=AF.Square),
                  reads=[src_keys[c]], writes=[sq_keys[c]])
        else:
            P.add(sq_eng, lambda e, c=c: e.tensor_tensor(out=sq[:, c, :ntok], in0=src[:, c, :ntok],
                                                         in1=src[:, c, :ntok], op=ALU.mult),
                  reads=[src_keys[c]], writes=[sq_keys[c]])
    for c in range(nch):
        P.add("pe", lambda e, c=c: e.matmul(ps_ss[:, :ntok], C.ones[:, :], sq[:, c, :ntok], start=(c == 0),
                                            stop=(c == nch - 1)),
              reads=[sq_keys[c], "ones"], writes=[ps_key])
    P.add("act", lambda e: e.activation(out=tmp[:, :ntok], in_=ps_ss[:, :ntok], func=AF.Sqrt, scale=1.0 / dim,
                                        bias=C.eps[:, 0:1]),
          reads=[ps_key, "eps"], writes=[tmp_key])
    P.add("dve", lambda e: e.reciprocal(out=rstd[:, :ntok], in_=tmp[:, :ntok]),
          reads=[tmp_key], writes=[rstd_key])


def build_post_mlp(T, KIN, has_bias, wmeta, wplan):
    nc = bass.Bass("TRN2", target_bir_lowering=False)
    KC = KIN // 128
    NB = T // TB
    xT = nc.dram_tensor("xT", [D, T], F32, kind="ExternalInput").ap()
    mT = nc.dram_tensor("mT", [KIN, T], BF16, kind="ExternalInput").ap()
    wdram = nc.dram_tensor("wstream", [128, wmeta[-1][0] + wmeta[-1][1]], F32, kind="ExternalInput").ap()
    gpost = nc.dram_tensor("gpost", [128, DC], F32, kind="ExternalInput").ap()
    gpre = nc.dram_tensor("gpre", [128, DC], F32, kind="ExternalInput").ap()
    gpost2 = nc.dram_tensor("gpost2", [128, DC], F32, kind="ExternalInput").ap()
    bout = nc.dram_tensor("bout", [128, DC], F32, kind="ExternalInput").ap()
    oT = nc.dram_tensor("oT", [D, T], F32, kind="ExternalOutput").ap()
    xTv = xT.rearrange("(c p) t -> p c t", p=128)
    oTv = oT.rearrange("(c p) t -> p c t", p=128)
    mTv = mT.rearrange("(c p) t -> p c t", p=128)

    P = Prog(nc)
    C = Ctx()
    C.ones = P.sb("ones", [128, 128], BF16)
    x_sb = P.sb("x_sb", [128, DC, TB], F32)
    y_sb = P.sb("y_sb", [128, DC, TB], F32)
    h_sb = P.sb("h_sb", [128, DC, TB], BF16)
    big = P.sb("big", [128, FC, TB], BF16)
    g_sb = P.sb("g_sb", [128, 4, DC], F32)
    rstd = P.sb("rstd", [128, TB], F32)
    tmp = P.sb("tmp", [128, TB], F32)
    tmp2 = P.sb("tmp2", [128, 2, TB], F32)
    NPS = 6
    pss = [P.ps(f"ps{i}") for i in range(NPS)]
    ps_ss = P.ps("ps_ss")
    ring = WRing(P, wdram, wmeta, nslots=6)

    C.eps = P.sb("eps", [128, 1], F32)
    P.add("dve", lambda e: e.memset(C.ones[:, :], 1.0), writes=["ones"])
    P.add("dve", lambda e: e.memset(C.eps[:, :], EPS), writes=["eps"])
    for i, g in enumerate([gpost, gpre, gpost2, bout]):
        P.dma("sp", g_sb[:, i, :], g[:, :], writes=[("g", i)])

    psi = [0]

    def next_ps():
        i = psi[0] % NPS
        psi[0] += 1
        return pss[i], ("ps", i)

    def matmul_group(wtiles, rhs_of_k, rhs_keys_of_k, evac):
        ps, pkey = next_ps()
        kbase = 0
        total = sum(nk for _, nk in wtiles)
        for tid, nk in wtiles:
            wap, wkey = ring.fetch(tid)
            for k in range(nk):
                kk = kbase + k
                P.add("pe", lambda e, wap=wap, k=k, kk=kk: e.matmul(
                    ps[:, :], wap[:, k * 128:(k + 1) * 128], rhs_of_k(kk), start=(kk == 0), stop=(kk == total - 1)),
                    reads=[wkey, rhs_keys_of_k(kk)], writes=[pkey])
            kbase += nk
        evac(ps, pkey)

    def post_norm_residual(gi):
        emit_rmsnorm_stats(P, C, y_sb, [("y", c) for c in range(DC)], h_sb, [("h", c) for c in range(DC)], DC,
                           ps_ss, "ps_ss", rstd, "rstd", tmp, "tmp", D)
        for c in range(DC):
            t2 = tmp2[:, c % 2, :]
            P.add("dve", lambda e, c=c, t2=t2: e.scalar_tensor_tensor(
                out=t2, in0=y_sb[:, c, :], scalar=g_sb[:, gi, c:c + 1], in1=rstd[:, :], op0=ALU.mult, op1=ALU.mult),
                reads=[("y", c), "rstd", ("g", gi)], writes=[("tmp2", c % 2)])
            P.add("dve", lambda e, c=c, t2=t2: e.tensor_tensor(out=x_sb[:, c, :], in0=x_sb[:, c, :], in1=t2,
                                                                 op=ALU.add),
                  reads=[("tmp2", c % 2), ("x", c)], writes=[("x", c)])

    for b in range(NB):
        ts = slice(b * TB, (b + 1) * TB)
        for c0 in range(0, DC, 4):
            P.dma("sp", x_sb[:, c0:c0 + 4, :], xTv[:, c0:c0 + 4, ts], writes=[("x", c) for c in range(c0, c0 + 4)])
        for c0 in range(0, KC, 8):
            c1 = min(KC, c0 + 8)
            P.dma("sp", big[:, c0:c1, :], mTv[:, c0:c1, ts], writes=[("big", c) for c in range(c0, c1)])
        for m in range(DC):
            def evac(ps, pkey, m=m):
                if has_bias:
                    P.add("act", lambda e: e.activation(out=y_sb[:, m, :], in_=ps[:, :], func=AF.Identity,
                                                        bias=g_sb[:, 3, m:m + 1]),
                          reads=[pkey, ("g", 3)], writes=[("y", m)])
                else:
                    P.add("act", lambda e: e.copy(out=y_sb[:, m, :], in_=ps[:, :]), reads=[pkey], writes=[("y", m)])
            matmul_group(wplan["wo"][m], lambda kk: big[:, kk, :], lambda kk: ("big", kk), evac)
        post_norm_residual(0)
        emit_rmsnorm_stats(P, C, x_sb, [("x", c) for c in range(DC)], h_sb, [("h", c) for c in range(DC)], DC,
                           ps_ss, "ps_ss", rstd, "rstd", tmp, "tmp", D)
        for c in range(DC):
            P.add("dve", lambda e, c=c: e.scalar_tensor_tensor(
                out=h_sb[:, c, :], in0=x_sb[:, c, :], scalar=g_sb[:, 1, c:c + 1], in1=rstd[:, :], op0=ALU.mult,
                op1=ALU.mult), reads=[("x", c), "rstd", ("g", 1)], writes=[("h", c)])
        for m in range(FC):
            def evac(ps, pkey, m=m):
                t2 = tmp2[:, m % 2, :]
                P.add("act", lambda e: e.activation(out=t2, in_=ps[:, :], func=AF.Relu), reads=[pkey],
                      writes=[("tmp2", m % 2)])
                P.add("dve", lambda e: e.tensor_tensor(out=big[:, m, :], in0=t2, in1=t2, op=ALU.mult),
                      reads=[("tmp2", m % 2)], writes=[("big", m)])
            matmul_group(wplan["w1"][m], lambda kk: h_sb[:, kk, :], lambda kk: ("h", kk), evac)
        for m in range(DC):
            def evac(ps, pkey, m=m):
                P.add("act", lambda e: e.copy(out=y_sb[:, m, :], in_=ps[:, :]), reads=[pkey], writes=[("y", m)])
            matmul_group(wplan["w2"][m], lambda kk: big[:, kk, :], lambda kk: ("big", kk), evac)
        post_norm_residual(2)
        for c0 in range(0, DC, 4):
            P.dma("sp", oTv[:, c0:c0 + 4, ts], x_sb[:, c0:c0 + 4, :], reads=[("x", c) for c in range(c0, c0 + 4)])
    P.emit()
    return nc


def prep_post_mlp_weights(w_o, w1, w2):
    ws = WStream()
    plan = {"wo": ws.add_matrix(w_o), "w1": ws.add_matrix(w1), "w2": ws.add_matrix(w2)}
    return ws.array(), ws.meta, plan


class Bld:
    def __init__(self, nc, wdram, wmeta, nps=6, nslots=6, with_ss=True):
        self.P = Prog(nc)
        P = self.P
        self.C = Ctx()
        self.C.ones = P.sb("ones", [128, 128], BF16)
        self.C.eps = P.sb("eps", [128, 1], F32)
        P.add("dve", lambda e: e.memset(self.C.ones[:, :], 1.0), writes=["ones"])
        P.add("dve", lambda e: e.memset(self.C.eps[:, :], EPS), writes=["eps"])
        self.pss = [P.ps(f"ps{i}") for i in range(nps)]
        self.nps = nps
        self.psi = 0
        if with_ss:
            self.ps_ss = P.ps("ps_ss")
            self.rstd = P.sb("rstd", [128, TB], F32)
            self.tmp = P.sb("tmp", [128, TB], F32)
        self.ring = WRing(P, wdram, wmeta, nslots=nslots) if wdram is not None else None

    def next_ps(self):
        i = self.psi % self.nps
        self.psi += 1
        return self.pss[i], ("ps", i)

    def group(self, wtiles, rhs_of_k, key_of_k, evac, mw=128, ncol=TB):
        P = self.P
        ps, pkey = self.next_ps()
        total = sum(nk for _, nk in wtiles)
        kbase = 0
        for tid, nk in wtiles:
            wap, wkey = self.ring.fetch(tid)
            for k in range(nk):
                kk = kbase + k
                P.add("pe", lambda e, wap=wap, k=k, kk=kk: e.matmul(
                    ps[0:mw, :ncol], wap[:, k * mw:(k + 1) * mw], rhs_of_k(kk), start=(kk == 0),
                    stop=(kk == total - 1)), reads=[wkey, key_of_k(kk)], writes=[pkey])
            kbase += nk
        evac(ps, pkey)

    def norm_stats(self, src, src_keys, sq, sq_keys, nch, dim, ntok=TB):
        emit_rmsnorm_stats(self.P, self.C, src, src_keys, sq, sq_keys, nch, self.ps_ss, "ps_ss", self.rstd, "rstd",
                           self.tmp, "tmp", dim, ntok)

    def norm_apply(self, dst, dst_keys, src, src_keys, g_ap_of_c, g_key, nch, eng="dve"):
        for c in range(nch):
            self.P.add(eng, lambda e, c=c: e.scalar_tensor_tensor(
                out=dst[:, c, :], in0=src[:, c, :], scalar=g_ap_of_c(c), in1=self.rstd[:, :], op0=ALU.mult,
                op1=ALU.mult), reads=[src_keys[c], "rstd", g_key], writes=[dst_keys[c]])


TWO_PI = 6.283185307179586
CW1 = 6.28125
CW2 = TWO_PI - CW1
PI_LO = 3.1415925


def emit_rope_tables(B, pos_i, pos_key, rc, n, cos2, sin2, tkey):
    P = B.P
    W = B.rope_ws
    ang, kfl, r, m = W[:, 0, :n], W[:, 1, :n], W[:, 2, :n], W[:, 3, :n]
    ki = B.rope_wi[:, :n]
    K = "ropews"
    P.add("dve", lambda e: e.tensor_copy(out=kfl, in_=pos_i), reads=[pos_key], writes=[K])
    P.add("dve", lambda e: e.tensor_scalar(out=ang, in0=kfl, scalar1=rc[:, 0:1], scalar2=None, op0=ALU.mult),
          reads=[K, "rc"], writes=[K])
    P.add("dve", lambda e: e.tensor_scalar(out=ki, in0=ang, scalar1=1.0 / TWO_PI, scalar2=None, op0=ALU.mult),
          reads=[K], writes=[K])
    P.add("dve", lambda e: e.tensor_copy(out=kfl, in_=ki), reads=[K], writes=[K])
    P.add("dve", lambda e: e.scalar_tensor_tensor(out=r, in0=kfl, scalar=-CW1, in1=ang, op0=ALU.mult, op1=ALU.add),
          reads=[K], writes=[K])
    P.add("dve", lambda e: e.scalar_tensor_tensor(out=r, in0=kfl, scalar=-CW2, in1=r, op0=ALU.mult, op1=ALU.add),
          reads=[K], writes=[K])
    P.add("dve", lambda e: e.tensor_scalar(out=m, in0=r, scalar1=np.pi, scalar2=-TWO_PI, op0=ALU.is_gt,
                                           op1=ALU.mult), reads=[K], writes=[K])
    P.add("dve", lambda e: e.tensor_tensor(out=r, in0=r, in1=m, op=ALU.add), reads=[K], writes=[K])
    P.add("dve", lambda e: e.tensor_scalar(out=r, in0=r, scalar1=PI_LO, scalar2=-PI_LO, op0=ALU.min, op1=ALU.max),
          reads=[K], writes=[K])
    P.add("act", lambda e: e.activation(out=m, in_=r, func=AF.Sin), reads=[K], writes=[K])
    P.add("act", lambda e: e.activation(out=ang, in_=r, func=AF.Sin, scale=0.5), reads=[K], writes=[K])
    P.add("dve", lambda e: e.tensor_scalar(out=sin2, in0=m, scalar1=rc[:, 1:2], scalar2=None, op0=ALU.mult),
          reads=[K, "rc"], writes=[tkey])
    P.add("dve", lambda e: e.tensor_tensor(out=ang, in0=ang, in1=ang, op=ALU.mult), reads=[K], writes=[K])
    P.add("dve", lambda e: e.tensor_scalar(out=cos2, in0=ang, scalar1=-2.0, scalar2=1.0, op0=ALU.mult, op1=ALU.add),
          reads=[K], writes=[tkey])


def rope_consts():
    half = 32
    inv = (10000.0 ** (-(np.arange(half, dtype=np.float32) / np.float32(half)))).astype(np.float32)
    rc = np.zeros((64, 2), np.float32)
    rc[:, 0] = np.concatenate([inv, inv])
    rc[:32, 1] = -1.0
    rc[32:, 1] = 1.0
    return rc


def build_mla_proj(T, wmeta, wplan, stop=99):
    nc = bass.Bass("TRN2", target_bir_lowering=False)
    NB = T // TB
    xT = nc.dram_tensor("xT", [D, T], F32, kind="ExternalInput").ap()
    pos = nc.dram_tensor("pos", [64, T], I32, kind="ExternalInput").ap()
    rcd = nc.dram_tensor("rc", [64, 2], F32, kind="ExternalInput").ap()
    wdram = nc.dram_tensor("wstream", [128, wmeta[-1][0] + wmeta[-1][1]], F32, kind="ExternalInput").ap()
    gd = nc.dram_tensor("gvecs", [128, DC + 8], F32, kind="ExternalInput").ap()
    qn_o = nc.dram_tensor("qn", [128, HEADS, T], BF16, kind="ExternalOutput").ap()
    qr_o = nc.dram_tensor("qr", [64, HEADS, T], BF16, kind="ExternalOutput").ap()
    kn_o = nc.dram_tensor("kn", [128, HEADS, T], BF16, kind="ExternalOutput").ap()
    kr_o = nc.dram_tensor("kr", [64, T], BF16, kind="ExternalOutput").ap()
    v_o = nc.dram_tensor("v", [T, D], BF16, kind="ExternalOutput").ap()
    xTv = xT.rearrange("(c p) t -> p c t", p=128)
    v_ov = v_o.rearrange("(t p) c -> p t c", p=128)

    B = Bld(nc, wdram, wmeta)
    P = B.P
    x_sb = P.sb("x_sb", [128, DC, TB], F32)
    h_sb = P.sb("h_sb", [128, DC, TB], BF16)
    cqp = P.sb("cqp", [128, 4, TB], F32)
    ckvp = P.sb("ckvp", [128, 4, TB], F32)
    cqn = P.sb("cqn", [128, 4, TB], BF16)
    ckvn = P.sb("ckvn", [128, 4, TB], BF16)
    g_sb = P.sb("g_sb", [128, DC + 8], F32)
    rc = P.sb("rc_sb", [64, 2], F32)
    pos_sb = P.sb("pos_sb", [64, TB], I32)
    B.rope_ws = P.sb("rope_ws", [64, 4, TB], F32)
    B.rope_wi = P.sb("rope_wi", [64, TB], I32)
    cos2 = P.sb("cos2", [64, TB], F32)
    sin2 = P.sb("sin2", [64, TB], F32)
    rt = P.sb("rt", [64, 2, 2, TB], F32)
    st_qn = P.sb("st_qn", [128, HEADS, TB], BF16)
    st_qr = P.sb("st_qr", [64, HEADS, TB], BF16)
    st_kn = P.sb("st_kn", [128, HEADS, TB], BF16)
    st_kr = P.sb("st_kr", [64, TB], BF16)
    st_v = P.sb("st_v", [128, 4, D], BF16)

    P.dma("sp", g_sb[:, :], gd[:, :], writes=["g"])
    P.dma("sp", rc[:, :], rcd[:, :], writes=["rc"])

    rti = [0]

    def rope_apply(ps_a, ka, ps_b, kb, out_ap, out_key):
        i = rti[0] % 2
        rti[0] += 1
        t1, t2 = rt[:, i, 0, :], rt[:, i, 1, :]
        P.add("dve", lambda e: e.tensor_tensor(out=t1, in0=ps_a[0:64, :], in1=cos2[:, :], op=ALU.mult),
              reads=[ka, "tables"], writes=[("rt", i, 0)])
        P.add("dve", lambda e: e.tensor_tensor(out=t2, in0=ps_b[0:64, :], in1=sin2[:, :], op=ALU.mult),
              reads=[kb, "tables"], writes=[("rt", i, 1)])
        P.add("dve", lambda e: e.tensor_tensor(out=out_ap, in0=t1, in1=t2, op=ALU.add),
              reads=[("rt", i, 0), ("rt", i, 1)], writes=[out_key])

    def pair_group(tiles_a, tiles_b, rhs_of_k, key_of_k, out_ap, out_key):
        hold = {}

        def ev_a(ps, pkey):
            hold["a"] = (ps, pkey)

        def ev_b(ps, pkey):
            pa, ka = hold["a"]
            rope_apply(pa, ka, ps, pkey, out_ap, out_key)
        B.group(tiles_a, rhs_of_k, key_of_k, ev_a, mw=64)
        B.group(tiles_b, rhs_of_k, key_of_k, ev_b, mw=64)

    def body(b):
        ts = slice(b * TB, (b + 1) * TB)
        for c0 in range(0, DC, 4):
            P.dma("sp", x_sb[:, c0:c0 + 4, :], xTv[:, c0:c0 + 4, ts], writes=[("x", c) for c in range(c0, c0 + 4)])
        P.dma("sp", pos_sb[:, :], pos[:, ts], writes=["pos"])
        emit_rope_tables(B, pos_sb[:, :], "pos", rc, TB, cos2[:, :], sin2[:, :], "tables")
        if stop <= 1:
            return
        B.norm_stats(x_sb, [("x", c) for c in range(DC)], h_sb, [("h", c) for c in range(DC)], DC, D)
        B.norm_apply(h_sb, [("h", c) for c in range(DC)], x_sb, [("x", c) for c in range(DC)],
                     lambda c: g_sb[:, c:c + 1], "g", DC)
        hk = lambda kk: h_sb[:, kk, :]
        hkey = lambda kk: ("h", kk)
        for m in range(4):
            B.group(wplan["dq"][m], hk, hkey, lambda ps, pkey, m=m: P.add(
                "act", lambda e: e.copy(out=cqp[:, m, :], in_=ps[:, :]), reads=[pkey], writes=[("cqp", m)]))
        for m in range(4):
            B.group(wplan["dkv"][m], hk, hkey, lambda ps, pkey, m=m: P.add(
                "act", lambda e: e.copy(out=ckvp[:, m, :], in_=ps[:, :]), reads=[pkey], writes=[("ckvp", m)]))
        if stop <= 2:
            return
        pair_group(wplan["kr"][0], wplan["kr_sw"][0], hk, hkey, st_kr[:, :], "st_kr")
        P.dma("sp", kr_o[:, ts], st_kr[:, :], reads=["st_kr"])
        if stop <= 3:
            return
        B.norm_stats(cqp, [("cqp", c) for c in range(4)], cqn, [("cqn", c) for c in range(4)], 4, 512)
        B.norm_apply(cqn, [("cqn", c) for c in range(4)], cqp, [("cqp", c) for c in range(4)],
                     lambda c: g_sb[:, DC + c:DC + c + 1], "g", 4)
        B.norm_stats(ckvp, [("ckvp", c) for c in range(4)], ckvn, [("ckvn", c) for c in range(4)], 4, 512)
        B.norm_apply(ckvn, [("ckvn", c) for c in range(4)], ckvp, [("ckvp", c) for c in range(4)],
                     lambda c: g_sb[:, DC + 4 + c:DC + 4 + c + 1], "g", 4)
        if stop <= 4:
            return
        qk = lambda kk: cqn[:, kk, :]
        qkey = lambda kk: ("cqn", kk)
        kk_ = lambda kk: ckvn[:, kk, :]
        kkey = lambda kk: ("ckvn", kk)
        for h in range(HEADS):
            B.group(wplan["uq_n"][h], qk, qkey, lambda ps, pkey, h=h: P.add(
                "act", lambda e: e.copy(out=st_qn[:, h, :], in_=ps[:, :]), reads=[pkey], writes=[("st_qn", h)]))
        P.dma("sp", qn_o[:, :, ts], st_qn[:, :, :], reads=[("st_qn", h) for h in range(HEADS)])
        if stop <= 5:
            return
        for h in range(HEADS):
            pair_group(wplan["uq_r"][h], wplan["uq_rs"][h], qk, qkey, st_qr[:, h, :], ("st_qr", h))
        P.dma("sp", qr_o[:, :, ts], st_qr[:, :, :], reads=[("st_qr", h) for h in range(HEADS)])
        if stop <= 6:
            return
        for h in range(HEADS):
            B.group(wplan["uk"][h], kk_, kkey, lambda ps, pkey, h=h: P.add(
                "act", lambda e: e.copy(out=st_kn[:, h, :], in_=ps[:, :]), reads=[pkey], writes=[("st_kn", h)]))
        P.dma("sp", kn_o[:, :, ts], st_kn[:, :, :], reads=[("st_kn", h) for h in range(HEADS)])
        if stop <= 7:
            return
        for cg in range(4):
            wap, wkey = B.ring.fetch(wplan["uv"][cg])
            for tt in range(4):
                ps, pkey = B.next_ps()
                for k in range(4):
                    P.add("pe", lambda e, ps=ps, k=k, tt=tt, wap=wap: e.matmul(
                        ps[:, :], ckvn[:, k, tt * 128:(tt + 1) * 128], wap[:, k * 512:(k + 1) * 512],
                        start=(k == 0), stop=(k == 3)), reads=[wkey, ("ckvn", k)], writes=[pkey])
                P.add("act", lambda e, ps=ps, tt=tt, cg=cg: e.copy(out=st_v[:, tt, cg * 512:(cg + 1) * 512],
                                                                    in_=ps[:, :]),
                      reads=[pkey], writes=[("st_v", tt, cg)])
        if stop <= 8:
            return
        for hf in range(2):
            P.dma("sp", v_ov[:, b * 4:(b + 1) * 4, hf * 1024:(hf + 1) * 1024], st_v[:, :, hf * 1024:(hf + 1) * 1024],
                  reads=[("st_v", tt, cg) for tt in range(4) for cg in range(2 * hf, 2 * hf + 2)])
    for b in range(NB):
        body(b)
    P.emit()
    return nc


def prep_mla_proj_weights(w_dq, w_dkv, w_uq, w_ukv):
    ws = WStream()
    plan = {}
    plan["dq"] = ws.add_matrix(w_dq)
    plan["dkv"] = ws.add_matrix(w_dkv[:, :512])
    kr = w_dkv[:, 512:576]
    plan["kr"] = ws.add_matrix(kr, mw=64)
    plan["kr_sw"] = ws.add_matrix(np.concatenate([kr[:, 32:], kr[:, :32]], axis=1), mw=64)
    uq = w_uq.reshape(512, HEADS, 192)
    plan["uq_n"] = ws.add_matrix(np.ascontiguousarray(uq[:, :, :128]).reshape(512, HEADS * 128))
    plan["uq_r"] = []
    plan["uq_rs"] = []
    for h in range(HEADS):
        r = uq[:, h, 128:192]
        plan["uq_r"].append(ws.add_matrix(r, mw=64)[0])
        plan["uq_rs"].append(ws.add_matrix(np.concatenate([r[:, 32:], r[:, :32]], axis=1), mw=64)[0])
    ukv = w_ukv.reshape(512, HEADS, 256)
    plan["uk"] = ws.add_matrix(np.ascontiguousarray(ukv[:, :, :128]).reshape(512, HEADS * 128))
    wv = np.ascontiguousarray(ukv[:, :, 128:]).reshape(512, HEADS * 128)
    plan["uv"] = []
    for cg in range(4):
        t = wv[:, cg * 512:(cg + 1) * 512].reshape(4, 128, 512).transpose(1, 0, 2).reshape(128, 2048)
        plan["uv"].append(ws.add_raw(t))
    return ws.array(), ws.meta, plan


ATT_SCALE = 192.0 ** -0.5


def attn_masks():
    k = np.arange(128)[:, None, None] + 128 * np.arange(4)[None, :, None]
    q = np.arange(512)[None, None, :]
    return ((k // 64) <= (q // 64)).astype(np.float32).astype(NPBF16)


def build_attn(S, NH):
    nc = bass.Bass("TRN2", target_bir_lowering=False)
    NQ = S // TB
    NKP = S // 1024
    qn = nc.dram_tensor("qn", [NH, 128, S], BF16, kind="ExternalInput").ap()
    qr = nc.dram_tensor("qr", [NH, 64, S], BF16, kind="ExternalInput").ap()
    kn = nc.dram_tensor("kn", [NH, 128, S], BF16, kind="ExternalInput").ap()
    kr = nc.dram_tensor("kr", [64, S], BF16, kind="ExternalInput").ap()
    v = nc.dram_tensor("v", [NH, 128, S // 128, 128], BF16, kind="ExternalInput").ap()
    msk = nc.dram_tensor("masks", [128, 4, 512], BF16, kind="ExternalInput").ap()
    o = nc.dram_tensor("o", [NH, 128, S], BF16, kind="ExternalOutput").ap()

    emit_attn(nc, S, NH, qn, qr, kn, kr, v, msk, o)
    return nc


def emit_attn(nc, S, NH, qn, qr, kn, kr, v, msk, o):
    NQ = S // TB
    NKP = S // 1024
    B = Bld(nc, None, None, nps=4, with_ss=False)
    P = B.P
    C = B.C
    ps_o = [P.ps(f"ps_o{i}") for i in range(2)]
    ps_d = [P.ps(f"ps_d{i}") for i in range(2)]
    kn_sb = [P.sb(f"kn_sb{i}", [128, S], BF16) for i in range(2)]
    v_sb = [P.sb(f"v_sb{i}", [128, S // 128, 128], BF16) for i in range(2)]
    kr_sb = P.sb("kr_sb", [128, S], BF16)
    m_sb = P.sb("m_sb", [128, 4, 512], BF16)
    qn_sb = [P.sb(f"qn_sb{i}", [128, TB], BF16) for i in range(2)]
    qr_sb = [P.sb(f"qr_sb{i}", [128, TB], BF16) for i in range(2)]
    NPT = 6
    pt = [P.sb(f"pt{i}", [128, TB], BF16) for i in range(NPT)]
    rec = [P.sb(f"rec{i}", [128, TB], F32) for i in range(2)]
    ost = [P.sb(f"ost{i}", [128, TB], BF16) for i in range(2)]
    acc = [P.sb(f"acc{i}", [128, TB], F32) for i in range(2)]
    ones_f = P.sb("ones_f", [128, 128], F32)
    P.add("dve", lambda e: e.memset(ones_f[:, :], 1.0), writes=["ones_f"])

    for j in range(4):
        P.dma("sp", m_sb[:, j, :], msk[:, j, :], writes=[("m", j)])
    for p in range(NKP):
        P.add("pool", lambda e, p=p: e.memset(kr_sb[:, p * 1024:(p + 1) * 1024], 0.0), writes=[("kr", p)])
    for i in range(2):
        P.add("pool", lambda e, i=i: e.memset(qr_sb[i][:, :], 0.0), writes=[("qr", i)])
    for p in range(NKP):
        P.dma("sp", kr_sb[0:64, p * 1024:(p + 1) * 1024], kr[:, p * 1024:(p + 1) * 1024], writes=[("kr", p)])
    pti = [0]
    qi = [0]
    def qblock(h, hp, qb, i):
        qs = slice(qb * TB, (qb + 1) * TB)
        P.dma("sp", qn_sb[i][:, :], qn[h, :, qs], writes=[("qn", i)])
        P.dma("sp", qr_sb[i][0:64, :], qr[h, :, qs], writes=[("qr", i)])
        nkt = 4 * (qb + 1)
        tiles = {}

        def S_(kt):
            ps, pkey = B.next_ps()
            j = pti[0] % NPT
            pti[0] += 1
            tiles[kt] = j
            ks = slice(kt * 128, (kt + 1) * 128)
            P.add("pe", lambda e: e.matmul(ps[:, :], kn_sb[hp][:, ks], qn_sb[i][:, :], start=True, stop=False),
                  reads=[("kn", hp, kt // 8), ("qn", i)], writes=[pkey])
            P.add("pe", lambda e: e.matmul(ps[:, :], kr_sb[:, ks], qr_sb[i][:, :], start=False, stop=True),
                  reads=[("kr", kt // 8), ("qr", i)], writes=[pkey])
            P.add("act", lambda e: e.activation(out=pt[j][:, :], in_=ps[:, :], func=AF.Exp, scale=ATT_SCALE),
                  reads=[pkey], writes=[("pt", j)])
            if kt >= 4 * qb:
                jj = kt - 4 * qb
                P.add("dve", lambda e: e.tensor_tensor(out=pt[j][:, :], in0=pt[j][:, :], in1=m_sb[:, jj, :],
                                                       op=ALU.mult),
                      reads=[("pt", j), ("m", jj)], writes=[("pt", j)])

        def PV_(kt):
            j = tiles[kt]
            P.add("pe", lambda e: e.matmul(ps_o[i][:, :], v_sb[hp][:, kt, :], pt[j][:, :], start=(kt == 0),
                                           stop=(kt == nkt - 1)),
                  reads=[("v", hp, kt // 8), ("pt", j)], writes=[("ps_o", i)])
            if kt == 0:
                P.add("dve", lambda e: e.tensor_copy(out=acc[i][:, :], in_=pt[j][:, :]),
                      reads=[("pt", j)], writes=[("acc", i)])
            else:
                P.add("dve", lambda e: e.tensor_tensor(out=acc[i][:, :], in0=acc[i][:, :], in1=pt[j][:, :],
                                                       op=ALU.add),
                      reads=[("pt", j), ("acc", i)], writes=[("acc", i)])

        LA = 3
        for kt in range(min(LA, nkt)):
            S_(kt)
        for kt in range(nkt):
            PV_(kt)
            if kt + LA < nkt:
                S_(kt + LA)
        P.add("pe", lambda e: e.matmul(ps_d[i][:, :], ones_f[:, :], acc[i][:, :], start=True, stop=True),
              reads=["ones_f", ("acc", i)], writes=[("ps_d", i)])
        P.add("dve", lambda e, i=i: e.reciprocal(out=rec[i][:, :], in_=ps_d[i][:, :]),
              reads=[("ps_d", i)], writes=[("rec", i)])
        P.add("dve", lambda e, i=i: e.tensor_tensor(out=ost[i][:, :], in0=ps_o[i][:, :], in1=rec[i][:, :],
                                                    op=ALU.mult),
              reads=[("ps_o", i), ("rec", i)], writes=[("ost", i)])
        P.dma("sp", o[h, :, qs], ost[i][:, :], reads=[("ost", i)])
    for h in range(NH):
        hp = h % 2
        for p in range(NKP):
            P.dma("sp", kn_sb[hp][:, p * 1024:(p + 1) * 1024], kn[h, :, p * 1024:(p + 1) * 1024],
                  writes=[("kn", hp, p)])
            P.dma("sp", v_sb[hp][:, p * 8:(p + 1) * 8, :], v[h, :, p * 8:(p + 1) * 8, :], writes=[("v", hp, p)])
        for qb in range(NQ):
            qblock(h, hp, qb, qi[0] % 2)
            qi[0] += 1
    P.emit()


CH = 84
NJ = 4


def build_lru(NT, NBATCH, wmeta, wplan):
    nc = bass.Bass("TRN2", target_bir_lowering=False)
    NB = NT // TB
    NBB = NB // NBATCH
    xT = nc.dram_tensor("xT", [D, NT], F32, kind="ExternalInput").ap()
    wdram = nc.dram_tensor("wstream", [128, wmeta[-1][0] + wmeta[-1][1]], F32, kind="ExternalInput").ap()
    wgd = nc.dram_tensor("wg", [CH, 2 * NJ * 2 * CH], F32, kind="ExternalInput").ap()
    vecd = nc.dram_tensor("vec", [CH, NJ, 10], F32, kind="ExternalInput").ap()
    gd = nc.dram_tensor("gpre", [128, DC], F32, kind="ExternalInput").ap()
    hy = nc.dram_tensor("hy", [NJ, CH, NT], BF16, kind="ExternalOutput").ap()
    xTv = xT.rearrange("(c p) t -> p c t", p=128)
    hyv = hy.rearrange("j p t -> p j t")

    B = Bld(nc, wdram, wmeta)
    P = B.P
    x_sb = P.sb("x_sb", [128, DC, TB], F32)
    h_sb = P.sb("h_sb", [128, DC, TB], BF16)
    g_sb = P.sb("g_sb", [128, DC], F32)
    wg_sb = P.sb("wg_sb", [128, 2, NJ, 2, CH], BF16)
    vec = P.sb("vec_sb", [128, NJ, 10], F32)
    cc = P.sb("cc", [128, 4, NJ], F32)
    one_c = P.sb("one_c", [128, 1], F32)
    y_sb = P.sb("y_sb", [128, NJ, TB], F32)
    uxb = P.sb("uxb", [128, NJ, TB + 4], F32)
    u_sb = P.sb("u_sb", [128, NJ, TB], F32)
    u_bf = P.sb("u_bf", [128, NJ, TB], BF16)
    r_sb = P.sb("r_sb", [128, NJ, TB], F32)
    ig_sb = P.sb("ig_sb", [128, NJ, TB], F32)
    a_sb = P.sb("a_sb", [128, NJ, TB], F32)
    m_sb = P.sb("m_sb", [128, NJ, TB], F32)
    inp_sb = P.sb("inp_sb", [128, NJ, TB], F32)
    hs_sb = P.sb("hs_sb", [128, NJ, TB], F32)
    ost = P.sb("ost", [128, NJ, TB], BF16)
    state = P.sb("state", [128, NJ], F32)

    P.dma("sp", g_sb[:, :], gd[:, :], writes=["g"])
    P.dma("sp", vec[0:CH, :, :], vecd[:, :, :], writes=["vec"])
    P.dma("pool", wg_sb[0:CH, :, :, :, :].rearrange("p a b c d -> p (a b c d)"), wgd[:, :], writes=["wg"])
    P.add("dve", lambda e: e.memset(one_c[:, :], 1.0), writes=["one_c"])
    e_ = cc[0:CH, 0, :]
    t_ = cc[0:CH, 1, :]
    P.add("act", lambda e: e.activation(out=e_, in_=vec[0:CH, :, 5], func=AF.Exp, scale=-1.0), reads=["vec"],
          writes=["cc"])
    coef = [1.0 / 5, -1.0 / 4, 1.0 / 3, -1.0 / 2, 1.0]
    P.add("dve", lambda e: e.tensor_scalar(out=t_, in0=e_, scalar1=-1.0 / 6, scalar2=coef[0], op0=ALU.mult,
                                           op1=ALU.add), reads=["cc"], writes=["cc"])
    for cf in coef[1:]:
        P.add("dve", lambda e: e.tensor_tensor(out=t_, in0=t_, in1=e_, op=ALU.mult), reads=["cc"], writes=["cc"])
        P.add("dve", lambda e, cf=cf: e.tensor_scalar(out=t_, in0=t_, scalar1=cf, scalar2=None, op0=ALU.add),
              reads=["cc"], writes=["cc"])
    P.add("dve", lambda e: e.tensor_tensor(out=t_, in0=t_, in1=e_, op=ALU.mult), reads=["cc"], writes=["cc"])
    P.add("dve", lambda e: e.tensor_scalar(out=cc[0:CH, 2, :], in0=t_, scalar1=-8.0, scalar2=None, op0=ALU.mult),
          reads=["cc"], writes=["cc"])
    P.add("dve", lambda e: e.tensor_scalar(out=cc[0:CH, 3, :], in0=t_, scalar1=-16.0, scalar2=None, op0=ALU.mult),
          reads=["cc"], writes=["cc"])

    def body(b):
        ts = slice(b * TB, (b + 1) * TB)
        if b % NBB == 0:
            P.add("dve", lambda e: e.memset(state[:, :], 0.0), writes=[("state", j) for j in range(NJ)])
            P.add("dve", lambda e: e.memset(uxb[:, :, 0:3], 0.0), writes=[("ux", j) for j in range(NJ)])
        for c0 in range(0, DC, 4):
            P.dma("sp", x_sb[:, c0:c0 + 4, :], xTv[:, c0:c0 + 4, ts], writes=[("x", c) for c in range(c0, c0 + 4)])
        B.norm_stats(x_sb, [("x", c) for c in range(DC)], h_sb, [("h", c) for c in range(DC)], DC, D)
        B.norm_apply(h_sb, [("h", c) for c in range(DC)], x_sb, [("x", c) for c in range(DC)],
                     lambda c: g_sb[:, c:c + 1], "g", DC)
        hk = lambda kk: h_sb[:, kk, :]
        hkey = lambda kk: ("h", kk)
        for j in range(NJ):
            B.group(wplan["wy"][j], hk, hkey, lambda ps, pkey, j=j: P.add(
                "act", lambda e: e.activation(out=y_sb[0:CH, j, :], in_=ps[0:CH, :], func=AF.Gelu,
                                              bias=vec[0:CH, j, 0:1]),
                reads=[pkey, "vec"], writes=[("y", j)]), mw=CH)
        for j in range(NJ):
            B.group(wplan["wx"][j], hk, hkey, lambda ps, pkey, j=j: P.add(
                "act", lambda e: e.activation(out=uxb[0:CH, j, 3:3 + TB], in_=ps[0:CH, :], func=AF.Identity,
                                              bias=vec[0:CH, j, 1:2]),
                reads=[pkey, "vec"], writes=[("ux", j)]), mw=CH)
        for j in range(NJ):
            P.add("dve", lambda e, j=j: e.tensor_scalar(out=u_sb[0:CH, j, :], in0=uxb[0:CH, j, 0:TB],
                                                        scalar1=vec[0:CH, j, 6:7], scalar2=vec[0:CH, j, 2:3],
                                                        op0=ALU.mult, op1=ALU.add),
                  reads=[("ux", j), "vec"], writes=[("u", j)])
            for i in range(1, 4):
                P.add("dve", lambda e, j=j, i=i: e.scalar_tensor_tensor(
                    out=u_sb[0:CH, j, :], in0=uxb[0:CH, j, i:i + TB], scalar=vec[0:CH, j, 6 + i:7 + i],
                    in1=u_sb[0:CH, j, :], op0=ALU.mult, op1=ALU.add),
                    reads=[("ux", j), "vec", ("u", j)], writes=[("u", j)])
            P.add("dve", lambda e, j=j: e.tensor_copy(out=uxb[0:CH, j, 0:3], in_=uxb[0:CH, j, TB:TB + 3]),
                  reads=[("ux", j)], writes=[("ux", j)])
            P.add("act", lambda e, j=j: e.copy(out=u_bf[0:CH, j, :], in_=u_sb[0:CH, j, :]),
                  reads=[("u", j)], writes=[("ubf", j)])
        for gi, dst, bidx in ((0, r_sb, 3), (1, ig_sb, 4)):
            for j in range(NJ):
                ps, pkey = B.next_ps()
                for i in range(2):
                    P.add("pe", lambda e, ps=ps, gi=gi, j=j, i=i: e.matmul(
                        ps[0:CH, :], wg_sb[0:CH, gi, j, i, :], u_bf[0:CH, 2 * (j // 2) + i, :], start=(i == 0),
                        stop=(i == 1)), reads=["wg", ("ubf", 2 * (j // 2) + i)], writes=[pkey])
                P.add("act", lambda e, ps=ps, j=j, dst=dst, bidx=bidx: e.activation(
                    out=dst[0:CH, j, :], in_=ps[0:CH, :], func=AF.Sigmoid, bias=vec[0:CH, j, bidx:bidx + 1]),
                    reads=[pkey, "vec"], writes=[("gate", gi, j)])
        for j in range(NJ):
            P.add("act", lambda e, j=j: e.activation(out=a_sb[0:CH, j, :], in_=r_sb[0:CH, j, :], func=AF.Exp,
                                                     scale=cc[0:CH, 2, j:j + 1]),
                  reads=[("gate", 0, j), "cc"], writes=[("a", j)])
            P.add("act", lambda e, j=j: e.activation(out=m_sb[0:CH, j, :], in_=r_sb[0:CH, j, :], func=AF.Exp,
                                                     scale=cc[0:CH, 3, j:j + 1]),
                  reads=[("gate", 0, j), "cc"], writes=[("m", j)])
        for j in range(NJ):
            P.add("dve", lambda e, j=j: e.tensor_scalar(out=m_sb[0:CH, j, :], in0=m_sb[0:CH, j, :], scalar1=-1.0,
                                                        scalar2=1.0, op0=ALU.mult, op1=ALU.add),
                  reads=[("m", j)], writes=[("m", j)])
        for j in range(NJ):
            P.add("act", lambda e, j=j: e.activation(out=m_sb[0:CH, j, :], in_=m_sb[0:CH, j, :], func=AF.Sqrt),
                  reads=[("m", j)], writes=[("m", j)])
        for j in range(NJ):
            P.add("dve", lambda e, j=j: e.tensor_tensor(out=inp_sb[0:CH, j, :], in0=ig_sb[0:CH, j, :],
                                                        in1=u_sb[0:CH, j, :], op=ALU.mult),
                  reads=[("gate", 1, j), ("u", j)], writes=[("inp", j)])
            P.add("dve", lambda e, j=j: e.tensor_tensor(out=inp_sb[0:CH, j, :], in0=inp_sb[0:CH, j, :],
                                                        in1=m_sb[0:CH, j, :], op=ALU.mult),
                  reads=[("inp", j), ("m", j)], writes=[("inp", j)])
            P.add("dve", lambda e, j=j: e.tensor_tensor_scan(
                out=hs_sb[0:CH, j, :], data0=a_sb[0:CH, j, :], data1=inp_sb[0:CH, j, :],
                initial=state[0:CH, j:j + 1], op0=ALU.mult, op1=ALU.add),
                reads=[("a", j), ("inp", j), ("state", j)], writes=[("hs", j)])
            P.add("dve", lambda e, j=j: e.tensor_copy(out=state[0:CH, j:j + 1], in_=hs_sb[0:CH, j, TB - 1:TB]),
                  reads=[("hs", j)], writes=[("state", j)])
            P.add("dve", lambda e, j=j: e.tensor_tensor(out=ost[0:CH, j, :], in0=hs_sb[0:CH, j, :],
                                                        in1=y_sb[0:CH, j, :], op=ALU.mult),
                  reads=[("hs", j), ("y", j)], writes=[("ost", j)])
        P.dma("sp", hyv[:, :, ts], ost[0:CH, :, :], reads=[("ost", j) for j in range(NJ)])

    for b in range(NB):
        body(b)
    P.emit()
    return nc


def prep_lru_weights(core, w_y, b_y, w_x, b_x, conv_w, conv_b, w_ga, b_ga, w_gi, b_gi, lam):
    c0 = core * NJ * CH
    c1 = c0 + NJ * CH
    ws = WStream()
    plan = {"wy": ws.add_matrix(np.ascontiguousarray(w_y[:, c0:c1]), mw=CH),
            "wx": ws.add_matrix(np.ascontiguousarray(w_x[:, c0:c1]), mw=CH)}
    wg = np.zeros((CH, 2, NJ, 2, CH), np.float32)
    for gi, w in enumerate((w_ga, w_gi)):
        for j in range(NJ):
            n = 2 * core + j // 2
            for i in range(2):
                wg[:, gi, j, i, :] = w[n, i * CH:(i + 1) * CH, (j % 2) * CH:(j % 2 + 1) * CH]
    vec = np.zeros((CH, NJ, 10), np.float32)
    for idx, vv in enumerate((b_y, b_x, conv_b, b_ga, b_gi, lam)):
        vec[:, :, idx] = vv[c0:c1].reshape(NJ, CH).T
    for i in range(4):
        vec[:, :, 6 + i] = conv_w[i, c0:c1].reshape(NJ, CH).T
    return ws.array(), ws.meta, plan, np.ascontiguousarray(wg.reshape(CH, -1)), vec


def _run(nc, in_maps):
    res = run_bass_kernel_spmd(nc, in_maps, core_ids=list(range(NCORES)))
    return res.results


def kernel_unfused(x, positions, mix_pre_g, mix_post_g, mlp_pre_g, mlp_post_g,
           mla_w_dq, mla_g_q, mla_w_uq, mla_w_dkv, mla_g_kv, mla_w_ukv, mla_w_o,
           lru_w_y, lru_b_y, lru_w_x, lru_b_x, lru_conv_w, lru_conv_b,
           lru_w_ga, lru_b_ga, lru_w_gi, lru_b_gi, lru_lam, lru_w_out, lru_b_out,
           mlp_w1, mlp_w2):
    f32 = lambda a: np.asarray(a, np.float32)
    x = f32(x)
    positions = np.asarray(positions, np.int32)
    NBATCH, S, _ = x.shape
    T = NBATCH * S // NCORES
    QPB = S // T
    NT = NBATCH * S
    NH = HEADS * NBATCH // NCORES
    xs = x.reshape(NCORES, T, D)
    xT = [np.ascontiguousarray(xs[c].T) for c in range(NCORES)]
    rc = rope_consts()
    masks = attn_masks()
    zero_b = np.zeros((128, DC), np.float32)
    depth = mix_pre_g.shape[0]

    def post_mlp(i, mT, w_o, b_o, KIN):
        warr, wmeta, wplan = prep_post_mlp_weights(f32(w_o), f32(mlp_w1[i]), f32(mlp_w2[i]))
        nc = build_post_mlp(T, KIN, b_o is not None, wmeta, wplan)
        gpost, gpre, gpost2 = gvec(mix_post_g[i]), gvec(mlp_pre_g[i]), gvec(mlp_post_g[i])
        bo = gvec(b_o) if b_o is not None else zero_b
        maps = [{"xT": xT[c], "mT": mT[c], "wstream": warr, "gpost": gpost, "gpre": gpre, "gpost2": gpost2,
                 "bout": bo} for c in range(NCORES)]
        r = _run(nc, maps)
        return [np.asarray(r[c]["oT"]) for c in range(NCORES)]

    for i in range(depth):
        j = i // 2
        if i % 2 == 0:
            warr, wmeta, wplan = prep_mla_proj_weights(f32(mla_w_dq[j]), f32(mla_w_dkv[j]), f32(mla_w_uq[j]),
                                                       f32(mla_w_ukv[j]))
            nc = build_mla_proj(T, wmeta, wplan)
            gv = np.ascontiguousarray(np.concatenate([gvec(mix_pre_g[i]), gvec(mla_g_q[j]), gvec(mla_g_kv[j])],
                                                     axis=1))
            maps = []
            for c in range(NCORES):
                b, q = divmod(c, QPB)
                pos = np.ascontiguousarray(np.broadcast_to(positions[b, q * T:(q + 1) * T][None, :], (64, T)))
                maps.append({"xT": xT[c], "pos": pos, "rc": rc, "wstream": warr, "gvecs": gv})
            r1 = _run(nc, maps)
            del maps
            maps = []
            for c in range(NCORES):
                b, hg = divmod(c, QPB)
                hs = slice(hg * NH, (hg + 1) * NH)
                cores = [b * QPB + q for q in range(QPB)]
                qn = np.concatenate([np.asarray(r1[cc]["qn"])[:, hs, :] for cc in cores], axis=2)
                qr = np.concatenate([np.asarray(r1[cc]["qr"])[:, hs, :] for cc in cores], axis=2)
                kn = np.concatenate([np.asarray(r1[cc]["kn"])[:, hs, :] for cc in cores], axis=2)
                kr = np.concatenate([np.asarray(r1[cc]["kr"]) for cc in cores], axis=1)
                v = np.concatenate([np.asarray(r1[cc]["v"])[:, hg * NH * 128:(hg + 1) * NH * 128] for cc in cores],
                                   axis=0)
                v = v.reshape(S // 128, 128, NH, 128).transpose(2, 1, 0, 3)
                maps.append({"qn": np.ascontiguousarray(qn.transpose(1, 0, 2)),
                             "qr": np.ascontiguousarray(qr.transpose(1, 0, 2)),
                             "kn": np.ascontiguousarray(kn.transpose(1, 0, 2)),
                             "kr": np.ascontiguousarray(kr), "v": np.ascontiguousarray(v), "masks": masks})
            del r1
            nc = build_attn(S, NH)
            r2 = _run(nc, maps)
            del maps
            mT = []
            for c in range(NCORES):
                b, q = divmod(c, QPB)
                parts = [np.asarray(r2[b * QPB + hg]["o"])[:, :, q * T:(q + 1) * T].reshape(NH * 128, T)
                         for hg in range(QPB)]
                mT.append(np.ascontiguousarray(np.concatenate(parts, axis=0)))
            del r2
            xT = post_mlp(i, mT, mla_w_o[j], None, D)
        else:
            xfull = np.ascontiguousarray(np.concatenate(xT, axis=1))
            gp = gvec(mix_pre_g[i])
            maps = []
            wm = wp = None
            for c in range(NCORES):
                warr, wm, wp, wg, vec = prep_lru_weights(
                    c, f32(lru_w_y[j]), f32(lru_b_y[j]), f32(lru_w_x[j]), f32(lru_b_x[j]), f32(lru_conv_w[j]),
                    f32(lru_conv_b[j]), f32(lru_w_ga[j]), f32(lru_b_ga[j]), f32(lru_w_gi[j]), f32(lru_b_gi[j]),
                    f32(lru_lam[j]))
                maps.append({"xT": xfull, "wstream": warr, "wg": wg, "vec": vec, "gpre": gp})
            nc = build_lru(NT, NBATCH, wm, wp)
            r4 = _run(nc, maps)
            del maps, xfull
            mfull = np.concatenate([np.asarray(r4[c]["hy"]).reshape(NJ * CH, NT) for c in range(NCORES)], axis=0)
            del r4
            mT = [np.ascontiguousarray(mfull[:, c * T:(c + 1) * T]) for c in range(NCORES)]
            del mfull
            xT = post_mlp(i, mT, lru_w_out[j], lru_b_out[j], DRNN)
    out = np.stack([xT[c].T for c in range(NCORES)], axis=0).reshape(NBATCH, S, D)
    return np.ascontiguousarray(out.astype(np.float32))


GROUPS = [[0, 1, 2, 3], [4, 5, 6, 7]]
TPG = 4
NHC = HEADS // TPG
FFC = DFF // TPG // 128
LJ = 8


def _blk(ap3, b):
    return ap3[b].rearrange("(c p) t -> p c t", p=128)


class Phase:
    def __init__(self, nc, wdram, wmeta, gdram, nps=6, nslots=6):
        self.B = Bld(nc, wdram, wmeta, nps=nps, nslots=nslots)
        P = self.B.P
        self.P = P
        self.x_sb = P.sb("x_sb", [128, DC, TB], F32)
        self.y_sb = P.sb("y_sb", [128, DC, TB], F32)
        self.h_sb = P.sb("h_sb", [128, DC, TB], BF16)
        self.g_sb = P.sb("g_sb", [128, 3, DC], F32)
        self.tmp2 = P.sb("tmp2", [128, 2, TB], F32)
        P.dma("sp", self.g_sb[:, :, :], gdram[:, :, :], writes=["g"])
        self.xk = [("x", c) for c in range(DC)]
        self.yk = [("y", c) for c in range(DC)]
        self.hk = [("h", c) for c in range(DC)]

    def prologue(self, b, x_src, red, has_bias, x_dst, pre_norm=True, ready=None):
        P, B = self.P, self.B
        x_sb, y_sb, h_sb, g_sb, tmp2 = self.x_sb, self.y_sb, self.h_sb, self.g_sb, self.tmp2
        for c0 in range(0, DC, 4):
            P.dma("sp", x_sb[:, c0:c0 + 4, :], x_src[:, c0:c0 + 4, :], writes=self.xk[c0:c0 + 4])
        if red is not None:
            rv = _blk(red, b)
            for c0 in range(0, DC, 4):
                P.dma("sp", y_sb[:, c0:c0 + 4, :], rv[:, c0:c0 + 4, :], writes=self.yk[c0:c0 + 4],
                      ext=(ready[b] if (ready is not None and c0 == 0) else None))
            if has_bias:
                for c in range(DC):
                    P.add("act", lambda e, c=c: e.activation(out=y_sb[:, c, :], in_=y_sb[:, c, :], func=AF.Identity,
                                                             bias=g_sb[:, 2, c:c + 1]),
                          reads=[("y", c), "g"], writes=[("y", c)])
            B.norm_stats(y_sb, self.yk, h_sb, self.hk, DC, D)
            for c in range(DC):
                t2 = tmp2[:, c % 2, :]
                P.add("dve", lambda e, c=c, t2=t2: e.scalar_tensor_tensor(
                    out=t2, in0=y_sb[:, c, :], scalar=g_sb[:, 0, c:c + 1], in1=B.rstd[:, :], op0=ALU.mult,
                    op1=ALU.mult), reads=[("y", c), "rstd", "g"], writes=[("tmp2", c % 2)])
                P.add("dve", lambda e, c=c, t2=t2: e.tensor_tensor(out=x_sb[:, c, :], in0=x_sb[:, c, :], in1=t2,
                                                                   op=ALU.add),
                      reads=[("tmp2", c % 2), ("x", c)], writes=[("x", c)])
            if x_dst is not None:
                for c0 in range(0, DC, 4):
                    P.dma("sp", x_dst[:, c0:c0 + 4, :], x_sb[:, c0:c0 + 4, :], reads=self.xk[c0:c0 + 4])
        if pre_norm:
            B.norm_stats(x_sb, self.xk, h_sb, self.hk, DC, D)
            B.norm_apply(h_sb, self.hk, x_sb, self.xk, lambda c: g_sb[:, 1, c:c + 1], "g", DC)

    def store_partial(self, ps, pkey, part, b, m, pst, idx):
        P = self.P
        i = idx % 2
        P.add("act", lambda e: e.copy(out=pst[:, i, :], in_=ps[:, :]), reads=[pkey], writes=[("pst", i)])
        P.dma("sp", part[b, m * 128:(m + 1) * 128, :], pst[:, i, :], reads=[("pst", i)], writes=[("partblk", b, m)])


def schedule_cc(P, part, red, b, ccops, delay=5):
    def go():
        ccops[b] = P.collective(part[b, :, :], red[b, :, :], reads=[("partblk", b, m) for m in range(DC)])
    if not hasattr(P, "pending_cc"):
        P.pending_cc = []
    P.pending_cc.append([delay, go])


def flush_cc(P):
    pend = getattr(P, "pending_cc", [])
    while pend:
        pend.pop(0)[1]()


def xsrc_view(xin, xres, b, from_input):
    if from_input:
        return xin.rearrange("(c p) t -> p c t", p=128)[:, :, b * TB:(b + 1) * TB]
    return _blk(xres, b)


def phase_allreduce(nc, part, red, NB, uid):
    G = Prog.G
    if G.get("cc_nc") is not nc:
        G["cc_nc"] = nc
        G["cc"] = nc.alloc_semaphore(name="cc_global")
        G["ccv"] = 0
    cc = G["cc"]
    with nc.Block() as blk:
        @blk.gpsimd
        def _(g):
            for b in range(NB):
                g.collective_compute("AllReduce", ALU.add, replica_groups=GROUPS, ins=[part[b, :, :]],
                                     outs=[red[b, :, :]]).then_inc(cc)
                G["ccv"] += 1
                g.wait_ge(cc, G["ccv"])


def phase_mlp(nc, S, xin, xres, part, red, from_input, has_bias, wdram, wmeta, wplan, gdram, ready, redo):
    NB = S // TB
    ph = Phase(nc, wdram, wmeta, gdram)
    P, B = ph.P, ph.B
    h1 = P.sb("h1", [128, FFC, TB], BF16)
    pst = P.sb("pst", [128, 2, TB], F32)

    def body(b):
        ph.prologue(b, xsrc_view(xin, xres, b, from_input), red, has_bias, _blk(xres, b), ready=ready)
        for m in range(FFC):
            def evac(ps, pkey, m=m):
                t2 = ph.tmp2[:, m % 2, :]
                P.add("act", lambda e: e.activation(out=t2, in_=ps[:, :], func=AF.Relu), reads=[pkey],
                      writes=[("tmp2", m % 2)])
                P.add("dve", lambda e: e.tensor_tensor(out=h1[:, m, :], in0=t2, in1=t2, op=ALU.mult),
                      reads=[("tmp2", m % 2)], writes=[("h1", m)])
            B.group(wplan["w1"][m], lambda kk: ph.h_sb[:, kk, :], lambda kk: ("h", kk), evac)
        for m in range(DC):
            B.group(wplan["w2"][m], lambda kk: h1[:, kk, :], lambda kk: ("h1", kk),
                    lambda ps, pkey, m=m: ph.store_partial(ps, pkey, part, b, m, pst, m))
        schedule_cc(P, part, redo, b, ccops)
    ccops = {}
    for b in range(NB):
        body(b)
    flush_cc(P)
    P.emit()
    return {b: op.ccval for b, op in ccops.items()}


def phase_wo(nc, S, o_s, part, wdram, wmeta, wplan, redo):
    NB = S // TB
    B = Bld(nc, wdram, wmeta, with_ss=False)
    P = B.P
    o_sb = P.sb("o_sb", [128, NHC, TB], BF16)
    pst = P.sb("pst", [128, 2, TB], F32)
    ov = o_s.rearrange("h p t -> p h t")

    def body(b):
        P.dma("sp", o_sb[:, :, :], ov[:, :, b * TB:(b + 1) * TB], writes=[("o", h) for h in range(NHC)])
        for m in range(DC):
            def evac(ps, pkey, m=m):
                i = m % 2
                P.add("act", lambda e: e.copy(out=pst[:, i, :], in_=ps[:, :]), reads=[pkey], writes=[("pst", i)])
                P.dma("sp", part[b, m * 128:(m + 1) * 128, :], pst[:, i, :], reads=[("pst", i)],
                      writes=[("partblk", b, m)])
            B.group(wplan["wo"][m], lambda kk: o_sb[:, kk, :], lambda kk: ("o", kk), evac)
        schedule_cc(P, part, redo, b, ccops)
    ccops = {}
    for b in range(NB):
        body(b)
    flush_cc(P)
    P.emit()
    return {b: op.ccval for b, op in ccops.items()}


def phase_final(nc, S, xres, red, outT, gdram, ready):
    NB = S // TB
    ph = Phase(nc, None, None, gdram)
    ov = outT.rearrange("(c p) t -> p c t", p=128)
    for b in range(NB):
        ph.prologue(b, _blk(xres, b), red, False, ov[:, :, b * TB:(b + 1) * TB], pre_norm=False, ready=ready)
    ph.P.emit()


def phase_mla_proj(nc, S, xin, xres, red, from_input, first, pos, rcd, wdram, wmeta, wplan, gdram, gqkv,
                   qn_o, qr_o, kn_o, kr_o, v_o, ready):
    NB = S // TB
    ph = Phase(nc, wdram, wmeta, gdram)
    P, B = ph.P, ph.B
    h_sb = ph.h_sb
    cqp = P.sb("cqp", [128, 4, TB], F32)
    ckvp = P.sb("ckvp", [128, 4, TB], F32)
    cqn = P.sb("cqn", [128, 4, TB], BF16)
    ckvn = P.sb("ckvn", [128, 4, TB], BF16)
    gq_sb = P.sb("gq_sb", [128, 8], F32)
    rc = P.sb("rc_sb", [64, 2], F32)
    pos_sb = P.sb("pos_sb", [64, TB], I32)
    B.rope_ws = P.sb("rope_ws", [64, 4, TB], F32)
    B.rope_wi = P.sb("rope_wi", [64, TB], I32)
    cos2 = P.sb("cos2", [64, TB], F32)
    sin2 = P.sb("sin2", [64, TB], F32)
    rt = P.sb("rt", [64, 2, 2, TB], F32)
    st_qn = P.sb("st_qn", [128, NHC, TB], BF16)
    st_qr = P.sb("st_qr", [64, NHC, TB], BF16)
    st_kn = P.sb("st_kn", [128, NHC, TB], BF16)
    st_kr = P.sb("st_kr", [64, TB], BF16)
    st_v = P.sb("st_v", [128, 4, NHC * 128], BF16)
    P.dma("sp", gq_sb[:, :], gqkv[:, :], writes=["gq"])
    P.dma("sp", rc[:, :], rcd[:, :], writes=["rc"])
    qn_v = qn_o.rearrange("h p t -> p h t")
    qr_v = qr_o.rearrange("h p t -> p h t")
    kn_v = kn_o.rearrange("h p t -> p h t")
    rti = [0]

    def rope_apply(ps_a, ka, ps_b, kb, out_ap, out_key):
        i = rti[0] % 2
        rti[0] += 1
        t1, t2 = rt[:, i, 0, :], rt[:, i, 1, :]
        P.add("dve", lambda e: e.tensor_tensor(out=t1, in0=ps_a[0:64, :], in1=cos2[:, :], op=ALU.mult),
              reads=[ka, "tables"], writes=[("rt", i, 0)])
        P.add("dve", lambda e: e.tensor_tensor(out=t2, in0=ps_b[0:64, :], in1=sin2[:, :], op=ALU.mult),
              reads=[kb, "tables"], writes=[("rt", i, 1)])
        P.add("dve", lambda e: e.tensor_tensor(out=out_ap, in0=t1, in1=t2, op=ALU.add),
              reads=[("rt", i, 0), ("rt", i, 1)], writes=[out_key])

    def pair_group(tiles_a, tiles_b, rhs_of_k, key_of_k, out_ap, out_key):
        hold = {}

        def ev_a(ps, pkey):
            hold["a"] = (ps, pkey)

        def ev_b(ps, pkey):
            pa, ka = hold["a"]
            rope_apply(pa, ka, ps, pkey, out_ap, out_key)
        B.group(tiles_a, rhs_of_k, key_of_k, ev_a, mw=64)
        B.group(tiles_b, rhs_of_k, key_of_k, ev_b, mw=64)

    def body(b):
        ts = slice(b * TB, (b + 1) * TB)
        ph.prologue(b, xsrc_view(xin, xres, b, from_input), None if first else red, False,
                    None if first else _blk(xres, b), ready=ready)
        P.dma("sp", pos_sb[:, :], pos[:, ts], writes=["pos"])
        emit_rope_tables(B, pos_sb[:, :], "pos", rc, TB, cos2[:, :], sin2[:, :], "tables")
        hk = lambda kk: h_sb[:, kk, :]
        hkey = lambda kk: ("h", kk)
        for m in range(4):
            B.group(wplan["dq"][m], hk, hkey, lambda ps, pkey, m=m: P.add(
                "act", lambda e: e.copy(out=cqp[:, m, :], in_=ps[:, :]), reads=[pkey], writes=[("cqp", m)]))
        for m in range(4):
            B.group(wplan["dkv"][m], hk, hkey, lambda ps, pkey, m=m: P.add(
                "act", lambda e: e.copy(out=ckvp[:, m, :], in_=ps[:, :]), reads=[pkey], writes=[("ckvp", m)]))
        pair_group(wplan["kr"][0], wplan["kr_sw"][0], hk, hkey, st_kr[:, :], "st_kr")
        P.dma("sp", kr_o[:, ts], st_kr[:, :], reads=["st_kr"])
        B.norm_stats(cqp, [("cqp", c) for c in range(4)], cqn, [("cqn", c) for c in range(4)], 4, 512)
        B.norm_apply(cqn, [("cqn", c) for c in range(4)], cqp, [("cqp", c) for c in range(4)],
                     lambda c: gq_sb[:, c:c + 1], "gq", 4)
        B.norm_stats(ckvp, [("ckvp", c) for c in range(4)], ckvn, [("ckvn", c) for c in range(4)], 4, 512)
        B.norm_apply(ckvn, [("ckvn", c) for c in range(4)], ckvp, [("ckvp", c) for c in range(4)],
                     lambda c: gq_sb[:, 4 + c:5 + c], "gq", 4)
        qk = lambda kk: cqn[:, kk, :]
        qkey = lambda kk: ("cqn", kk)
        kk_ = lambda kk: ckvn[:, kk, :]
        kkey = lambda kk: ("ckvn", kk)
        for h in range(NHC):
            B.group(wplan["uq_n"][h], qk, qkey, lambda ps, pkey, h=h: P.add(
                "act", lambda e: e.copy(out=st_qn[:, h, :], in_=ps[:, :]), reads=[pkey], writes=[("st_qn", h)]))
        P.dma("sp", qn_v[:, :, ts], st_qn[:, :, :], reads=[("st_qn", h) for h in range(NHC)])
        for h in range(NHC):
            pair_group(wplan["uq_r"][h], wplan["uq_rs"][h], qk, qkey, st_qr[:, h, :], ("st_qr", h))
        P.dma("sp", qr_v[:, :, ts], st_qr[:, :, :], reads=[("st_qr", h) for h in range(NHC)])
        for h in range(NHC):
            B.group(wplan["uk"][h], kk_, kkey, lambda ps, pkey, h=h: P.add(
                "act", lambda e: e.copy(out=st_kn[:, h, :], in_=ps[:, :]), reads=[pkey], writes=[("st_kn", h)]))
        P.dma("sp", kn_v[:, :, ts], st_kn[:, :, :], reads=[("st_kn", h) for h in range(NHC)])
        wap, wkey = B.ring.fetch(wplan["uv"])
        for tt in range(4):
            ps, pkey = B.next_ps()
            for k in range(4):
                P.add("pe", lambda e, ps=ps, k=k, tt=tt: e.matmul(
                    ps[:, :], ckvn[:, k, tt * 128:(tt + 1) * 128], wap[:, k * 512:(k + 1) * 512],
                    start=(k == 0), stop=(k == 3)), reads=[wkey, ("ckvn", k)], writes=[pkey])
            P.add("act", lambda e, ps=ps, tt=tt: e.copy(out=st_v[:, tt, :], in_=ps[:, :]),
                  reads=[pkey], writes=[("st_v", tt)])
        for h in range(NHC):
            P.dma("sp", v_o[h, :, b * 4:(b + 1) * 4, :], st_v[:, :, h * 128:(h + 1) * 128],
                  reads=[("st_v", tt) for tt in range(4)])
    for b in range(NB):
        body(b)
    P.emit()


def prep_mla_core(r, w_dq, w_dkv, w_uq, w_ukv, w_o):
    ws = WStream()
    plan = {}
    plan["dq"] = ws.add_matrix(w_dq)
    plan["dkv"] = ws.add_matrix(w_dkv[:, :512])
    kr = w_dkv[:, 512:576]
    plan["kr"] = ws.add_matrix(kr, mw=64)
    plan["kr_sw"] = ws.add_matrix(np.concatenate([kr[:, 32:], kr[:, :32]], axis=1), mw=64)
    uq = w_uq.reshape(512, HEADS, 192)[:, r * NHC:(r + 1) * NHC, :]
    plan["uq_n"] = ws.add_matrix(np.ascontiguousarray(uq[:, :, :128]).reshape(512, NHC * 128))
    plan["uq_r"] = []
    plan["uq_rs"] = []
    for h in range(NHC):
        rr = uq[:, h, 128:192]
        plan["uq_r"].append(ws.add_matrix(rr, mw=64)[0])
        plan["uq_rs"].append(ws.add_matrix(np.concatenate([rr[:, 32:], rr[:, :32]], axis=1), mw=64)[0])
    ukv = w_ukv.reshape(512, HEADS, 256)[:, r * NHC:(r + 1) * NHC, :]
    plan["uk"] = ws.add_matrix(np.ascontiguousarray(ukv[:, :, :128]).reshape(512, NHC * 128))
    wv = np.ascontiguousarray(ukv[:, :, 128:]).reshape(512, NHC * 128)
    plan["uv"] = ws.add_raw(wv.reshape(4, 128, 512).transpose(1, 0, 2).reshape(128, 2048))
    wso = WStream()
    plano = {"wo": wso.add_matrix(np.ascontiguousarray(w_o[r * NHC * 128:(r + 1) * NHC * 128, :]))}
    return ws.array(), ws.meta, plan, wso.array(), wso.meta, plano


def phase_lru(nc, S, xin, xres, part, red, from_input, wdram, wmeta, wplan, gdram, wgd, vecd, ready, redo):
    NB = S // TB
    ph = Phase(nc, wdram, wmeta, gdram, nslots=4)
    P, B = ph.P, ph.B
    h_sb = ph.h_sb
    HJ = 4
    wg_sb = P.sb("wg_sb", [128, 2, LJ, 2, CH], BF16)
    vec = P.sb("vec_sb", [128, LJ, 10], F32)
    cc = P.sb("cc", [128, 4, LJ], F32)
    yl = P.sb("yl", [128, HJ, TB], F32)
    uxb = P.sb("uxb", [128, HJ, TB + 4], F32)
    u_sb = P.sb("u_sb", [128, HJ, TB], F32)
    u_bf = P.sb("u_bf", [128, HJ, TB], BF16)
    r_sb = P.sb("r_sb", [128, HJ, TB], F32)
    ig_sb = P.sb("ig_sb", [128, HJ, TB], F32)
    a_sb = P.sb("a_sb", [128, HJ, TB], F32)
    m_sb = P.sb("m_sb", [128, HJ, TB], F32)
    hs_sb = P.sb("hs_sb", [128, HJ, TB], F32)
    ost = P.sb("ost", [128, LJ, TB], BF16)
    pst = P.sb("pst", [128, 2, TB], F32)
    state = P.sb("state", [128, LJ], F32)
    halo = P.sb("halo", [128, LJ, 4], F32)
    P.dma("sp", vec[0:CH, :, :], vecd[:, :, :], writes=["vec"])
    P.dma("pool", wg_sb[0:CH, :, :, :, :].rearrange("p a b c d -> p (a b c d)"), wgd[:, :], writes=["wg"])
    e_ = cc[0:CH, 0, :]
    t_ = cc[0:CH, 1, :]
    P.add("act", lambda e: e.activation(out=e_, in_=vec[0:CH, :, 5], func=AF.Exp, scale=-1.0), reads=["vec"],
          writes=["cc"])
    coef = [1.0 / 5, -1.0 / 4, 1.0 / 3, -1.0 / 2, 1.0]
    P.add("dve", lambda e: e.tensor_scalar(out=t_, in0=e_, scalar1=-1.0 / 6, scalar2=coef[0], op0=ALU.mult,
                                           op1=ALU.add), reads=["cc"], writes=["cc"])
    for cf in coef[1:]:
        P.add("dve", lambda e: e.tensor_tensor(out=t_, in0=t_, in1=e_, op=ALU.mult), reads=["cc"], writes=["cc"])
        P.add("dve", lambda e, cf=cf: e.tensor_scalar(out=t_, in0=t_, scalar1=cf, scalar2=None, op0=ALU.add),
              reads=["cc"], writes=["cc"])
    P.add("dve", lambda e: e.tensor_tensor(out=t_, in0=t_, in1=e_, op=ALU.mult), reads=["cc"], writes=["cc"])
    P.add("dve", lambda e: e.tensor_scalar(out=cc[0:CH, 2, :], in0=t_, scalar1=-8.0, scalar2=None, op0=ALU.mult),
          reads=["cc"], writes=["cc"])
    P.add("dve", lambda e: e.tensor_scalar(out=cc[0:CH, 3, :], in0=t_, scalar1=-16.0, scalar2=None, op0=ALU.mult),
          reads=["cc"], writes=["cc"])
    P.add("dve", lambda e: e.memset(state[:, :], 0.0), writes=[("state", j) for j in range(LJ)])
    P.add("dve", lambda e: e.memset(halo[:, :, :], 0.0), writes=[("halo", j) for j in range(LJ)])
    hk = lambda kk: h_sb[:, kk, :]
    hkey = lambda kk: ("h", kk)

    def half(b, hf):
        J0 = hf * HJ
        for j in range(HJ):
            J = J0 + j
            B.group(wplan["wy"][J], hk, hkey, lambda ps, pkey, j=j, J=J: P.add(
                "act", lambda e: e.activation(out=yl[0:CH, j, :], in_=ps[0:CH, :], func=AF.Gelu,
                                              bias=vec[0:CH, J, 0:1]),
                reads=[pkey, "vec"], writes=[("yl", j)]), mw=CH)
        for j in range(HJ):
            J = J0 + j
            B.group(wplan["wx"][J], hk, hkey, lambda ps, pkey, j=j, J=J: P.add(
                "act", lambda e: e.activation(out=uxb[0:CH, j, 3:3 + TB], in_=ps[0:CH, :], func=AF.Identity,
                                              bias=vec[0:CH, J, 1:2]),
                reads=[pkey, "vec"], writes=[("ux", j)]), mw=CH)
        for j in range(HJ):
            J = J0 + j
            P.add("dve", lambda e, j=j, J=J: e.tensor_copy(out=uxb[0:CH, j, 0:3], in_=halo[0:CH, J, 0:3]),
                  reads=[("halo", J), ("ux", j)], writes=[("ux", j)])
            P.add("dve", lambda e, j=j, J=J: e.tensor_scalar(out=u_sb[0:CH, j, :], in0=uxb[0:CH, j, 0:TB],
                                                             scalar1=vec[0:CH, J, 6:7], scalar2=vec[0:CH, J, 2:3],
                                                             op0=ALU.mult, op1=ALU.add),
                  reads=[("ux", j), "vec"], writes=[("u", j)])
            for i in range(1, 4):
                P.add("dve", lambda e, j=j, J=J, i=i: e.scalar_tensor_tensor(
                    out=u_sb[0:CH, j, :], in0=uxb[0:CH, j, i:i + TB], scalar=vec[0:CH, J, 6 + i:7 + i],
                    in1=u_sb[0:CH, j, :], op0=ALU.mult, op1=ALU.add),
                    reads=[("ux", j), "vec", ("u", j)], writes=[("u", j)])
            P.add("dve", lambda e, j=j, J=J: e.tensor_copy(out=halo[0:CH, J, 0:3], in_=uxb[0:CH, j, TB:TB + 3]),
                  reads=[("ux", j)], writes=[("halo", J)])
            P.add("act", lambda e, j=j: e.copy(out=u_bf[0:CH, j, :], in_=u_sb[0:CH, j, :]),
                  reads=[("u", j)], writes=[("ubf", j)])
        for gi, dst, bidx in ((0, r_sb, 3), (1, ig_sb, 4)):
            for j in range(HJ):
                J = J0 + j
                ps, pkey = B.next_ps()
                for i in range(2):
                    P.add("pe", lambda e, ps=ps, gi=gi, j=j, J=J, i=i: e.matmul(
                        ps[0:CH, :], wg_sb[0:CH, gi, J, i, :], u_bf[0:CH, 2 * (j // 2) + i, :], start=(i == 0),
                        stop=(i == 1)), reads=["wg", ("ubf", 2 * (j // 2) + i)], writes=[pkey])
                P.add("act", lambda e, ps=ps, j=j, J=J, dst=dst, bidx=bidx: e.activation(
                    out=dst[0:CH, j, :], in_=ps[0:CH, :], func=AF.Sigmoid, bias=vec[0:CH, J, bidx:bidx + 1]),
                    reads=[pkey, "vec"], writes=[("gate", gi, j)])
        for j in range(HJ):
            J = J0 + j
            P.add("act", lambda e, j=j, J=J: e.activation(out=a_sb[0:CH, j, :], in_=r_sb[0:CH, j, :], func=AF.Exp,
                                                          scale=cc[0:CH, 2, J:J + 1]),
                  reads=[("gate", 0, j), "cc"], writes=[("a", j)])
            P.add("act", lambda e, j=j, J=J: e.activation(out=m_sb[0:CH, j, :], in_=r_sb[0:CH, j, :], func=AF.Exp,
                                                          scale=cc[0:CH, 3, J:J + 1]),
                  reads=[("gate", 0, j), "cc"], writes=[("m", j)])
        for j in range(HJ):
            P.add("dve", lambda e, j=j: e.tensor_scalar(out=m_sb[0:CH, j, :], in0=m_sb[0:CH, j, :], scalar1=-1.0,
                                                        scalar2=1.0, op0=ALU.mult, op1=ALU.add),
                  reads=[("m", j)], writes=[("m", j)])
        for j in range(HJ):
            P.add("act", lambda e, j=j: e.activation(out=m_sb[0:CH, j, :], in_=m_sb[0:CH, j, :], func=AF.Sqrt),
                  reads=[("m", j)], writes=[("m", j)])
        for j in range(HJ):
            J = J0 + j
            P.add("dve", lambda e, j=j: e.tensor_tensor(out=ig_sb[0:CH, j, :], in0=ig_sb[0:CH, j, :],
                                                        in1=u_sb[0:CH, j, :], op=ALU.mult),
                  reads=[("gate", 1, j), ("u", j)], writes=[("gate", 1, j)])
            P.add("dve", lambda e, j=j: e.tensor_tensor(out=ig_sb[0:CH, j, :], in0=ig_sb[0:CH, j, :],
                                                        in1=m_sb[0:CH, j, :], op=ALU.mult),
                  reads=[("gate", 1, j), ("m", j)], writes=[("gate", 1, j)])
            P.add("dve", lambda e, j=j, J=J: e.tensor_tensor_scan(
                out=hs_sb[0:CH, j, :], data0=a_sb[0:CH, j, :], data1=ig_sb[0:CH, j, :],
                initial=state[0:CH, J:J + 1], op0=ALU.mult, op1=ALU.add),
                reads=[("a", j), ("gate", 1, j), ("state", J)], writes=[("hs", j)])
            P.add("dve", lambda e, j=j, J=J: e.tensor_copy(out=state[0:CH, J:J + 1], in_=hs_sb[0:CH, j, TB - 1:TB]),
                  reads=[("hs", j)], writes=[("state", J)])
            P.add("dve", lambda e, j=j, J=J: e.tensor_tensor(out=ost[0:CH, J, :], in0=hs_sb[0:CH, j, :],
                                                             in1=yl[0:CH, j, :], op=ALU.mult),
                  reads=[("hs", j), ("yl", j)], writes=[("ost", J)])

    def body(b):
        ph.prologue(b, xsrc_view(xin, xres, b, from_input), red, False, _blk(xres, b), ready=ready)
        half(b, 0)
        half(b, 1)
        for m in range(DC):
            ps, pkey = B.next_ps()
            wap, wkey = B.ring.fetch(wplan["wout"][m])
            for J in range(LJ):
                P.add("pe", lambda e, ps=ps, J=J, wap=wap: e.matmul(
                    ps[:, :], wap[0:CH, J * 128:(J + 1) * 128], ost[0:CH, J, :], start=(J == 0), stop=(J == LJ - 1)),
                    reads=[wkey, ("ost", J)], writes=[pkey])
            ph.store_partial(ps, pkey, part, b, m, pst, m)
        schedule_cc(P, part, redo, b, ccops)
    ccops = {}
    for b in range(NB):
        body(b)
    flush_cc(P)
    P.emit()
    return {b: op.ccval for b, op in ccops.items()}


def prep_lru_core(r, w_y, b_y, w_x, b_x, conv_w, conv_b, w_ga, b_ga, w_gi, b_gi, lam, w_out):
    c0 = r * LJ * CH
    c1 = c0 + LJ * CH
    ws = WStream()
    plan = {"wy": ws.add_matrix(np.ascontiguousarray(w_y[:, c0:c1]), mw=CH),
            "wx": ws.add_matrix(np.ascontiguousarray(w_x[:, c0:c1]), mw=CH), "wout": []}
    wo = w_out[c0:c1, :].reshape(LJ, CH, DC, 128)
    for m in range(DC):
        t = np.zeros((128, LJ, 128), np.float32)
        t[:CH] = wo[:, :, m, :].transpose(1, 0, 2)
        plan["wout"].append(ws.add_raw(t.reshape(128, LJ * 128)))
    wg = np.zeros((CH, 2, LJ, 2, CH), np.float32)
    for gi, w in enumerate((w_ga, w_gi)):
        for J in range(LJ):
            n = (LJ // 2) * r + J // 2
            for i in range(2):
                wg[:, gi, J, i, :] = w[n, i * CH:(i + 1) * CH, (J % 2) * CH:(J % 2 + 1) * CH]
    vec = np.zeros((CH, LJ, 10), np.float32)
    for idx, vv in enumerate((b_y, b_x, conv_b, b_ga, b_gi, lam)):
        vec[:, :, idx] = vv[c0:c1].reshape(LJ, CH).T
    for i in range(4):
        vec[:, :, 6 + i] = conv_w[i, c0:c1].reshape(LJ, CH).T
    return ws.array(), ws.meta, plan, np.ascontiguousarray(wg.reshape(CH, -1)), vec


def prep_mlp_core(r, w1, w2):
    ws = WStream()
    n = DFF // TPG
    plan = {"w1": ws.add_matrix(np.ascontiguousarray(w1[:, r * n:(r + 1) * n])),
            "w2": ws.add_matrix(np.ascontiguousarray(w2[r * n:(r + 1) * n, :]))}
    return ws.array(), ws.meta, plan


def g3(gpost, gpre, bias):
    z = np.zeros((128, DC), np.float32)
    return np.ascontiguousarray(np.stack([gvec(gpost) if gpost is not None else z,
                                          gvec(gpre) if gpre is not None else z,
                                          gvec(bias) if bias is not None else z], axis=1))


def build_fused(S, depth, metas):
    nc = bass.Bass("TRN2", target_bir_lowering=False)
    NB = S // TB

    def din(name, shape, dt=F32):
        return nc.dram_tensor(name, list(shape), dt, kind="ExternalInput").ap()

    def wsz(meta):
        return meta[-1][0] + meta[-1][1]
    xin = din("xT", [D, S])
    pos = din("pos", [64, S], I32)
    rcd = din("rc", [64, 2])
    msk = din("masks", [128, 4, 512], BF16)
    outT = nc.dram_tensor("outT", [D, S], F32, kind="ExternalOutput").ap()
    xres = nc.dram_tensor("xres", [NB, D, TB], F32)
    parts = [nc.dram_tensor(f"part{i}", [NB, D, TB], F32) for i in range(2)]
    reds = [nc.dram_tensor(f"red{i}", [NB, D, TB], F32) for i in range(2)]
    qn_s = nc.dram_tensor("qn_s", [NHC, 128, S], BF16).ap()
    qr_s = nc.dram_tensor("qr_s", [NHC, 64, S], BF16).ap()
    kn_s = nc.dram_tensor("kn_s", [NHC, 128, S], BF16).ap()
    kr_s = nc.dram_tensor("kr_s", [64, S], BF16).ap()
    v_s = nc.dram_tensor("v_s", [NHC, 128, S // 128, 128], BF16).ap()
    o_s = nc.dram_tensor("o_s", [NHC, 128, S], BF16).ap()
    first = True
    from_input = True
    k = 0
    ready = None
    prev_bias = False
    for i in range(depth):
        j = i // 2
        if i % 2 == 0:
            wm, wp, wmo, wpo = metas[f"mla{j}"]
            phase_mla_proj(nc, S, xin, xres, reds[(k - 1) % 2], from_input, first, pos, rcd,
                           din(f"w_mla{j}", [128, wsz(wm)]), wm, wp, din(f"g_mix{i}", [128, 3, DC]),
                           din(f"gq{j}", [128, 8]), qn_s, qr_s, kn_s, kr_s, v_s, ready)
            emit_attn(nc, S, NHC, qn_s, qr_s, kn_s, kr_s, v_s, msk, o_s)
            ready = phase_wo(nc, S, o_s, parts[k % 2], din(f"w_o{j}", [128, wsz(wmo)]), wmo, wpo, reds[k % 2])
            prev_bias = False
        else:
            wm, wp = metas[f"lru{j}"]
            ready = phase_lru(nc, S, xin, xres, parts[k % 2], reds[(k - 1) % 2], from_input,
                              din(f"w_lru{j}", [128, wsz(wm)]), wm, wp, din(f"g_mix{i}", [128, 3, DC]),
                              din(f"wg{j}", [CH, 2 * LJ * 2 * CH]), din(f"vec{j}", [CH, LJ, 10]), ready, reds[k % 2])
            prev_bias = True
        k += 1
        wm, wp = metas[f"mlp{i}"]
        ready = phase_mlp(nc, S, xin, xres, parts[k % 2], reds[(k - 1) % 2], from_input, prev_bias,
                          din(f"w_mlp{i}", [128, wsz(wm)]), wm, wp, din(f"g_mlp{i}", [128, 3, DC]), ready,
                          reds[k % 2])
        k += 1
        first = False
        from_input = False
    phase_final(nc, S, xres, reds[(k - 1) % 2], outT, din("g_fin", [128, 3, DC]), ready)
    return nc


def kernel(x, positions, mix_pre_g, mix_post_g, mlp_pre_g, mlp_post_g,
           mla_w_dq, mla_g_q, mla_w_uq, mla_w_dkv, mla_g_kv, mla_w_ukv, mla_w_o,
           lru_w_y, lru_b_y, lru_w_x, lru_b_x, lru_conv_w, lru_conv_b,
           lru_w_ga, lru_b_ga, lru_w_gi, lru_b_gi, lru_lam, lru_w_out, lru_b_out,
           mlp_w1, mlp_w2):
    f32 = lambda a: np.asarray(a, np.float32)
    x = f32(x)
    positions = np.asarray(positions, np.int32)
    NBATCH, S, _ = x.shape
    assert NBATCH * TPG == NCORES
    depth = mix_pre_g.shape[0]
    rc = rope_consts()
    masks = attn_masks()
    maps = [dict() for _ in range(NCORES)]
    metas = {}
    xTb = [np.ascontiguousarray(x[b].T) for b in range(NBATCH)]
    posb = [np.ascontiguousarray(np.broadcast_to(positions[b][None, :], (64, S))) for b in range(NBATCH)]
    for c in range(NCORES):
        b, r = divmod(c, TPG)
        m = maps[c]
        m["xT"] = xTb[b]
        m["pos"] = posb[b]
        m["rc"] = rc
        m["masks"] = masks
        for i in range(depth):
            j = i // 2
            if i % 2 == 0:
                wa, wm, wp, wao, wmo, wpo = prep_mla_core(r, f32(mla_w_dq[j]), f32(mla_w_dkv[j]), f32(mla_w_uq[j]),
                                                          f32(mla_w_ukv[j]), f32(mla_w_o[j]))
                metas[f"mla{j}"] = (wm, wp, wmo, wpo)
                m[f"w_mla{j}"] = wa
                m[f"w_o{j}"] = wao
                m[f"gq{j}"] = np.ascontiguousarray(np.concatenate([gvec(mla_g_q[j]), gvec(mla_g_kv[j])], axis=1))
                m[f"g_mix{i}"] = g3(mlp_post_g[i - 1] if i > 0 else None, mix_pre_g[i], None)
                m[f"g_mlp{i}"] = g3(mix_post_g[i], mlp_pre_g[i], None)
            else:
                wa, wm, wp, wg, vec = prep_lru_core(
                    r, f32(lru_w_y[j]), f32(lru_b_y[j]), f32(lru_w_x[j]), f32(lru_b_x[j]), f32(lru_conv_w[j]),
                    f32(lru_conv_b[j]), f32(lru_w_ga[j]), f32(lru_b_ga[j]), f32(lru_w_gi[j]), f32(lru_b_gi[j]),
                    f32(lru_lam[j]), f32(lru_w_out[j]))
                metas[f"lru{j}"] = (wm, wp)
                m[f"w_lru{j}"] = wa
                m[f"wg{j}"] = wg
                m[f"vec{j}"] = vec
                m[f"g_mix{i}"] = g3(mlp_post_g[i - 1], mix_pre_g[i], None)
                m[f"g_mlp{i}"] = g3(mix_post_g[i], mlp_pre_g[i], lru_b_out[j])
            wa, wm, wp = prep_mlp_core(r, f32(mlp_w1[i]), f32(mlp_w2[i]))
            metas[f"mlp{i}"] = (wm, wp)
            m[f"w_mlp{i}"] = wa
        m["g_fin"] = g3(mlp_post_g[depth - 1], None, None)
    nc = build_fused(S, depth, metas)
    res = run_bass_kernel_spmd(nc, maps, core_ids=list(range(NCORES))).results
    out = np.stack([np.asarray(res[b * TPG]["outT"]).T for b in range(NBATCH)], axis=0)
    return np.ascontiguousarray(out.astype(np.float32))
```

```python
import contextlib
import numpy as np
import ml_dtypes
import concourse.bass as bass
import concourse.mybir as mybir
from concourse.bass_utils import run_bass_kernel_spmd

F32 = mybir.dt.float32
BF16 = mybir.dt.bfloat16
I32 = mybir.dt.int32
ALU = mybir.AluOpType
AF = mybir.ActivationFunctionType
NPBF16 = ml_dtypes.bfloat16

D = 2048
DC = D // 128
DFF = 8192
FC = DFF // 128
EPS = 1e-6
TB = 512
NCORES = 8
HEADS = 16
DRNN = 2688
RC = DRNN // 128


class Op:
    __slots__ = ("eng", "fn", "deps", "idx", "signal", "cnt", "is_dma", "dsem", "dval", "gid", "ext", "is_cc", "ccval")


class Prog:
    ENGS = ["pe", "act", "dve", "pool", "sp"]
    EPOCH = 12000
    NDMA_SEM = 8
    CCWIN = 33

    _uid = [0]
    G = {}

    def __init__(self, nc):
        self.nc = nc
        Prog._uid[0] += 1
        self.pfx = f"p{Prog._uid[0]}_"
        self.ops = {e: [] for e in self.ENGS}
        self.last_w = {}
        self.readers = {}
        self.stack = contextlib.ExitStack()
        self.ngid = 0

    def sb(self, name, shape, dt):
        return self.stack.enter_context(self.nc.sbuf_tensor(self.pfx + name, list(shape), dt))

    def ps(self, name, shape=(128, 512), dt=F32):
        return self.stack.enter_context(self.nc.psum_tensor(self.pfx + name, list(shape), dt))

    def add(self, eng, fn, reads=(), writes=(), dma=False, ext=None, cc=False):
        op = Op()
        op.ext = ext
        op.is_cc = cc
        op.ccval = None
        op.eng = eng
        op.fn = fn
        op.is_dma = dma
        op.signal = False
        op.gid = self.ngid
        self.ngid += 1
        deps = {}
        for k in reads:
            w = self.last_w.get(k)
            if w is not None:
                deps[w.gid] = w
        for k in writes:
            w = self.last_w.get(k)
            if w is not None:
                deps[w.gid] = w
            for r in self.readers.get(k, ()):
                deps[r.gid] = r
        for k in reads:
            self.readers.setdefault(k, []).append(op)
        for k in writes:
            self.last_w[k] = op
            self.readers[k] = []
        op.deps = list(deps.values())
        op.idx = len(self.ops[eng])
        self.ops[eng].append(op)
        return op

    def dma(self, eng, out, in_, reads=(), writes=(), ext=None):
        return self.add(eng, lambda e: e.dma_start(out=out, in_=in_), reads, writes, dma=True, ext=ext)

    def collective(self, in_ap, out_ap, reads=()):
        return self.add("pool", lambda e: e.collective_compute("AllReduce", ALU.add, replica_groups=GROUPS,
                                                               ins=[in_ap], outs=[out_ap]), reads, (), cc=True)

    def emit(self):
        nc = self.nc
        for e in self.ENGS:
            for op in self.ops[e]:
                best = {}
                dd = []
                for d in op.deps:
                    if d.is_dma:
                        dd.append(d)
                        continue
                    if d.eng == op.eng and op.eng == "pe" and not op.is_dma:
                        continue
                    b = best.get(d.eng)
                    if b is None or d.idx > b.idx:
                        best[d.eng] = d
                for d in best.values():
                    d.signal = True
                    dd.append(d)
                op.deps = dd
        G = Prog.G
        if G.get("nc") is not nc:
            G.clear()
            G["nc"] = nc
            G["sems"] = {e: [] for e in self.ENGS}
            G["cnt"] = {e: 0 for e in self.ENGS}
            G["dsets"] = {e: [] for e in self.ENGS}
            G["dvals"] = {e: None for e in self.ENGS}
            G["dj"] = {e: 0 for e in self.ENGS}
            G["nalloc"] = 0

        def new_sem(name):
            G["nalloc"] += 1
            return nc.alloc_semaphore(name=f"g{G['nalloc']}_{name}")
        base = dict(G["cnt"])
        if "cc" not in G:
            G["cc"] = nc.alloc_semaphore(name="cc_global")
            G["ccv"] = 0
        for op in self.ops["pool"]:
            if op.is_cc:
                G["ccv"] += 1
                op.ccval = G["ccv"]
        ccsem = G["cc"]
        for e in self.ENGS:
            c = G["cnt"][e]
            for op in self.ops[e]:
                if op.signal and not op.is_dma:
                    c += 1
                    op.cnt = c
            G["cnt"][e] = c
            need = (c + self.EPOCH - 1) // self.EPOCH
            while len(G["sems"][e]) < max(1, need):
                G["sems"][e].append(new_sem(f"s_{e}"))
        sems = G["sems"]
        DLIM = 30000
        for e in self.ENGS:
            for op in self.ops[e]:
                if op.is_dma:
                    if G["dvals"][e] is None or max(G["dvals"][e]) + 16 > DLIM:
                        G["dsets"][e].append([new_sem(f"d_{e}") for _ in range(self.NDMA_SEM)])
                        G["dvals"][e] = [0] * self.NDMA_SEM
                    sidx = G["dj"][e] % self.NDMA_SEM
                    G["dj"][e] += 1
                    G["dvals"][e][sidx] += 16
                    op.dsem = G["dsets"][e][-1][sidx]
                    op.dval = G["dvals"][e][sidx]
        EP = self.EPOCH
        self.stats = {}

        def run(eng_name, eng):
            known = dict(base)
            known_dma = set()
            nw = 0
            for op in self.ops[eng_name]:
                for d in op.deps:
                    if d.is_dma:
                        if d.gid in known_dma:
                            continue
                        eng.wait_ge(d.dsem, d.dval)
                        known_dma.add(d.gid)
                        nw += 1
                    else:
                        if d.cnt <= known[d.eng]:
                            continue
                        ep = (d.cnt - 1) // EP
                        eng.wait_ge(sems[d.eng][ep], d.cnt - ep * EP)
                        known[d.eng] = d.cnt
                        nw += 1
                if op.ext is not None:
                    eng.wait_ge(ccsem, op.ext)
                if op.is_cc:
                    if op.ccval > self.CCWIN:
                        eng.wait_ge(ccsem, op.ccval - self.CCWIN)
                    op.fn(eng).then_inc(ccsem)
                elif op.is_dma:
                    if op.dval > 16:
                        eng.wait_ge(op.dsem, op.dval - 16)
                    op.fn(eng).then_inc(op.dsem, 16)
                else:
                    ins = op.fn(eng)
                    if op.signal:
                        ep = (op.cnt - 1) // EP
                        ins.then_inc(sems[eng_name][ep], 1)
            last = {}
            for op in self.ops[eng_name]:
                if op.is_dma:
                    last[id(op.dsem)] = op
            for op in last.values():
                eng.wait_ge(op.dsem, op.dval)
            self.stats[eng_name] = (len(self.ops[eng_name]), nw)

        with nc.Block() as block:
            @block.tensor
            def _(pe):
                run("pe", pe)

            @block.scalar
            def _(act):
                run("act", act)

            @block.vector
            def _(dve):
                run("dve", dve)

            @block.gpsimd
            def _(pool):
                run("pool", pool)

            @block.sync
            def _(sp):
                run("sp", sp)
        self.stack.close()


class WStream:
    def __init__(self):
        self.tiles = []
        self.meta = []
        self.off = 0

    def add_matrix(self, w, mw=128):
        K, M = w.shape
        kc = K // 128
        kmax = 2048 // mw
        out = []
        wt = w.reshape(kc, 128, M // mw, mw)
        for m in range(M // mw):
            lst = []
            for k0 in range(0, kc, kmax):
                nk = min(kmax, kc - k0)
                t = np.ascontiguousarray(wt[k0:k0 + nk, :, m, :].transpose(1, 0, 2))
                lst.append((len(self.tiles), nk))
                self.tiles.append(t.reshape(128, nk * mw))
                self.meta.append((self.off, nk * mw))
                self.off += nk * mw
            out.append(lst)
        return out

    def add_raw(self, t):
        n = t.shape[1]
        tid = len(self.tiles)
        self.tiles.append(np.ascontiguousarray(t))
        self.meta.append((self.off, n))
        self.off += n
        return tid

    def array(self):
        return np.ascontiguousarray(np.concatenate(self.tiles, axis=1).astype(np.float32))


class WRing:
    CONV = 4096

    def __init__(self, P, wdram, meta, nslots=6):
        self.P = P
        self.meta = meta
        self.nslots = nslots
        self.buf = P.sb("wring", [128, nslots, 16 * 128], BF16)
        self.next = 0
        total = meta[-1][0] + meta[-1][1]
        self.wbf = P.nc.dram_tensor(P.pfx + "wbf", [128, total], BF16).ap()
        for ci, c0 in enumerate(range(0, total, self.CONV)):
            c1 = min(total, c0 + self.CONV)
            P.dma("pool", self.wbf[:, c0:c1], wdram[:, c0:c1], writes=[("wconv", ci)])

    def fetch(self, tid):
        slot = self.next % self.nslots
        self.next += 1
        off, ne = self.meta[tid]
        key = ("w", slot)
        rk = [("wconv", ci) for ci in range(off // self.CONV, (off + ne - 1) // self.CONV + 1)]
        self.P.dma("pool", self.buf[:, slot, 0:ne], self.wbf[:, off:off + ne], reads=rk, writes=[key])
        pend = getattr(self.P, "pending_cc", None)
        if pend:
            for it in pend:
                it[0] -= 1
            while pend and pend[0][0] <= 0:
                pend.pop(0)[1]()
        return self.buf[:, slot, :], key


def gvec(g):
    g = np.asarray(g, np.float32)
    return np.ascontiguousarray(g.reshape(-1, 128).T)


class Ctx:
    pass


def emit_rmsnorm_stats(P, C, src, src_keys, sq, sq_keys, nch, ps_ss, ps_key, rstd, rstd_key, tmp, tmp_key,
                       dim, ntok=TB, sq_eng="act"):
    for c in range(nch):
        if sq_eng == "act":
            P.add("act", lambda e, c=c: e.activation(out=sq[:, c, :ntok], in_=src[:, c, :ntok], func=AF.Square),
                  reads=[src_keys[c]], writes=[sq_keys[c]])
        else:
            P.add(sq_eng, lambda e, c=c: e.tensor_tensor(out=sq[:, c, :ntok], in0=src[:, c, :ntok],
                                                         in1=src[:, c, :ntok], op=ALU.mult),
                  reads=[src_keys[c]], writes=[sq_keys[c]])
    for c in range(nch):
        P.add("pe", lambda e, c=c: e.matmul(ps_ss[:, :ntok], C.ones[:, :], sq[:, c, :ntok], start=(c == 0),
                                            stop=(c == nch - 1)),
              reads=[sq_keys[c], "ones"], writes=[ps_key])
    P.add("act", lambda e: e.activation(out=tmp[:, :ntok], in_=ps_ss[:, :ntok], func=AF.Sqrt, scale=1.0 / dim,
                                        bias=C.eps[:, 0:1]),
          reads=[ps_key, "eps"], writes=[tmp_key])
    P.add("dve", lambda e: e.reciprocal(out=rstd[:, :ntok], in_=tmp[:, :ntok]),
          reads=[tmp_key], writes=[rstd_key])


def build_post_mlp(T, KIN, has_bias, wmeta, wplan):
    nc = bass.Bass("TRN2", target_bir_lowering=False)
    KC = KIN // 128
    NB = T // TB
    xT = nc.dram_tensor("xT", [D, T], F32, kind="ExternalInput").ap()
    mT = nc.dram_tensor("mT", [KIN, T], BF16, kind="ExternalInput").ap()
    wdram = nc.dram_tensor("wstream", [128, wmeta[-1][0] + wmeta[-1][1]], F32, kind="ExternalInput").ap()
    gpost = nc.dram_tensor("gpost", [128, DC], F32, kind="ExternalInput").ap()
    gpre = nc.dram_tensor("gpre", [128, DC], F32, kind="ExternalInput").ap()
    gpost2 = nc.dram_tensor("gpost2", [128, DC], F32, kind="ExternalInput").ap()
    bout = nc.dram_tensor("bout", [128, DC], F32, kind="ExternalInput").ap()
    oT = nc.dram_tensor("oT", [D, T], F32, kind="ExternalOutput").ap()
    xTv = xT.rearrange("(c p) t -> p c t", p=128)
    oTv = oT.rearrange("(c p) t -> p c t", p=128)
    mTv = mT.rearrange("(c p) t -> p c t", p=128)

    P = Prog(nc)
    C = Ctx()
    C.ones = P.sb("ones", [128, 128], BF16)
    x_sb = P.sb("x_sb", [128, DC, TB], F32)
    y_sb = P.sb("y_sb", [128, DC, TB], F32)
    h_sb = P.sb("h_sb", [128, DC, TB], BF16)
    big = P.sb("big", [128, FC, TB], BF16)
    g_sb = P.sb("g_sb", [128, 4, DC], F32)
    rstd = P.sb("rstd", [128, TB], F32)
    tmp = P.sb("tmp", [128, TB], F32)
    tmp2 = P.sb("tmp2", [128, 2, TB], F32)
    NPS = 6
    pss = [P.ps(f"ps{i}") for i in range(NPS)]
    ps_ss = P.ps("ps_ss")
    ring = WRing(P, wdram, wmeta, nslots=6)

    C.eps = P.sb("eps", [128, 1], F32)
    P.add("dve", lambda e: e.memset(C.ones[:, :], 1.0), writes=["ones"])
    P.add("dve", lambda e: e.memset(C.eps[:, :], EPS), writes=["eps"])
    for i, g in enumerate([gpost, gpre, gpost2, bout]):
        P.dma("sp", g_sb[:, i, :], g[:, :], writes=[("g", i)])

    psi = [0]

    def next_ps():
        i = psi[0] % NPS
        psi[0] += 1
        return pss[i], ("ps", i)

    def matmul_group(wtiles, rhs_of_k, rhs_keys_of_k, evac):
        ps, pkey = next_ps()
        kbase = 0
        total = sum(nk for _, nk in wtiles)
        for tid, nk in wtiles:
            wap, wkey = ring.fetch(tid)
            for k in range(nk):
                kk = kbase + k
                P.add("pe", lambda e, wap=wap, k=k, kk=kk: e.matmul(
                    ps[:, :], wap[:, k * 128:(k + 1) * 128], rhs_of_k(kk), start=(kk == 0), stop=(kk == total - 1)),
                    reads=[wkey, rhs_keys_of_k(kk)], writes=[pkey])
            kbase += nk
        evac(ps, pkey)

    def post_norm_residual(gi):
        emit_rmsnorm_stats(P, C, y_sb, [("y", c) for c in range(DC)], h_sb, [("h", c) for c in range(DC)], DC,
                           ps_ss, "ps_ss", rstd, "rstd", tmp, "tmp", D)
        for c in range(DC):
            t2 = tmp2[:, c % 2, :]
            P.add("dve", lambda e, c=c, t2=t2: e.scalar_tensor_tensor(
                out=t2, in0=y_sb[:, c, :], scalar=g_sb[:, gi, c:c + 1], in1=rstd[:, :], op0=ALU.mult, op1=ALU.mult),
                reads=[("y", c), "rstd", ("g", gi)], writes=[("tmp2", c % 2)])
            P.add("dve", lambda e, c=c, t2=t2: e.tensor_tensor(out=x_sb[:, c, :], in0=x_sb[:, c, :], in1=t2,
                                                                 op=ALU.add),
                  reads=[("tmp2", c % 2), ("x", c)], writes=[("x", c)])

    for b in range(NB):
        ts = slice(b * TB, (b + 1) * TB)
        for c0 in range(0, DC, 4):
            P.dma("sp", x_sb[:, c0:c0 + 4, :], xTv[:, c0:c0 + 4, ts], writes=[("x", c) for c in range(c0, c0 + 4)])
        for c0 in range(0, KC, 8):
            c1 = min(KC, c0 + 8)
            P.dma("sp", big[:, c0:c1, :], mTv[:, c0:c1, ts], writes=[("big", c) for c in range(c0, c1)])
        for m in range(DC):
            def evac(ps, pkey, m=m):
                if has_bias:
                    P.add("act", lambda e: e.activation(out=y_sb[:, m, :], in_=ps[:, :], func=AF.Identity,
                                                        bias=g_sb[:, 3, m:m + 1]),
                          reads=[pkey, ("g", 3)], writes=[("y", m)])
                else:
                    P.add("act", lambda e: e.copy(out=y_sb[:, m, :], in_=ps[:, :]), reads=[pkey], writes=[("y", m)])
            matmul_group(wplan["wo"][m], lambda kk: big[:, kk, :], lambda kk: ("big", kk), evac)
        post_norm_residual(0)
        emit_rmsnorm_stats(P, C, x_sb, [("x", c) for c in range(DC)], h_sb, [("h", c) for c in range(DC)], DC,
                           ps_ss, "ps_ss", rstd, "rstd", tmp, "tmp", D)
        for c in range(DC):
            P.add("dve", lambda e, c=c: e.scalar_tensor_tensor(
                out=h_sb[:, c, :], in0=x_sb[:, c, :], scalar=g_sb[:, 1, c:c + 1], in1=rstd[:, :], op0=ALU.mult,
                op1=ALU.mult), reads=[("x", c), "rstd", ("g", 1)], writes=[("h", c)])
        for m in range(FC):
            def evac(ps, pkey, m=m):
                t2 = tmp2[:, m % 2, :]
                P.add("act", lambda e: e.activation(out=t2, in_=ps[:, :], func=AF.Relu), reads=[pkey],
                      writes=[("tmp2", m % 2)])
                P.add("dve", lambda e: e.tensor_tensor(out=big[:, m, :], in0=t2, in1=t2, op=ALU.mult),
                      reads=[("tmp2", m % 2)], writes=[("big", m)])
            matmul_group(wplan["w1"][m], lambda kk: h_sb[:, kk, :], lambda kk: ("h", kk), evac)
        for m in range(DC):
            def evac(ps, pkey, m=m):
                P.add("act", lambda e: e.copy(out=y_sb[:, m, :], in_=ps[:, :]), reads=[pkey], writes=[("y", m)])
            matmul_group(wplan["w2"][m], lambda kk: big[:, kk, :], lambda kk: ("big", kk), evac)
        post_norm_residual(2)
        for c0 in range(0, DC, 4):
            P.dma("sp", oTv[:, c0:c0 + 4, ts], x_sb[:, c0:c0 + 4, :], reads=[("x", c) for c in range(c0, c0 + 4)])
    P.emit()
    return nc


def prep_post_mlp_weights(w_o, w1, w2):
    ws = WStream()
    plan = {"wo": ws.add_matrix(w_o), "w1": ws.add_matrix(w1), "w2": ws.add_matrix(w2)}
    return ws.array(), ws.meta, plan


class Bld:
    def __init__(self, nc, wdram, wmeta, nps=6, nslots=6, with_ss=True):
        self.P = Prog(nc)
        P = self.P
        self.C = Ctx()
        self.C.ones = P.sb("ones", [128, 128], BF16)
        self.C.eps = P.sb("eps", [128, 1], F32)
        P.add("dve", lambda e: e.memset(self.C.ones[:, :], 1.0), writes=["ones"])
        P.add("dve", lambda e: e.memset(self.C.eps[:, :], EPS), writes=["eps"])
        self.pss = [P.ps(f"ps{i}") for i in range(nps)]
        self.nps = nps
        self.psi = 0
        if with_ss:
            self.ps_ss = P.ps("ps_ss")
            self.rstd = P.sb("rstd", [128, TB], F32)
            self.tmp = P.sb("tmp", [128, TB], F32)
        self.ring = WRing(P, wdram, wmeta, nslots=nslots) if wdram is not None else None

    def next_ps(self):
        i = self.psi % self.nps
        self.psi += 1
        return self.pss[i], ("ps", i)

    def group(self, wtiles, rhs_of_k, key_of_k, evac, mw=128, ncol=TB):
        P = self.P
        ps, pkey = self.next_ps()
        total = sum(nk for _, nk in wtiles)
        kbase = 0
        for tid, nk in wtiles:
            wap, wkey = self.ring.fetch(tid)
            for k in range(nk):
                kk = kbase + k
                P.add("pe", lambda e, wap=wap, k=k, kk=kk: e.matmul(
                    ps[0:mw, :ncol], wap[:, k * mw:(k + 1) * mw], rhs_of_k(kk), start=(kk == 0),
                    stop=(kk == total - 1)), reads=[wkey, key_of_k(kk)], writes=[pkey])
            kbase += nk
        evac(ps, pkey)

    def norm_stats(self, src, src_keys, sq, sq_keys, nch, dim, ntok=TB):
        emit_rmsnorm_stats(self.P, self.C, src, src_keys, sq, sq_keys, nch, self.ps_ss, "ps_ss", self.rstd, "rstd",
                           self.tmp, "tmp", dim, ntok)

    def norm_apply(self, dst, dst_keys, src, src_keys, g_ap_of_c, g_key, nch, eng="dve"):
        for c in range(nch):
            self.P.add(eng, lambda e, c=c: e.scalar_tensor_tensor(
                out=dst[:, c, :], in0=src[:, c, :], scalar=g_ap_of_c(c), in1=self.rstd[:, :], op0=ALU.mult,
                op1=ALU.mult), reads=[src_keys[c], "rstd", g_key], writes=[dst_keys[c]])


TWO_PI = 6.283185307179586
CW1 = 6.28125
CW2 = TWO_PI - CW1
PI_LO = 3.1415925


def emit_rope_tables(B, pos_i, pos_key, rc, n, cos2, sin2, tkey):
    P = B.P
    W = B.rope_ws
    ang, kfl, r, m = W[:, 0, :n], W[:, 1, :n], W[:, 2, :n], W[:, 3, :n]
    ki = B.rope_wi[:, :n]
    K = "ropews"
    P.add("dve", lambda e: e.tensor_copy(out=kfl, in_=pos_i), reads=[pos_key], writes=[K])
    P.add("dve", lambda e: e.tensor_scalar(out=ang, in0=kfl, scalar1=rc[:, 0:1], scalar2=None, op0=ALU.mult),
          reads=[K, "rc"], writes=[K])
    P.add("dve", lambda e: e.tensor_scalar(out=ki, in0=ang, scalar1=1.0 / TWO_PI, scalar2=None, op0=ALU.mult),
          reads=[K], writes=[K])
    P.add("dve", lambda e: e.tensor_copy(out=kfl, in_=ki), reads=[K], writes=[K])
    P.add("dve", lambda e: e.scalar_tensor_tensor(out=r, in0=kfl, scalar=-CW1, in1=ang, op0=ALU.mult, op1=ALU.add),
          reads=[K], writes=[K])
    P.add("dve", lambda e: e.scalar_tensor_tensor(out=r, in0=kfl, scalar=-CW2, in1=r, op0=ALU.mult, op1=ALU.add),
          reads=[K], writes=[K])
    P.add("dve", lambda e: e.tensor_scalar(out=m, in0=r, scalar1=np.pi, scalar2=-TWO_PI, op0=ALU.is_gt,
                                           op1=ALU.mult), reads=[K], writes=[K])
    P.add("dve", lambda e: e.tensor_tensor(out=r, in0=r, in1=m, op=ALU.add), reads=[K], writes=[K])
    P.add("dve", lambda e: e.tensor_scalar(out=r, in0=r, scalar1=PI_LO, scalar2=-PI_LO, op0=ALU.min, op1=ALU.max),
          reads=[K], writes=[K])
    P.add("act", lambda e: e.activation(out=m, in_=r, func=AF.Sin), reads=[K], writes=[K])
    P.add("act", lambda e: e.activation(out=ang, in_=r, func=AF.Sin, scale=0.5), reads=[K], writes=[K])
    P.add("dve", lambda e: e.tensor_scalar(out=sin2, in0=m, scalar1=rc[:, 1:2], scalar2=None, op0=ALU.mult),
          reads=[K, "rc"], writes=[tkey])
    P.add("dve", lambda e: e.tensor_tensor(out=ang, in0=ang, in1=ang, op=ALU.mult), reads=[K], writes=[K])
    P.add("dve", lambda e: e.tensor_scalar(out=cos2, in0=ang, scalar1=-2.0, scalar2=1.0, op0=ALU.mult, op1=ALU.add),
          reads=[K], writes=[tkey])


def rope_consts():
    half = 32
    inv = (10000.0 ** (-(np.arange(half, dtype=np.float32) / np.float32(half)))).astype(np.float32)
    rc = np.zeros((64, 2), np.float32)
    rc[:, 0] = np.concatenate([inv, inv])
    rc[:32, 1] = -1.0
    rc[32:, 1] = 1.0
    return rc


def build_mla_proj(T, wmeta, wplan, stop=99):
    nc = bass.Bass("TRN2", target_bir_lowering=False)
    NB = T // TB
    xT = nc.dram_tensor("xT", [D, T], F32, kind="ExternalInput").ap()
    pos = nc.dram_tensor("pos", [64, T], I32, kind="ExternalInput").ap()
    rcd = nc.dram_tensor("rc", [64, 2], F32, kind="ExternalInput").ap()
    wdram = nc.dram_tensor("wstream", [128, wmeta[-1][0] + wmeta[-1][1]], F32, kind="ExternalInput").ap()
    gd = nc.dram_tensor("gvecs", [128, DC + 8], F32, kind="ExternalInput").ap()
    qn_o = nc.dram_tensor("qn", [128, HEADS, T], BF16, kind="ExternalOutput").ap()
    qr_o = nc.dram_tensor("qr", [64, HEADS, T], BF16, kind="ExternalOutput").ap()
    kn_o = nc.dram_tensor("kn", [128, HEADS, T], BF16, kind="ExternalOutput").ap()
    kr_o = nc.dram_tensor("kr", [64, T], BF16, kind="ExternalOutput").ap()
    v_o = nc.dram_tensor("v", [T, D], BF16, kind="ExternalOutput").ap()
    xTv = xT.rearrange("(c p) t -> p c t", p=128)
    v_ov = v_o.rearrange("(t p) c -> p t c", p=128)

    B = Bld(nc, wdram, wmeta)
    P = B.P
    x_sb = P.sb("x_sb", [128, DC, TB], F32)
    h_sb = P.sb("h_sb", [128, DC, TB], BF16)
    cqp = P.sb("cqp", [128, 4, TB], F32)
    ckvp = P.sb("ckvp", [128, 4, TB], F32)
    cqn = P.sb("cqn", [128, 4, TB], BF16)
    ckvn = P.sb("ckvn", [128, 4, TB], BF16)
    g_sb = P.sb("g_sb", [128, DC + 8], F32)
    rc = P.sb("rc_sb", [64, 2], F32)
    pos_sb = P.sb("pos_sb", [64, TB], I32)
    B.rope_ws = P.sb("rope_ws", [64, 4, TB], F32)
    B.rope_wi = P.sb("rope_wi", [64, TB], I32)
    cos2 = P.sb("cos2", [64, TB], F32)
    sin2 = P.sb("sin2", [64, TB], F32)
    rt = P.sb("rt", [64, 2, 2, TB], F32)
    st_qn = P.sb("st_qn", [128, HEADS, TB], BF16)
    st_qr = P.sb("st_qr", [64, HEADS, TB], BF16)
    st_kn = P.sb("st_kn", [128, HEADS, TB], BF16)
    st_kr = P.sb("st_kr", [64, TB], BF16)
    st_v = P.sb("st_v", [128, 4, D], BF16)

    P.dma("sp", g_sb[:, :], gd[:, :], writes=["g"])
    P.dma("sp", rc[:, :], rcd[:, :], writes=["rc"])

    rti = [0]

    def rope_apply(ps_a, ka, ps_b, kb, out_ap, out_key):
        i = rti[0] % 2
        rti[0] += 1
        t1, t2 = rt[:, i, 0, :], rt[:, i, 1, :]
        P.add("dve", lambda e: e.tensor_tensor(out=t1, in0=ps_a[0:64, :], in1=cos2[:, :], op=ALU.mult),
              reads=[ka, "tables"], writes=[("rt", i, 0)])
        P.add("dve", lambda e: e.tensor_tensor(out=t2, in0=ps_b[0:64, :], in1=sin2[:, :], op=ALU.mult),
              reads=[kb, "tables"], writes=[("rt", i, 1)])
        P.add("dve", lambda e: e.tensor_tensor(out=out_ap, in0=t1, in1=t2, op=ALU.add),
              reads=[("rt", i, 0), ("rt", i, 1)], writes=[out_key])

    def pair_group(tiles_a, tiles_b, rhs_of_k, key_of_k, out_ap, out_key):
        hold = {}

        def ev_a(ps, pkey):
            hold["a"] = (ps, pkey)

        def ev_b(ps, pkey):
            pa, ka = hold["a"]
            rope_apply(pa, ka, ps, pkey, out_ap, out_key)
        B.group(tiles_a, rhs_of_k, key_of_k, ev_a, mw=64)
        B.group(tiles_b, rhs_of_k, key_of_k, ev_b, mw=64)

    def body(b):
        ts = slice(b * TB, (b + 1) * TB)
        for c0 in range(0, DC, 4):
            P.dma("sp", x_sb[:, c0:c0 + 4, :], xTv[:, c0:c0 + 4, ts], writes=[("x", c) for c in range(c0, c0 + 4)])
        P.dma("sp", pos_sb[:, :], pos[:, ts], writes=["pos"])
        emit_rope_tables(B, pos_sb[:, :], "pos", rc, TB, cos2[:, :], sin2[:, :], "tables")
        if stop <= 1:
            return
        B.norm_stats(x_sb, [("x", c) for c in range(DC)], h_sb, [("h", c) for c in range(DC)], DC, D)
        B.norm_apply(h_sb, [("h", c) for c in range(DC)], x_sb, [("x", c) for c in range(DC)],
                     lambda c: g_sb[:, c:c + 1], "g", DC)
        hk = lambda kk: h_sb[:, kk, :]
        hkey = lambda kk: ("h", kk)
        for m in range(4):
            B.group(wplan["dq"][m], hk, hkey, lambda ps, pkey, m=m: P.add(
                "act", lambda e: e.copy(out=cqp[:, m, :], in_=ps[:, :]), reads=[pkey], writes=[("cqp", m)]))
        for m in range(4):
            B.group(wplan["dkv"][m], hk, hkey, lambda ps, pkey, m=m: P.add(
                "act", lambda e: e.copy(out=ckvp[:, m, :], in_=ps[:, :]), reads=[pkey], writes=[("ckvp", m)]))
        if stop <= 2:
            return
        pair_group(wplan["kr"][0], wplan["kr_sw"][0], hk, hkey, st_kr[:, :], "st_kr")
        P.dma("sp", kr_o[:, ts], st_kr[:, :], reads=["st_kr"])
        if stop <= 3:
            return
        B.norm_stats(cqp, [("cqp", c) for c in range(4)], cqn, [("cqn", c) for c in range(4)], 4, 512)
        B.norm_apply(cqn, [("cqn", c) for c in range(4)], cqp, [("cqp", c) for c in range(4)],
                     lambda c: g_sb[:, DC + c:DC + c + 1], "g", 4)
        B.norm_stats(ckvp, [("ckvp", c) for c in range(4)], ckvn, [("ckvn", c) for c in range(4)], 4, 512)
        B.norm_apply(ckvn, [("ckvn", c) for c in range(4)], ckvp, [("ckvp", c) for c in range(4)],
                     lambda c: g_sb[:, DC + 4 + c:DC + 4 + c + 1], "g", 4)
        if stop <= 4:
            return
        qk = lambda kk: cqn[:, kk, :]
        qkey = lambda kk: ("cqn", kk)
        kk_ = lambda kk: ckvn[:, kk, :]
        kkey = lambda kk: ("ckvn", kk)
        for h in range(HEADS):
            B.group(wplan["uq_n"][h], qk, qkey, lambda ps, pkey, h=h: P.add(
                "act", lambda e: e.copy(out=st_qn[:, h, :], in_=ps[:, :]), reads=[pkey], writes=[("st_qn", h)]))
        P.dma("sp", qn_o[:, :, ts], st_qn[:, :, :], reads=[("st_qn", h) for h in range(HEADS)])
        if stop <= 5:
            return
        for h in range(HEADS):
            pair_group(wplan["uq_r"][h], wplan["uq_rs"][h], qk, qkey, st_qr[:, h, :], ("st_qr", h))
        P.dma("sp", qr_o[:, :, ts], st_qr[:, :, :], reads=[("st_qr", h) for h in range(HEADS)])
        if stop <= 6:
            return
        for h in range(HEADS):
            B.group(wplan["uk"][h], kk_, kkey, lambda ps, pkey, h=h: P.add(
                "act", lambda e: e.copy(out=st_kn[:, h, :], in_=ps[:, :]), reads=[pkey], writes=[("st_kn", h)]))
        P.dma("sp", kn_o[:, :, ts], st_kn[:, :, :], reads=[("st_kn", h) for h in range(HEADS)])
        if stop <= 7:
            return
        for cg in range(4):
            wap, wkey = B.ring.fetch(wplan["uv"][cg])
            for tt in range(4):
                ps, pkey = B.next_ps()
                for k in range(4):
                    P.add("pe", lambda e, ps=ps, k=k, tt=tt, wap=wap: e.matmul(
                        ps[:, :], ckvn[:, k, tt * 128:(tt + 1) * 128], wap[:, k * 512:(k + 1) * 512],
                        start=(k == 0), stop=(k == 3)), reads=[wkey, ("ckvn", k)], writes=[pkey])
                P.add("act", lambda e, ps=ps, tt=tt, cg=cg: e.copy(out=st_v[:, tt, cg * 512:(cg + 1) * 512],
                                                                    in_=ps[:, :]),
                      reads=[pkey], writes=[("st_v", tt, cg)])
        if stop <= 8:
            return
        for hf in range(2):
            P.dma("sp", v_ov[:, b * 4:(b + 1) * 4, hf * 1024:(hf + 1) * 1024], st_v[:, :, hf * 1024:(hf + 1) * 1024],
                  reads=[("st_v", tt, cg) for tt in range(4) for cg in range(2 * hf, 2 * hf + 2)])
    for b in range(NB):
        body(b)
    P.emit()
    return nc


def prep_mla_proj_weights(w_dq, w_dkv, w_uq, w_ukv):
    ws = WStream()
    plan = {}
    plan["dq"] = ws.add_matrix(w_dq)
    plan["dkv"] = ws.add_matrix(w_dkv[:, :512])
    kr = w_dkv[:, 512:576]
    plan["kr"] = ws.add_matrix(kr, mw=64)
    plan["kr_sw"] = ws.add_matrix(np.concatenate([kr[:, 32:], kr[:, :32]], axis=1), mw=64)
    uq = w_uq.reshape(512, HEADS, 192)
    plan["uq_n"] = ws.add_matrix(np.ascontiguousarray(uq[:, :, :128]).reshape(512, HEADS * 128))
    plan["uq_r"] = []
    plan["uq_rs"] = []
    for h in range(HEADS):
        r = uq[:, h, 128:192]
        plan["uq_r"].append(ws.add_matrix(r, mw=64)[0])
        plan["uq_rs"].append(ws.add_matrix(np.concatenate([r[:, 32:], r[:, :32]], axis=1), mw=64)[0])
    ukv = w_ukv.reshape(512, HEADS, 256)
    plan["uk"] = ws.add_matrix(np.ascontiguousarray(ukv[:, :, :128]).reshape(512, HEADS * 128))
    wv = np.ascontiguousarray(ukv[:, :, 128:]).reshape(512, HEADS * 128)
    plan["uv"] = []
    for cg in range(4):
        t = wv[:, cg * 512:(cg + 1) * 512].reshape(4, 128, 512).transpose(1, 0, 2).reshape(128, 2048)
        plan["uv"].append(ws.add_raw(t))
    return ws.array(), ws.meta, plan


ATT_SCALE = 192.0 ** -0.5


def attn_masks():
    k = np.arange(128)[:, None, None] + 128 * np.arange(4)[None, :, None]
    q = np.arange(512)[None, None, :]
    return ((k // 64) <= (q // 64)).astype(np.float32).astype(NPBF16)


def build_attn(S, NH):
    nc = bass.Bass("TRN2", target_bir_lowering=False)
    NQ = S // TB
    NKP = S // 1024
    qn = nc.dram_tensor("qn", [NH, 128, S], BF16, kind="ExternalInput").ap()
    qr = nc.dram_tensor("qr", [NH, 64, S], BF16, kind="ExternalInput").ap()
    kn = nc.dram_tensor("kn", [NH, 128, S], BF16, kind="ExternalInput").ap()
    kr = nc.dram_tensor("kr", [64, S], BF16, kind="ExternalInput").ap()
    v = nc.dram_tensor("v", [NH, 128, S // 128, 128], BF16, kind="ExternalInput").ap()
    msk = nc.dram_tensor("masks", [128, 4, 512], BF16, kind="ExternalInput").ap()
    o = nc.dram_tensor("o", [NH, 128, S], BF16, kind="ExternalOutput").ap()

    emit_attn(nc, S, NH, qn, qr, kn, kr, v, msk, o)
    return nc


def emit_attn(nc, S, NH, qn, qr, kn, kr, v, msk, o):
    NQ = S // TB
    NKP = S // 1024
    B = Bld(nc, None, None, nps=4, with_ss=False)
    P = B.P
    C = B.C
    ps_o = [P.ps(f"ps_o{i}") for i in range(2)]
    ps_d = [P.ps(f"ps_d{i}") for i in range(2)]
    kn_sb = [P.sb(f"kn_sb{i}", [128, S], BF16) for i in range(2)]
    v_sb = [P.sb(f"v_sb{i}", [128, S // 128, 128], BF16) for i in range(2)]
    kr_sb = P.sb("kr_sb", [128, S], BF16)
    m_sb = P.sb("m_sb", [128, 4, 512], BF16)
    qn_sb = [P.sb(f"qn_sb{i}", [128, TB], BF16) for i in range(2)]
    qr_sb = [P.sb(f"qr_sb{i}", [128, TB], BF16) for i in range(2)]
    NPT = 6
    pt = [P.sb(f"pt{i}", [128, TB], BF16) for i in range(NPT)]
    rec = [P.sb(f"rec{i}", [128, TB], F32) for i in range(2)]
    ost = [P.sb(f"ost{i}", [128, TB], BF16) for i in range(2)]
    acc = [P.sb(f"acc{i}", [128, TB], F32) for i in range(2)]
    ones_f = P.sb("ones_f", [128, 128], F32)
    P.add("dve", lambda e: e.memset(ones_f[:, :], 1.0), writes=["ones_f"])

    for j in range(4):
        P.dma("sp", m_sb[:, j, :], msk[:, j, :], writes=[("m", j)])
    for p in range(NKP):
        P.add("pool", lambda e, p=p: e.memset(kr_sb[:, p * 1024:(p + 1) * 1024], 0.0), writes=[("kr", p)])
    for i in range(2):
        P.add("pool", lambda e, i=i: e.memset(qr_sb[i][:, :], 0.0), writes=[("qr", i)])
    for p in range(NKP):
        P.dma("sp", kr_sb[0:64, p * 1024:(p + 1) * 1024], kr[:, p * 1024:(p + 1) * 1024], writes=[("kr", p)])
    pti = [0]
    qi = [0]
    def qblock(h, hp, qb, i):
        qs = slice(qb * TB, (qb + 1) * TB)
        P.dma("sp", qn_sb[i][:, :], qn[h, :, qs], writes=[("qn", i)])
        P.dma("sp", qr_sb[i][0:64, :], qr[h, :, qs], writes=[("qr", i)])
        nkt = 4 * (qb + 1)
        tiles = {}

        def S_(kt):
            ps, pkey = B.next_ps()
            j = pti[0] % NPT
            pti[0] += 1
            tiles[kt] = j
            ks = slice(kt * 128, (kt + 1) * 128)
            P.add("pe", lambda e: e.matmul(ps[:, :], kn_sb[hp][:, ks], qn_sb[i][:, :], start=True, stop=False),
                  reads=[("kn", hp, kt // 8), ("qn", i)], writes=[pkey])
            P.add("pe", lambda e: e.matmul(ps[:, :], kr_sb[:, ks], qr_sb[i][:, :], start=False, stop=True),
                  reads=[("kr", kt // 8), ("qr", i)], writes=[pkey])
            P.add("act", lambda e: e.activation(out=pt[j][:, :], in_=ps[:, :], func=AF.Exp, scale=ATT_SCALE),
                  reads=[pkey], writes=[("pt", j)])
            if kt >= 4 * qb:
                jj = kt - 4 * qb
                P.add("dve", lambda e: e.tensor_tensor(out=pt[j][:, :], in0=pt[j][:, :], in1=m_sb[:, jj, :],
                                                       op=ALU.mult),
                      reads=[("pt", j), ("m", jj)], writes=[("pt", j)])

        def PV_(kt):
            j = tiles[kt]
            P.add("pe", lambda e: e.matmul(ps_o[i][:, :], v_sb[hp][:, kt, :], pt[j][:, :], start=(kt == 0),
                                           stop=(kt == nkt - 1)),
                  reads=[("v", hp, kt // 8), ("pt", j)], writes=[("ps_o", i)])
            if kt == 0:
                P.add("dve", lambda e: e.tensor_copy(out=acc[i][:, :], in_=pt[j][:, :]),
                      reads=[("pt", j)], writes=[("acc", i)])
            else:
                P.add("dve", lambda e: e.tensor_tensor(out=acc[i][:, :], in0=acc[i][:, :], in1=pt[j][:, :],
                                                       op=ALU.add),
                      reads=[("pt", j), ("acc", i)], writes=[("acc", i)])

        LA = 3
        for kt in range(min(LA, nkt)):
            S_(kt)
        for kt in range(nkt):
            PV_(kt)
            if kt + LA < nkt:
                S_(kt + LA)
        P.add("pe", lambda e: e.matmul(ps_d[i][:, :], ones_f[:, :], acc[i][:, :], start=True, stop=True),
              reads=["ones_f", ("acc", i)], writes=[("ps_d", i)])
        P.add("dve", lambda e, i=i: e.reciprocal(out=rec[i][:, :], in_=ps_d[i][:, :]),
              reads=[("ps_d", i)], writes=[("rec", i)])
        P.add("dve", lambda e, i=i: e.tensor_tensor(out=ost[i][:, :], in0=ps_o[i][:, :], in1=rec[i][:, :],
                                                    op=ALU.mult),
              reads=[("ps_o", i), ("rec", i)], writes=[("ost", i)])
        P.dma("sp", o[h, :, qs], ost[i][:, :], reads=[("ost", i)])
    for h in range(NH):
        hp = h % 2
        for p in range(NKP):
            P.dma("sp", kn_sb[hp][:, p * 1024:(p + 1) * 1024], kn[h, :, p * 1024:(p + 1) * 1024],
                  writes=[("kn", hp, p)])
            P.dma("sp", v_sb[hp][:, p * 8:(p + 1) * 8, :], v[h, :, p * 8:(p + 1) * 8, :], writes=[("v", hp, p)])
        for qb in range(NQ):
            qblock(h, hp, qb, qi[0] % 2)
            qi[0] += 1
    P.emit()


CH = 84
NJ = 4


def build_lru(NT, NBATCH, wmeta, wplan):
    nc = bass.Bass("TRN2", target_bir_lowering=False)
    NB = NT // TB
    NBB = NB // NBATCH
    xT = nc.dram_tensor("xT", [D, NT], F32, kind="ExternalInput").ap()
    wdram = nc.dram_tensor("wstream", [128, wmeta[-1][0] + wmeta[-1][1]], F32, kind="ExternalInput").ap()
    wgd = nc.dram_tensor("wg", [CH, 2 * NJ * 2 * CH], F32, kind="ExternalInput").ap()
    vecd = nc.dram_tensor("vec", [CH, NJ, 10], F32, kind="ExternalInput").ap()
    gd = nc.dram_tensor("gpre", [128, DC], F32, kind="ExternalInput").ap()
    hy = nc.dram_tensor("hy", [NJ, CH, NT], BF16, kind="ExternalOutput").ap()
    xTv = xT.rearrange("(c p) t -> p c t", p=128)
    hyv = hy.rearrange("j p t -> p j t")

    B = Bld(nc, wdram, wmeta)
    P = B.P
    x_sb = P.sb("x_sb", [128, DC, TB], F32)
    h_sb = P.sb("h_sb", [128, DC, TB], BF16)
    g_sb = P.sb("g_sb", [128, DC], F32)
    wg_sb = P.sb("wg_sb", [128, 2, NJ, 2, CH], BF16)
    vec = P.sb("vec_sb", [128, NJ, 10], F32)
    cc = P.sb("cc", [128, 4, NJ], F32)
    one_c = P.sb("one_c", [128, 1], F32)
    y_sb = P.sb("y_sb", [128, NJ, TB], F32)
    uxb = P.sb("uxb", [128, NJ, TB + 4], F32)
    u_sb = P.sb("u_sb", [128, NJ, TB], F32)
    u_bf = P.sb("u_bf", [128, NJ, TB], BF16)
    r_sb = P.sb("r_sb", [128, NJ, TB], F32)
    ig_sb = P.sb("ig_sb", [128, NJ, TB], F32)
    a_sb = P.sb("a_sb", [128, NJ, TB], F32)
    m_sb = P.sb("m_sb", [128, NJ, TB], F32)
    inp_sb = P.sb("inp_sb", [128, NJ, TB], F32)
    hs_sb = P.sb("hs_sb", [128, NJ, TB], F32)
    ost = P.sb("ost", [128, NJ, TB], BF16)
    state = P.sb("state", [128, NJ], F32)

    P.dma("sp", g_sb[:, :], gd[:, :], writes=["g"])
    P.dma("sp", vec[0:CH, :, :], vecd[:, :, :], writes=["vec"])
    P.dma("pool", wg_sb[0:CH, :, :, :, :].rearrange("p a b c d -> p (a b c d)"), wgd[:, :], writes=["wg"])
    P.add("dve", lambda e: e.memset(one_c[:, :], 1.0), writes=["one_c"])
    e_ = cc[0:CH, 0, :]
    t_ = cc[0:CH, 1, :]
    P.add("act", lambda e: e.activation(out=e_, in_=vec[0:CH, :, 5], func=AF.Exp, scale=-1.0), reads=["vec"],
          writes=["cc"])
    coef = [1.0 / 5, -1.0 / 4, 1.0 / 3, -1.0 / 2, 1.0]
    P.add("dve", lambda e: e.tensor_scalar(out=t_, in0=e_, scalar1=-1.0 / 6, scalar2=coef[0], op0=ALU.mult,
                                           op1=ALU.add), reads=["cc"], writes=["cc"])
    for cf in coef[1:]:
        P.add("dve", lambda e: e.tensor_tensor(out=t_, in0=t_, in1=e_, op=ALU.mult), reads=["cc"], writes=["cc"])
        P.add("dve", lambda e, cf=cf: e.tensor_scalar(out=t_, in0=t_, scalar1=cf, scalar2=None, op0=ALU.add),
              reads=["cc"], writes=["cc"])
    P.add("dve", lambda e: e.tensor_tensor(out=t_, in0=t_, in1=e_, op=ALU.mult), reads=["cc"], writes=["cc"])
    P.add("dve", lambda e: e.tensor_scalar(out=cc[0:CH, 2, :], in0=t_, scalar1=-8.0, scalar2=None, op0=ALU.mult),
          reads=["cc"], writes=["cc"])
    P.add("dve", lambda e: e.tensor_scalar(out=cc[0:CH, 3, :], in0=t_, scalar1=-16.0, scalar2=None, op0=ALU.mult),
          reads=["cc"], writes=["cc"])

    def body(b):
        ts = slice(b * TB, (b + 1) * TB)
        if b % NBB == 0:
            P.add("dve", lambda e: e.memset(state[:, :], 0.0), writes=[("state", j) for j in range(NJ)])
            P.add("dve", lambda e: e.memset(uxb[:, :, 0:3], 0.0), writes=[("ux", j) for j in range(NJ)])
        for c0 in range(0, DC, 4):
            P.dma("sp", x_sb[:, c0:c0 + 4, :], xTv[:, c0:c0 + 4, ts], writes=[("x", c) for c in range(c0, c0 + 4)])
        B.norm_stats(x_sb, [("x", c) for c in range(DC)], h_sb, [("h", c) for c in range(DC)], DC, D)
        B.norm_apply(h_sb, [("h", c) for c in range(DC)], x_sb, [("x", c) for c in range(DC)],
                     lambda c: g_sb[:, c:c + 1], "g", DC)
        hk = lambda kk: h_sb[:, kk, :]
        hkey = lambda kk: ("h", kk)
        for j in range(NJ):
            B.group(wplan["wy"][j], hk, hkey, lambda ps, pkey, j=j: P.add(
                "act", lambda e: e.activation(out=y_sb[0:CH, j, :], in_=ps[0:CH, :], func=AF.Gelu,
                                              bias=vec[0:CH, j, 0:1]),
                reads=[pkey, "vec"], writes=[("y", j)]), mw=CH)
        for j in range(NJ):
            B.group(wplan["wx"][j], hk, hkey, lambda ps, pkey, j=j: P.add(
                "act", lambda e: e.activation(out=uxb[0:CH, j, 3:3 + TB], in_=ps[0:CH, :], func=AF.Identity,
                                              bias=vec[0:CH, j, 1:2]),
                reads=[pkey, "vec"], writes=[("ux", j)]), mw=CH)
        for j in range(NJ):
            P.add("dve", lambda e, j=j: e.tensor_scalar(out=u_sb[0:CH, j, :], in0=uxb[0:CH, j, 0:TB],
                                                        scalar1=vec[0:CH, j, 6:7], scalar2=vec[0:CH, j, 2:3],
                                                        op0=ALU.mult, op1=ALU.add),
                  reads=[("ux", j), "vec"], writes=[("u", j)])
            for i in range(1, 4):
                P.add("dve", lambda e, j=j, i=i: e.scalar_tensor_tensor(
                    out=u_sb[0:CH, j, :], in0=uxb[0:CH, j, i:i + TB], scalar=vec[0:CH, j, 6 + i:7 + i],
                    in1=u_sb[0:CH, j, :], op0=ALU.mult, op1=ALU.add),
                    reads=[("ux", j), "vec", ("u", j)], writes=[("u", j)])
            P.add("dve", lambda e, j=j: e.tensor_copy(out=uxb[0:CH, j, 0:3], in_=uxb[0:CH, j, TB:TB + 3]),
                  reads=[("ux", j)], writes=[("ux", j)])
            P.add("act", lambda e, j=j: e.copy(out=u_bf[0:CH, j, :], in_=u_sb[0:CH, j, :]),
                  reads=[("u", j)], writes=[("ubf", j)])
        for gi, dst, bidx in ((0, r_sb, 3), (1, ig_sb, 4)):
            for j in range(NJ):
                ps, pkey = B.next_ps()
                for i in range(2):
                    P.add("pe", lambda e, ps=ps, gi=gi, j=j, i=i: e.matmul(
                        ps[0:CH, :], wg_sb[0:CH, gi, j, i, :], u_bf[0:CH, 2 * (j // 2) + i, :], start=(i == 0),
                        stop=(i == 1)), reads=["wg", ("ubf", 2 * (j // 2) + i)], writes=[pkey])
                P.add("act", lambda e, ps=ps, j=j, dst=dst, bidx=bidx: e.activation(
                    out=dst[0:CH, j, :], in_=ps[0:CH, :], func=AF.Sigmoid, bias=vec[0:CH, j, bidx:bidx + 1]),
                    reads=[pkey, "vec"], writes=[("gate", gi, j)])
        for j in range(NJ):
            P.add("act", lambda e, j=j: e.activation(out=a_sb[0:CH, j, :], in_=r_sb[0:CH, j, :], func=AF.Exp,
                                                     scale=cc[0:CH, 2, j:j + 1]),
                  reads=[("gate", 0, j), "cc"], writes=[("a", j)])
            P.add("act", lambda e, j=j: e.activation(out=m_sb[0:CH, j, :], in_=r_sb[0:CH, j, :], func=AF.Exp,
                                                     scale=cc[0:CH, 3, j:j + 1]),
                  reads=[("gate", 0, j), "cc"], writes=[("m", j)])
        for j in range(NJ):
            P.add("dve", lambda e, j=j: e.tensor_scalar(out=m_sb[0:CH, j, :], in0=m_sb[0:CH, j, :], scalar1=-1.0,
                                                        scalar2=1.0, op0=ALU.mult, op1=ALU.add),
                  reads=[("m", j)], writes=[("m", j)])
        for j in range(NJ):
            P.add("act", lambda e, j=j: e.activation(out=m_sb[0:CH, j, :], in_=m_sb[0:CH, j, :], func=AF.Sqrt),
                  reads=[("m", j)], writes=[("m", j)])
        for j in range(NJ):
            P.add("dve", lambda e, j=j: e.tensor_tensor(out=inp_sb[0:CH, j, :], in0=ig_sb[0:CH, j, :],
                                                        in1=u_sb[0:CH, j, :], op=ALU.mult),
                  reads=[("gate", 1, j), ("u", j)], writes=[("inp", j)])
            P.add("dve", lambda e, j=j: e.tensor_tensor(out=inp_sb[0:CH, j, :], in0=inp_sb[0:CH, j, :],
                                                        in1=m_sb[0:CH, j, :], op=ALU.mult),
                  reads=[("inp", j), ("m", j)], writes=[("inp", j)])
            P.add("dve", lambda e, j=j: e.tensor_tensor_scan(
                out=hs_sb[0:CH, j, :], data0=a_sb[0:CH, j, :], data1=inp_sb[0:CH, j, :],
                initial=state[0:CH, j:j + 1], op0=ALU.mult, op1=ALU.add),
                reads=[("a", j), ("inp", j), ("state", j)], writes=[("hs", j)])
            P.add("dve", lambda e, j=j: e.tensor_copy(out=state[0:CH, j:j + 1], in_=hs_sb[0:CH, j, TB - 1:TB]),
                  reads=[("hs", j)], writes=[("state", j)])
            P.add("dve", lambda e, j=j: e.tensor_tensor(out=ost[0:CH, j, :], in0=hs_sb[0:CH, j, :],
                                                        in1=y_sb[0:CH, j, :], op=ALU.mult),
                  reads=[("hs", j), ("y", j)], writes=[("ost", j)])
        P.dma("sp", hyv[:, :, ts], ost[0:CH, :, :], reads=[("ost", j) for j in range(NJ)])

    for b in range(NB):
        body(b)
    P.emit()
    return nc


def prep_lru_weights(core, w_y, b_y, w_x, b_x, conv_w, conv_b, w_ga, b_ga, w_gi, b_gi, lam):
    c0 = core * NJ * CH
    c1 = c0 + NJ * CH
    ws = WStream()
    plan = {"wy": ws.add_matrix(np.ascontiguousarray(w_y[:, c0:c1]), mw=CH),
            "wx": ws.add_matrix(np.ascontiguousarray(w_x[:, c0:c1]), mw=CH)}
    wg = np.zeros((CH, 2, NJ, 2, CH), np.float32)
    for gi, w in enumerate((w_ga, w_gi)):
        for j in range(NJ):
            n = 2 * core + j // 2
            for i in range(2):
                wg[:, gi, j, i, :] = w[n, i * CH:(i + 1) * CH, (j % 2) * CH:(j % 2 + 1) * CH]
    vec = np.zeros((CH, NJ, 10), np.float32)
    for idx, vv in enumerate((b_y, b_x, conv_b, b_ga, b_gi, lam)):
        vec[:, :, idx] = vv[c0:c1].reshape(NJ, CH).T
    for i in range(4):
        vec[:, :, 6 + i] = conv_w[i, c0:c1].reshape(NJ, CH).T
    return ws.array(), ws.meta, plan, np.ascontiguousarray(wg.reshape(CH, -1)), vec


def _run(nc, in_maps):
    res = run_bass_kernel_spmd(nc, in_maps, core_ids=list(range(NCORES)))
    return res.results


def kernel_unfused(x, positions, mix_pre_g, mix_post_g, mlp_pre_g, mlp_post_g,
           mla_w_dq, mla_g_q, mla_w_uq, mla_w_dkv, mla_g_kv, mla_w_ukv, mla_w_o,
           lru_w_y, lru_b_y, lru_w_x, lru_b_x, lru_conv_w, lru_conv_b,
           lru_w_ga, lru_b_ga, lru_w_gi, lru_b_gi, lru_lam, lru_w_out, lru_b_out,
           mlp_w1, mlp_w2):
    f32 = lambda a: np.asarray(a, np.float32)
    x = f32(x)
    positions = np.asarray(positions, np.int32)
    NBATCH, S, _ = x.shape
    T = NBATCH * S // NCORES
    QPB = S // T
    NT = NBATCH * S
    NH = HEADS * NBATCH // NCORES
    xs = x.reshape(NCORES, T, D)
    xT = [np.ascontiguousarray(xs[c].T) for c in range(NCORES)]
    rc = rope_consts()
    masks = attn_masks()
    zero_b = np.zeros((128, DC), np.float32)
    depth = mix_pre_g.shape[0]

    def post_mlp(i, mT, w_o, b_o, KIN):
        warr, wmeta, wplan = prep_post_mlp_weights(f32(w_o), f32(mlp_w1[i]), f32(mlp_w2[i]))
        nc = build_post_mlp(T, KIN, b_o is not None, wmeta, wplan)
        gpost, gpre, gpost2 = gvec(mix_post_g[i]), gvec(mlp_pre_g[i]), gvec(mlp_post_g[i])
        bo = gvec(b_o) if b_o is not None else zero_b
        maps = [{"xT": xT[c], "mT": mT[c], "wstream": warr, "gpost": gpost, "gpre": gpre, "gpost2": gpost2,
                 "bout": bo} for c in range(NCORES)]
        r = _run(nc, maps)
        return [np.asarray(r[c]["oT"]) for c in range(NCORES)]

    for i in range(depth):
        j = i // 2
        if i % 2 == 0:
            warr, wmeta, wplan = prep_mla_proj_weights(f32(mla_w_dq[j]), f32(mla_w_dkv[j]), f32(mla_w_uq[j]),
                                                       f32(mla_w_ukv[j]))
            nc = build_mla_proj(T, wmeta, wplan)
            gv = np.ascontiguousarray(np.concatenate([gvec(mix_pre_g[i]), gvec(mla_g_q[j]), gvec(mla_g_kv[j])],
                                                     axis=1))
            maps = []
            for c in range(NCORES):
                b, q = divmod(c, QPB)
                pos = np.ascontiguousarray(np.broadcast_to(positions[b, q * T:(q + 1) * T][None, :], (64, T)))
                maps.append({"xT": xT[c], "pos": pos, "rc": rc, "wstream": warr, "gvecs": gv})
            r1 = _run(nc, maps)
            del maps
            maps = []
            for c in range(NCORES):
                b, hg = divmod(c, QPB)
                hs = slice(hg * NH, (hg + 1) * NH)
                cores = [b * QPB + q for q in range(QPB)]
                qn = np.concatenate([np.asarray(r1[cc]["qn"])[:, hs, :] for cc in cores], axis=2)
                qr = np.concatenate([np.asarray(r1[cc]["qr"])[:, hs, :] for cc in cores], axis=2)
                kn = np.concatenate([np.asarray(r1[cc]["kn"])[:, hs, :] for cc in cores], axis=2)
                kr = np.concatenate([np.asarray(r1[cc]["kr"]) for cc in cores], axis=1)
                v = np.concatenate([np.asarray(r1[cc]["v"])[:, hg * NH * 128:(hg + 1) * NH * 128] for cc in cores],
                                   axis=0)
                v = v.reshape(S // 128, 128, NH, 128).transpose(2, 1, 0, 3)
                maps.append({"qn": np.ascontiguousarray(qn.transpose(1, 0, 2)),
                             "qr": np.ascontiguousarray(qr.transpose(1, 0, 2)),
                             "kn": np.ascontiguousarray(kn.transpose(1, 0, 2)),
                             "kr": np.ascontiguousarray(kr), "v": np.ascontiguousarray(v), "masks": masks})
            del r1
            nc = build_attn(S, NH)
            r2 = _run(nc, maps)
            del maps
            mT = []
            for c in range(NCORES):
                b, q = divmod(c, QPB)
                parts = [np.asarray(r2[b * QPB + hg]["o"])[:, :, q * T:(q + 1) * T].reshape(NH * 128, T)
                         for hg in range(QPB)]
                mT.append(np.ascontiguousarray(np.concatenate(parts, axis=0)))
            del r2
            xT = post_mlp(i, mT, mla_w_o[j], None, D)
        else:
            xfull = np.ascontiguousarray(np.concatenate(xT, axis=1))
            gp = gvec(mix_pre_g[i])
            maps = []
            wm = wp = None
            for c in range(NCORES):
                warr, wm, wp, wg, vec = prep_lru_weights(
                    c, f32(lru_w_y[j]), f32(lru_b_y[j]), f32(lru_w_x[j]), f32(lru_b_x[j]), f32(lru_conv_w[j]),
                    f32(lru_conv_b[j]), f32(lru_w_ga[j]), f32(lru_b_ga[j]), f32(lru_w_gi[j]), f32(lru_b_gi[j]),
                    f32(lru_lam[j]))
                maps.append({"xT": xfull, "wstream": warr, "wg": wg, "vec": vec, "gpre": gp})
            nc = build_lru(NT, NBATCH, wm, wp)
            r4 = _run(nc, maps)
            del maps, xfull
            mfull = np.concatenate([np.asarray(r4[c]["hy"]).reshape(NJ * CH, NT) for c in range(NCORES)], axis=0)
            del r4
            mT = [np.ascontiguousarray(mfull[:, c * T:(c + 1) * T]) for c in range(NCORES)]
            del mfull
            xT = post_mlp(i, mT, lru_w_out[j], lru_b_out[j], DRNN)
    out = np.stack([xT[c].T for c in range(NCORES)], axis=0).reshape(NBATCH, S, D)
    return np.ascontiguousarray(out.astype(np.float32))


GROUPS = [[0, 1, 2, 3], [4, 5, 6, 7]]
TPG = 4
NHC = HEADS // TPG
FFC = DFF // TPG // 128
LJ = 8


def _blk(ap3, b):
    return ap3[b].rearrange("(c p) t -> p c t", p=128)


class Phase:
    def __init__(self, nc, wdram, wmeta, gdram, nps=6, nslots=6):
        self.B = Bld(nc, wdram, wmeta, nps=nps, nslots=nslots)
        P = self.B.P
        self.P = P
        self.x_sb = P.sb("x_sb", [128, DC, TB], F32)
        self.y_sb = P.sb("y_sb", [128, DC, TB], F32)
        self.h_sb = P.sb("h_sb", [128, DC, TB], BF16)
        self.g_sb = P.sb("g_sb", [128, 3, DC], F32)
        self.tmp2 = P.sb("tmp2", [128, 2, TB], F32)
        P.dma("sp", self.g_sb[:, :, :], gdram[:, :, :], writes=["g"])
        self.xk = [("x", c) for c in range(DC)]
        self.yk = [("y", c) for c in range(DC)]
        self.hk = [("h", c) for c in range(DC)]

    def _loads(self, b, x_src, red, ready):
        P = self.P
        for c0 in range(0, DC, 4):
            P.dma("sp", self.x_sb[:, c0:c0 + 4, :], x_src[:, c0:c0 + 4, :], writes=self.xk[c0:c0 + 4])
        if red is not None:
            rv = _blk(red, b)
            for c0 in range(0, DC, 4):
                P.dma("sp", self.y_sb[:, c0:c0 + 4, :], rv[:, c0:c0 + 4, :], writes=self.yk[c0:c0 + 4],
                      ext=(ready[b] if (ready is not None and c0 == 0) else None))

    def prologue(self, b, x_src, red, has_bias, x_dst, pre_norm=True, ready=None, nxt=None):
        P, B = self.P, self.B
        x_sb, y_sb, h_sb, g_sb, tmp2 = self.x_sb, self.y_sb, self.h_sb, self.g_sb, self.tmp2
        if not getattr(self, "_pref", False):
            self._loads(b, x_src, red, ready)
        self._pref = False
        if red is not None:
            if has_bias:
                for c in range(DC):
                    P.add("act", lambda e, c=c: e.activation(out=y_sb[:, c, :], in_=y_sb[:, c, :], func=AF.Identity,
                                                             bias=g_sb[:, 2, c:c + 1]),
                          reads=[("y", c), "g"], writes=[("y", c)])
            B.norm_stats(y_sb, self.yk, h_sb, self.hk, DC, D)
            for c in range(DC):
                t2 = tmp2[:, c % 2, :]
                P.add("dve", lambda e, c=c, t2=t2: e.scalar_tensor_tensor(
                    out=t2, in0=y_sb[:, c, :], scalar=g_sb[:, 0, c:c + 1], in1=B.rstd[:, :], op0=ALU.mult,
                    op1=ALU.mult), reads=[("y", c), "rstd", "g"], writes=[("tmp2", c % 2)])
                P.add("dve", lambda e, c=c, t2=t2: e.tensor_tensor(out=x_sb[:, c, :], in0=x_sb[:, c, :], in1=t2,
                                                                   op=ALU.add),
                      reads=[("tmp2", c % 2), ("x", c)], writes=[("x", c)])
            if x_dst is not None:
                for c0 in range(0, DC, 4):
                    P.dma("sp", x_dst[:, c0:c0 + 4, :], x_sb[:, c0:c0 + 4, :], reads=self.xk[c0:c0 + 4])
        if pre_norm:
            B.norm_stats(x_sb, self.xk, h_sb, self.hk, DC, D)
            B.norm_apply(h_sb, self.hk, x_sb, self.xk, lambda c: g_sb[:, 1, c:c + 1], "g", DC)
        if nxt is not None:
            self._loads(b + 1, nxt, red, ready)
            self._pref = True

    def store_partial(self, ps, pkey, part, b, m, pst, idx):
        P = self.P
        i = idx % 2
        P.add("act", lambda e: e.copy(out=pst[:, i, :], in_=ps[:, :]), reads=[pkey], writes=[("pst", i)])
        P.dma("sp", part[b, m * 128:(m + 1) * 128, :], pst[:, i, :], reads=[("pst", i)], writes=[("partblk", b, m)])


def schedule_cc(P, part, red, b, ccops, delay=5):
    def go():
        ccops[b] = P.collective(part[b, :, :], red[b, :, :], reads=[("partblk", b, m) for m in range(DC)])
    if not hasattr(P, "pending_cc"):
        P.pending_cc = []
    P.pending_cc.append([delay, go])


def flush_cc(P):
    pend = getattr(P, "pending_cc", [])
    while pend:
        pend.pop(0)[1]()


def xsrc_view(xin, xres, b, from_input):
    if from_input:
        return xin.rearrange("(c p) t -> p c t", p=128)[:, :, b * TB:(b + 1) * TB]
    return _blk(xres, b)


def phase_allreduce(nc, part, red, NB, uid):
    G = Prog.G
    if G.get("cc_nc") is not nc:
        G["cc_nc"] = nc
        G["cc"] = nc.alloc_semaphore(name="cc_global")
        G["ccv"] = 0
    cc = G["cc"]
    with nc.Block() as blk:
        @blk.gpsimd
        def _(g):
            for b in range(NB):
                g.collective_compute("AllReduce", ALU.add, replica_groups=GROUPS, ins=[part[b, :, :]],
                                     outs=[red[b, :, :]]).then_inc(cc)
                G["ccv"] += 1
                g.wait_ge(cc, G["ccv"])


def phase_mlp(nc, S, xin, xres, part, red, from_input, has_bias, wdram, wmeta, wplan, gdram, ready, redo):
    NB = S // TB
    ph = Phase(nc, wdram, wmeta, gdram)
    P, B = ph.P, ph.B
    h1 = P.sb("h1", [128, FFC, TB], BF16)
    pst = P.sb("pst", [128, 2, TB], F32)

    def body(b):
        ph.prologue(b, xsrc_view(xin, xres, b, from_input), red, has_bias, _blk(xres, b), ready=ready,
                    nxt=(xsrc_view(xin, xres, b + 1, from_input) if b + 1 < NB else None))
        for m in range(FFC):
            def evac(ps, pkey, m=m):
                t2 = ph.tmp2[:, m % 2, :]
                P.add("act", lambda e: e.activation(out=t2, in_=ps[:, :], func=AF.Relu), reads=[pkey],
                      writes=[("tmp2", m % 2)])
                P.add("dve", lambda e: e.tensor_tensor(out=h1[:, m, :], in0=t2, in1=t2, op=ALU.mult),
                      reads=[("tmp2", m % 2)], writes=[("h1", m)])
            B.group(wplan["w1"][m], lambda kk: ph.h_sb[:, kk, :], lambda kk: ("h", kk), evac)
        for m in range(DC):
            B.group(wplan["w2"][m], lambda kk: h1[:, kk, :], lambda kk: ("h1", kk),
                    lambda ps, pkey, m=m: ph.store_partial(ps, pkey, part, b, m, pst, m))
        schedule_cc(P, part, redo, b, ccops)
    ccops = {}
    for b in range(NB):
        body(b)
    flush_cc(P)
    P.emit()
    return {b: op.ccval for b, op in ccops.items()}


def phase_wo(nc, S, o_s, part, wdram, wmeta, wplan, redo):
    NB = S // TB
    B = Bld(nc, wdram, wmeta, with_ss=False)
    P = B.P
    o_sb = P.sb("o_sb", [128, NHC, TB], BF16)
    pst = P.sb("pst", [128, 2, TB], F32)
    ov = o_s.rearrange("h p t -> p h t")

    def body(b):
        P.dma("sp", o_sb[:, :, :], ov[:, :, b * TB:(b + 1) * TB], writes=[("o", h) for h in range(NHC)])
        for m in range(DC):
            def evac(ps, pkey, m=m):
                i = m % 2
                P.add("act", lambda e: e.copy(out=pst[:, i, :], in_=ps[:, :]), reads=[pkey], writes=[("pst", i)])
                P.dma("sp", part[b, m * 128:(m + 1) * 128, :], pst[:, i, :], reads=[("pst", i)],
                      writes=[("partblk", b, m)])
            B.group(wplan["wo"][m], lambda kk: o_sb[:, kk, :], lambda kk: ("o", kk), evac)
        schedule_cc(P, part, redo, b, ccops)
    ccops = {}
    for b in range(NB):
        body(b)
    flush_cc(P)
    P.emit()
    return {b: op.ccval for b, op in ccops.items()}


def phase_final(nc, S, xres, red, outT, gdram, ready):
    NB = S // TB
    ph = Phase(nc, None, None, gdram)
    ov = outT.rearrange("(c p) t -> p c t", p=128)
    for b in range(NB):
        ph.prologue(b, _blk(xres, b), red, False, ov[:, :, b * TB:(b + 1) * TB], pre_norm=False, ready=ready,
                    nxt=(_blk(xres, b + 1) if b + 1 < NB else None))
    ph.P.emit()


def phase_mla_proj(nc, S, xin, xres, red, from_input, first, pos, rcd, wdram, wmeta, wplan, gdram, gqkv,
                   qn_o, qr_o, kn_o, kr_o, v_o, ready):
    NB = S // TB
    ph = Phase(nc, wdram, wmeta, gdram)
    P, B = ph.P, ph.B
    h_sb = ph.h_sb
    cqp = P.sb("cqp", [128, 4, TB], F32)
    ckvp = P.sb("ckvp", [128, 4, TB], F32)
    cqn = P.sb("cqn", [128, 4, TB], BF16)
    ckvn = P.sb("ckvn", [128, 4, TB], BF16)
    gq_sb = P.sb("gq_sb", [128, 8], F32)
    rc = P.sb("rc_sb", [64, 2], F32)
    pos_sb = P.sb("pos_sb", [64, TB], I32)
    B.rope_ws = P.sb("rope_ws", [64, 4, TB], F32)
    B.rope_wi = P.sb("rope_wi", [64, TB], I32)
    cos2 = P.sb("cos2", [64, TB], F32)
    sin2 = P.sb("sin2", [64, TB], F32)
    rt = P.sb("rt", [64, 2, 2, TB], F32)
    st_qn = P.sb("st_qn", [128, NHC, TB], BF16)
    st_qr = P.sb("st_qr", [64, NHC, TB], BF16)
    st_kn = P.sb("st_kn", [128, NHC, TB], BF16)
    st_kr = P.sb("st_kr", [64, TB], BF16)
    st_v = P.sb("st_v", [128, 4, NHC * 128], BF16)
    P.dma("sp", gq_sb[:, :], gqkv[:, :], writes=["gq"])
    P.dma("sp", rc[:, :], rcd[:, :], writes=["rc"])
    qn_v = qn_o.rearrange("h p t -> p h t")
    qr_v = qr_o.rearrange("h p t -> p h t")
    kn_v = kn_o.rearrange("h p t -> p h t")
    rti = [0]

    def rope_apply(ps_a, ka, ps_b, kb, out_ap, out_key):
        i = rti[0] % 2
        rti[0] += 1
        t1, t2 = rt[:, i, 0, :], rt[:, i, 1, :]
        P.add("dve", lambda e: e.tensor_tensor(out=t1, in0=ps_a[0:64, :], in1=cos2[:, :], op=ALU.mult),
              reads=[ka, "tables"], writes=[("rt", i, 0)])
        P.add("dve", lambda e: e.tensor_tensor(out=t2, in0=ps_b[0:64, :], in1=sin2[:, :], op=ALU.mult),
              reads=[kb, "tables"], writes=[("rt", i, 1)])
        P.add("dve", lambda e: e.tensor_tensor(out=out_ap, in0=t1, in1=t2, op=ALU.add),
              reads=[("rt", i, 0), ("rt", i, 1)], writes=[out_key])

    def pair_group(tiles_a, tiles_b, rhs_of_k, key_of_k, out_ap, out_key):
        hold = {}

        def ev_a(ps, pkey):
            hold["a"] = (ps, pkey)

        def ev_b(ps, pkey):
            pa, ka = hold["a"]
            rope_apply(pa, ka, ps, pkey, out_ap, out_key)
        B.group(tiles_a, rhs_of_k, key_of_k, ev_a, mw=64)
        B.group(tiles_b, rhs_of_k, key_of_k, ev_b, mw=64)

    def body(b):
        ts = slice(b * TB, (b + 1) * TB)
        ph.prologue(b, xsrc_view(xin, xres, b, from_input), None if first else red, False,
                    None if first else _blk(xres, b), ready=ready,
                    nxt=(xsrc_view(xin, xres, b + 1, from_input) if b + 1 < NB else None))
        P.dma("sp", pos_sb[:, :], pos[:, ts], writes=["pos"])
        emit_rope_tables(B, pos_sb[:, :], "pos", rc, TB, cos2[:, :], sin2[:, :], "tables")
        hk = lambda kk: h_sb[:, kk, :]
        hkey = lambda kk: ("h", kk)
        for m in range(4):
            B.group(wplan["dq"][m], hk, hkey, lambda ps, pkey, m=m: P.add(
                "act", lambda e: e.copy(out=cqp[:, m, :], in_=ps[:, :]), reads=[pkey], writes=[("cqp", m)]))
        for m in range(4):
            B.group(wplan["dkv"][m], hk, hkey, lambda ps, pkey, m=m: P.add(
                "act", lambda e: e.copy(out=ckvp[:, m, :], in_=ps[:, :]), reads=[pkey], writes=[("ckvp", m)]))
        pair_group(wplan["kr"][0], wplan["kr_sw"][0], hk, hkey, st_kr[:, :], "st_kr")
        P.dma("sp", kr_o[:, ts], st_kr[:, :], reads=["st_kr"])
        B.norm_stats(cqp, [("cqp", c) for c in range(4)], cqn, [("cqn", c) for c in range(4)], 4, 512)
        B.norm_apply(cqn, [("cqn", c) for c in range(4)], cqp, [("cqp", c) for c in range(4)],
                     lambda c: gq_sb[:, c:c + 1], "gq", 4)
        B.norm_stats(ckvp, [("ckvp", c) for c in range(4)], ckvn, [("ckvn", c) for c in range(4)], 4, 512)
        B.norm_apply(ckvn, [("ckvn", c) for c in range(4)], ckvp, [("ckvp", c) for c in range(4)],
                     lambda c: gq_sb[:, 4 + c:5 + c], "gq", 4)
        qk = lambda kk: cqn[:, kk, :]
        qkey = lambda kk: ("cqn", kk)
        kk_ = lambda kk: ckvn[:, kk, :]
        kkey = lambda kk: ("ckvn", kk)
        for h in range(NHC):
            B.group(wplan["uq_n"][h], qk, qkey, lambda ps, pkey, h=h: P.add(
                "act", lambda e: e.copy(out=st_qn[:, h, :], in_=ps[:, :]), reads=[pkey], writes=[("st_qn", h)]))
        P.dma("sp", qn_v[:, :, ts], st_qn[:, :, :], reads=[("st_qn", h) for h in range(NHC)])
        for h in range(NHC):
            pair_group(wplan["uq_r"][h], wplan["uq_rs"][h], qk, qkey, st_qr[:, h, :], ("st_qr", h))
        P.dma("sp", qr_v[:, :, ts], st_qr[:, :, :], reads=[("st_qr", h) for h in range(NHC)])
        for h in range(NHC):
            B.group(wplan["uk"][h], kk_, kkey, lambda ps, pkey, h=h: P.add(
                "act", lambda e: e.copy(out=st_kn[:, h, :], in_=ps[:, :]), reads=[pkey], writes=[("st_kn", h)]))
        P.dma("sp", kn_v[:, :, ts], st_kn[:, :, :], reads=[("st_kn", h) for h in range(NHC)])
        wap, wkey = B.ring.fetch(wplan["uv"])
        for tt in range(4):
            ps, pkey = B.next_ps()
            for k in range(4):
                P.add("pe", lambda e, ps=ps, k=k, tt=tt: e.matmul(
                    ps[:, :], ckvn[:, k, tt * 128:(tt + 1) * 128], wap[:, k * 512:(k + 1) * 512],
                    start=(k == 0), stop=(k == 3)), reads=[wkey, ("ckvn", k)], writes=[pkey])
            P.add("act", lambda e, ps=ps, tt=tt: e.copy(out=st_v[:, tt, :], in_=ps[:, :]),
                  reads=[pkey], writes=[("st_v", tt)])
        for h in range(NHC):
            P.dma("sp", v_o[h, :, b * 4:(b + 1) * 4, :], st_v[:, :, h * 128:(h + 1) * 128],
                  reads=[("st_v", tt) for tt in range(4)])
    for b in range(NB):
        body(b)
    P.emit()


def prep_mla_core(r, w_dq, w_dkv, w_uq, w_ukv, w_o):
    ws = WStream()
    plan = {}
    plan["dq"] = ws.add_matrix(w_dq)
    plan["dkv"] = ws.add_matrix(w_dkv[:, :512])
    kr = w_dkv[:, 512:576]
    plan["kr"] = ws.add_matrix(kr, mw=64)
    plan["kr_sw"] = ws.add_matrix(np.concatenate([kr[:, 32:], kr[:, :32]], axis=1), mw=64)
    uq = w_uq.reshape(512, HEADS, 192)[:, r * NHC:(r + 1) * NHC, :]
    plan["uq_n"] = ws.add_matrix(np.ascontiguousarray(uq[:, :, :128]).reshape(512, NHC * 128))
    plan["uq_r"] = []
    plan["uq_rs"] = []
    for h in range(NHC):
        rr = uq[:, h, 128:192]
        plan["uq_r"].append(ws.add_matrix(rr, mw=64)[0])
        plan["uq_rs"].append(ws.add_matrix(np.concatenate([rr[:, 32:], rr[:, :32]], axis=1), mw=64)[0])
    ukv = w_ukv.reshape(512, HEADS, 256)[:, r * NHC:(r + 1) * NHC, :]
    plan["uk"] = ws.add_matrix(np.ascontiguousarray(ukv[:, :, :128]).reshape(512, NHC * 128))
    wv = np.ascontiguousarray(ukv[:, :, 128:]).reshape(512, NHC * 128)
    plan["uv"] = ws.add_raw(wv.reshape(4, 128, 512).transpose(1, 0, 2).reshape(128, 2048))
    wso = WStream()
    plano = {"wo": wso.add_matrix(np.ascontiguousarray(w_o[r * NHC * 128:(r + 1) * NHC * 128, :]))}
    return ws.array(), ws.meta, plan, wso.array(), wso.meta, plano


def phase_lru(nc, S, xin, xres, part, red, from_input, wdram, wmeta, wplan, gdram, wgd, vecd, ready, redo):
    NB = S // TB
    ph = Phase(nc, wdram, wmeta, gdram, nslots=4)
    P, B = ph.P, ph.B
    h_sb = ph.h_sb
    HJ = 4
    wg_sb = P.sb("wg_sb", [128, 2, LJ, 2, CH], BF16)
    vec = P.sb("vec_sb", [128, LJ, 10], F32)
    cc = P.sb("cc", [128, 4, LJ], F32)
    yl = P.sb("yl", [128, HJ, TB], F32)
    uxb = P.sb("uxb", [128, HJ, TB + 4], F32)
    u_sb = P.sb("u_sb", [128, HJ, TB], F32)
    u_bf = P.sb("u_bf", [128, HJ, TB], BF16)
    r_sb = P.sb("r_sb", [128, HJ, TB], F32)
    ig_sb = P.sb("ig_sb", [128, HJ, TB], F32)
    a_sb = P.sb("a_sb", [128, HJ, TB], F32)
    m_sb = P.sb("m_sb", [128, HJ, TB], F32)
    hs_sb = P.sb("hs_sb", [128, HJ, TB], F32)
    ost = P.sb("ost", [128, LJ, TB], BF16)
    pst = P.sb("pst", [128, 2, TB], F32)
    state = P.sb("state", [128, LJ], F32)
    halo = P.sb("halo", [128, LJ, 4], F32)
    P.dma("sp", vec[0:CH, :, :], vecd[:, :, :], writes=["vec"])
    P.dma("pool", wg_sb[0:CH, :, :, :, :].rearrange("p a b c d -> p (a b c d)"), wgd[:, :], writes=["wg"])
    e_ = cc[0:CH, 0, :]
    t_ = cc[0:CH, 1, :]
    P.add("act", lambda e: e.activation(out=e_, in_=vec[0:CH, :, 5], func=AF.Exp, scale=-1.0), reads=["vec"],
          writes=["cc"])
    coef = [1.0 / 5, -1.0 / 4, 1.0 / 3, -1.0 / 2, 1.0]
    P.add("dve", lambda e: e.tensor_scalar(out=t_, in0=e_, scalar1=-1.0 / 6, scalar2=coef[0], op0=ALU.mult,
                                           op1=ALU.add), reads=["cc"], writes=["cc"])
    for cf in coef[1:]:
        P.add("dve", lambda e: e.tensor_tensor(out=t_, in0=t_, in1=e_, op=ALU.mult), reads=["cc"], writes=["cc"])
        P.add("dve", lambda e, cf=cf: e.tensor_scalar(out=t_, in0=t_, scalar1=cf, scalar2=None, op0=ALU.add),
              reads=["cc"], writes=["cc"])
    P.add("dve", lambda e: e.tensor_tensor(out=t_, in0=t_, in1=e_, op=ALU.mult), reads=["cc"], writes=["cc"])
    P.add("dve", lambda e: e.tensor_scalar(out=cc[0:CH, 2, :], in0=t_, scalar1=-8.0, scalar2=None, op0=ALU.mult),
          reads=["cc"], writes=["cc"])
    P.add("dve", lambda e: e.tensor_scalar(out=cc[0:CH, 3, :], in0=t_, scalar1=-16.0, scalar2=None, op0=ALU.mult),
          reads=["cc"], writes=["cc"])
    P.add("dve", lambda e: e.memset(state[:, :], 0.0), writes=[("state", j) for j in range(LJ)])
    P.add("dve", lambda e: e.memset(halo[:, :, :], 0.0), writes=[("halo", j) for j in range(LJ)])
    hk = lambda kk: h_sb[:, kk, :]
    hkey = lambda kk: ("h", kk)

    def half(b, hf):
        J0 = hf * HJ
        for j in range(HJ):
            J = J0 + j
            B.group(wplan["wy"][J], hk, hkey, lambda ps, pkey, j=j, J=J: P.add(
                "act", lambda e: e.activation(out=yl[0:CH, j, :], in_=ps[0:CH, :], func=AF.Gelu,
                                              bias=vec[0:CH, J, 0:1]),
                reads=[pkey, "vec"], writes=[("yl", j)]), mw=CH)
        for j in range(HJ):
            J = J0 + j
            B.group(wplan["wx"][J], hk, hkey, lambda ps, pkey, j=j, J=J: P.add(
                "act", lambda e: e.activation(out=uxb[0:CH, j, 3:3 + TB], in_=ps[0:CH, :], func=AF.Identity,
                                              bias=vec[0:CH, J, 1:2]),
                reads=[pkey, "vec"], writes=[("ux", j)]), mw=CH)
        for j in range(HJ):
            J = J0 + j
            P.add("dve", lambda e, j=j, J=J: e.tensor_copy(out=uxb[0:CH, j, 0:3], in_=halo[0:CH, J, 0:3]),
                  reads=[("halo", J), ("ux", j)], writes=[("ux", j)])
            P.add("dve", lambda e, j=j, J=J: e.tensor_scalar(out=u_sb[0:CH, j, :], in0=uxb[0:CH, j, 0:TB],
                                                             scalar1=vec[0:CH, J, 6:7], scalar2=vec[0:CH, J, 2:3],
                                                             op0=ALU.mult, op1=ALU.add),
                  reads=[("ux", j), "vec"], writes=[("u", j)])
            for i in range(1, 4):
                P.add("dve", lambda e, j=j, J=J, i=i: e.scalar_tensor_tensor(
                    out=u_sb[0:CH, j, :], in0=uxb[0:CH, j, i:i + TB], scalar=vec[0:CH, J, 6 + i:7 + i],
                    in1=u_sb[0:CH, j, :], op0=ALU.mult, op1=ALU.add),
                    reads=[("ux", j), "vec", ("u", j)], writes=[("u", j)])
            P.add("dve", lambda e, j=j, J=J: e.tensor_copy(out=halo[0:CH, J, 0:3], in_=uxb[0:CH, j, TB:TB + 3]),
                  reads=[("ux", j)], writes=[("halo", J)])
            P.add("act", lambda e, j=j: e.copy(out=u_bf[0:CH, j, :], in_=u_sb[0:CH, j, :]),
                  reads=[("u", j)], writes=[("ubf", j)])
        for gi, dst, bidx in ((0, r_sb, 3), (1, ig_sb, 4)):
            for j in range(HJ):
                J = J0 + j
                ps, pkey = B.next_ps()
                for i in range(2):
                    P.add("pe", lambda e, ps=ps, gi=gi, j=j, J=J, i=i: e.matmul(
                        ps[0:CH, :], wg_sb[0:CH, gi, J, i, :], u_bf[0:CH, 2 * (j // 2) + i, :], start=(i == 0),
                        stop=(i == 1)), reads=["wg", ("ubf", 2 * (j // 2) + i)], writes=[pkey])
                P.add("act", lambda e, ps=ps, j=j, J=J, dst=dst, bidx=bidx: e.activation(
                    out=dst[0:CH, j, :], in_=ps[0:CH, :], func=AF.Sigmoid, bias=vec[0:CH, J, bidx:bidx + 1]),
                    reads=[pkey, "vec"], writes=[("gate", gi, j)])
        for j in range(HJ):
            J = J0 + j
            P.add("act", lambda e, j=j, J=J: e.activation(out=a_sb[0:CH, j, :], in_=r_sb[0:CH, j, :], func=AF.Exp,
                                                          scale=cc[0:CH, 2, J:J + 1]),
                  reads=[("gate", 0, j), "cc"], writes=[("a", j)])
            P.add("act", lambda e, j=j, J=J: e.activation(out=m_sb[0:CH, j, :], in_=r_sb[0:CH, j, :], func=AF.Exp,
                                                          scale=cc[0:CH, 3, J:J + 1]),
                  reads=[("gate", 0, j), "cc"], writes=[("m", j)])
        for j in range(HJ):
            P.add("dve", lambda e, j=j: e.tensor_scalar(out=m_sb[0:CH, j, :], in0=m_sb[0:CH, j, :], scalar1=-1.0,
                                                        scalar2=1.0, op0=ALU.mult, op1=ALU.add),
                  reads=[("m", j)], writes=[("m", j)])
        for j in range(HJ):
            P.add("act", lambda e, j=j: e.activation(out=m_sb[0:CH, j, :], in_=m_sb[0:CH, j, :], func=AF.Sqrt),
                  reads=[("m", j)], writes=[("m", j)])
        for j in range(HJ):
            J = J0 + j
            P.add("dve", lambda e, j=j: e.tensor_tensor(out=ig_sb[0:CH, j, :], in0=ig_sb[0:CH, j, :],
                                                        in1=u_sb[0:CH, j, :], op=ALU.mult),
                  reads=[("gate", 1, j), ("u", j)], writes=[("gate", 1, j)])
            P.add("dve", lambda e, j=j: e.tensor_tensor(out=ig_sb[0:CH, j, :], in0=ig_sb[0:CH, j, :],
                                                        in1=m_sb[0:CH, j, :], op=ALU.mult),
                  reads=[("gate", 1, j), ("m", j)], writes=[("gate", 1, j)])
            P.add("dve", lambda e, j=j, J=J: e.tensor_tensor_scan(
                out=hs_sb[0:CH, j, :], data0=a_sb[0:CH, j, :], data1=ig_sb[0:CH, j, :],
                initial=state[0:CH, J:J + 1], op0=ALU.mult, op1=ALU.add),
                reads=[("a", j), ("gate", 1, j), ("state", J)], writes=[("hs", j)])
            P.add("dve", lambda e, j=j, J=J: e.tensor_copy(out=state[0:CH, J:J + 1], in_=hs_sb[0:CH, j, TB - 1:TB]),
                  reads=[("hs", j)], writes=[("state", J)])
            P.add("dve", lambda e, j=j, J=J: e.tensor_tensor(out=ost[0:CH, J, :], in0=hs_sb[0:CH, j, :],
                                                             in1=yl[0:CH, j, :], op=ALU.mult),
                  reads=[("hs", j), ("yl", j)], writes=[("ost", J)])

    def body(b):
        ph.prologue(b, xsrc_view(xin, xres, b, from_input), red, False, _blk(xres, b), ready=ready,
                    nxt=(xsrc_view(xin, xres, b + 1, from_input) if b + 1 < NB else None))
        half(b, 0)
        half(b, 1)
        for m in range(DC):
            ps, pkey = B.next_ps()
            wap, wkey = B.ring.fetch(wplan["wout"][m])
            for J in range(LJ):
                P.add("pe", lambda e, ps=ps, J=J, wap=wap: e.matmul(
                    ps[:, :], wap[0:CH, J * 128:(J + 1) * 128], ost[0:CH, J, :], start=(J == 0), stop=(J == LJ - 1)),
                    reads=[wkey, ("ost", J)], writes=[pkey])
            ph.store_partial(ps, pkey, part, b, m, pst, m)
        schedule_cc(P, part, redo, b, ccops)
    ccops = {}
    for b in range(NB):
        body(b)
    flush_cc(P)
    P.emit()
    return {b: op.ccval for b, op in ccops.items()}


def prep_lru_core(r, w_y, b_y, w_x, b_x, conv_w, conv_b, w_ga, b_ga, w_gi, b_gi, lam, w_out):
    c0 = r * LJ * CH
    c1 = c0 + LJ * CH
    ws = WStream()
    plan = {"wy": ws.add_matrix(np.ascontiguousarray(w_y[:, c0:c1]), mw=CH),
            "wx": ws.add_matrix(np.ascontiguousarray(w_x[:, c0:c1]), mw=CH), "wout": []}
    wo = w_out[c0:c1, :].reshape(LJ, CH, DC, 128)
    for m in range(DC):
        t = np.zeros((128, LJ, 128), np.float32)
        t[:CH] = wo[:, :, m, :].transpose(1, 0, 2)
        plan["wout"].append(ws.add_raw(t.reshape(128, LJ * 128)))
    wg = np.zeros((CH, 2, LJ, 2, CH), np.float32)
    for gi, w in enumerate((w_ga, w_gi)):
        for J in range(LJ):
            n = (LJ // 2) * r + J // 2
            for i in range(2):
                wg[:, gi, J, i, :] = w[n, i * CH:(i + 1) * CH, (J % 2) * CH:(J % 2 + 1) * CH]
    vec = np.zeros((CH, LJ, 10), np.float32)
    for idx, vv in enumerate((b_y, b_x, conv_b, b_ga, b_gi, lam)):
        vec[:, :, idx] = vv[c0:c1].reshape(LJ, CH).T
    for i in range(4):
        vec[:, :, 6 + i] = conv_w[i, c0:c1].reshape(LJ, CH).T
    return ws.array(), ws.meta, plan, np.ascontiguousarray(wg.reshape(CH, -1)), vec


def prep_mlp_core(r, w1, w2):
    ws = WStream()
    n = DFF // TPG
    plan = {"w1": ws.add_matrix(np.ascontiguousarray(w1[:, r * n:(r + 1) * n])),
            "w2": ws.add_matrix(np.ascontiguousarray(w2[r * n:(r + 1) * n, :]))}
    return ws.array(), ws.meta, plan


def g3(gpost, gpre, bias):
    z = np.zeros((128, DC), np.float32)
    return np.ascontiguousarray(np.stack([gvec(gpost) if gpost is not None else z,
                                          gvec(gpre) if gpre is not None else z,
                                          gvec(bias) if bias is not None else z], axis=1))


def build_fused(S, depth, metas):
    nc = bass.Bass("TRN2", target_bir_lowering=False)
    NB = S // TB

    def din(name, shape, dt=F32):
        return nc.dram_tensor(name, list(shape), dt, kind="ExternalInput").ap()

    def wsz(meta):
        return meta[-1][0] + meta[-1][1]
    xin = din("xT", [D, S])
    pos = din("pos", [64, S], I32)
    rcd = din("rc", [64, 2])
    msk = din("masks", [128, 4, 512], BF16)
    outT = nc.dram_tensor("outT", [D, S], F32, kind="ExternalOutput").ap()
    xres = nc.dram_tensor("xres", [NB, D, TB], F32)
    parts = [nc.dram_tensor(f"part{i}", [NB, D, TB], F32) for i in range(2)]
    reds = [nc.dram_tensor(f"red{i}", [NB, D, TB], F32) for i in range(2)]
    qn_s = nc.dram_tensor("qn_s", [NHC, 128, S], BF16).ap()
    qr_s = nc.dram_tensor("qr_s", [NHC, 64, S], BF16).ap()
    kn_s = nc.dram_tensor("kn_s", [NHC, 128, S], BF16).ap()
    kr_s = nc.dram_tensor("kr_s", [64, S], BF16).ap()
    v_s = nc.dram_tensor("v_s", [NHC, 128, S // 128, 128], BF16).ap()
    o_s = nc.dram_tensor("o_s", [NHC, 128, S], BF16).ap()
    first = True
    from_input = True
    k = 0
    ready = None
    prev_bias = False
    for i in range(depth):
        j = i // 2
        if i % 2 == 0:
            wm, wp, wmo, wpo = metas[f"mla{j}"]
            phase_mla_proj(nc, S, xin, xres, reds[(k - 1) % 2], from_input, first, pos, rcd,
                           din(f"w_mla{j}", [128, wsz(wm)]), wm, wp, din(f"g_mix{i}", [128, 3, DC]),
                           din(f"gq{j}", [128, 8]), qn_s, qr_s, kn_s, kr_s, v_s, ready)
            emit_attn(nc, S, NHC, qn_s, qr_s, kn_s, kr_s, v_s, msk, o_s)
            ready = phase_wo(nc, S, o_s, parts[k % 2], din(f"w_o{j}", [128, wsz(wmo)]), wmo, wpo, reds[k % 2])
            prev_bias = False
        else:
            wm, wp = metas[f"lru{j}"]
            ready = phase_lru(nc, S, xin, xres, parts[k % 2], reds[(k - 1) % 2], from_input,
                              din(f"w_lru{j}", [128, wsz(wm)]), wm, wp, din(f"g_mix{i}", [128, 3, DC]),
                              din(f"wg{j}", [CH, 2 * LJ * 2 * CH]), din(f"vec{j}", [CH, LJ, 10]), ready, reds[k % 2])
            prev_bias = True
        k += 1
        wm, wp = metas[f"mlp{i}"]
        ready = phase_mlp(nc, S, xin, xres, parts[k % 2], reds[(k - 1) % 2], from_input, prev_bias,
                          din(f"w_mlp{i}", [128, wsz(wm)]), wm, wp, din(f"g_mlp{i}", [128, 3, DC]), ready,
                          reds[k % 2])
        k += 1
        first = False
        from_input = False
    phase_final(nc, S, xres, reds[(k - 1) % 2], outT, din("g_fin", [128, 3, DC]), ready)
    return nc


def kernel(x, positions, mix_pre_g, mix_post_g, mlp_pre_g, mlp_post_g,
           mla_w_dq, mla_g_q, mla_w_uq, mla_w_dkv, mla_g_kv, mla_w_ukv, mla_w_o,
           lru_w_y, lru_b_y, lru_w_x, lru_b_x, lru_conv_w, lru_conv_b,
           lru_w_ga, lru_b_ga, lru_w_gi, lru_b_gi, lru_lam, lru_w_out, lru_b_out,
           mlp_w1, mlp_w2):
    f32 = lambda a: np.asarray(a, np.float32)
    x = f32(x)
    positions = np.asarray(positions, np.int32)
    NBATCH, S, _ = x.shape
    assert NBATCH * TPG == NCORES
    depth = mix_pre_g.shape[0]
    rc = rope_consts()
    masks = attn_masks()
    maps = [dict() for _ in range(NCORES)]
    metas = {}
    xTb = [np.ascontiguousarray(x[b].T) for b in range(NBATCH)]
    posb = [np.ascontiguousarray(np.broadcast_to(positions[b][None, :], (64, S))) for b in range(NBATCH)]
    for c in range(NCORES):
        b, r = divmod(c, TPG)
        m = maps[c]
        m["xT"] = xTb[b]
        m["pos"] = posb[b]
        m["rc"] = rc
        m["masks"] = masks
        for i in range(depth):
            j = i // 2
            if i % 2 == 0:
                wa, wm, wp, wao, wmo, wpo = prep_mla_core(r, f32(mla_w_dq[j]), f32(mla_w_dkv[j]), f32(mla_w_uq[j]),
                                                          f32(mla_w_ukv[j]), f32(mla_w_o[j]))
                metas[f"mla{j}"] = (wm, wp, wmo, wpo)
                m[f"w_mla{j}"] = wa
                m[f"w_o{j}"] = wao
                m[f"gq{j}"] = np.ascontiguousarray(np.concatenate([gvec(mla_g_q[j]), gvec(mla_g_kv[j])], axis=1))
                m[f"g_mix{i}"] = g3(mlp_post_g[i - 1] if i > 0 else None, mix_pre_g[i], None)
                m[f"g_mlp{i}"] = g3(mix_post_g[i], mlp_pre_g[i], None)
            else:
                wa, wm, wp, wg, vec = prep_lru_core(
                    r, f32(lru_w_y[j]), f32(lru_b_y[j]), f32(lru_w_x[j]), f32(lru_b_x[j]), f32(lru_conv_w[j]),
                    f32(lru_conv_b[j]), f32(lru_w_ga[j]), f32(lru_b_ga[j]), f32(lru_w_gi[j]), f32(lru_b_gi[j]),
                    f32(lru_lam[j]), f32(lru_w_out[j]))
                metas[f"lru{j}"] = (wm, wp)
                m[f"w_lru{j}"] = wa
                m[f"wg{j}"] = wg
                m[f"vec{j}"] = vec
                m[f"g_mix{i}"] = g3(mlp_post_g[i - 1], mix_pre_g[i], None)
                m[f"g_mlp{i}"] = g3(mix_post_g[i], mlp_pre_g[i], lru_b_out[j])
            wa, wm, wp = prep_mlp_core(r, f32(mlp_w1[i]), f32(mlp_w2[i]))
            metas[f"mlp{i}"] = (wm, wp)
            m[f"w_mlp{i}"] = wa
        m["g_fin"] = g3(mlp_post_g[depth - 1], None, None)
    nc = build_fused(S, depth, metas)
    res = run_bass_kernel_spmd(nc, maps, core_ids=list(range(NCORES))).results
    out = np.stack([np.asarray(res[b * TPG]["outT"]).T for b in range(NBATCH)], axis=0)
    return np.ascontiguousarray(out.astype(np.float32))
```
